# Optimizing a Trainium2 kernel written in Bass

```python
import math
import jax, jax.numpy as jnp
from jax import lax
import numpy as np

D_MODEL = 1024
BATCH = 8
SEQ = 2048
DEPTH = 4
DEC_BATCH = 128
DEC_SEQ = 1
PAST_LEN = 16384
PAGE_SIZE = 128

MIX_DIM = D_MODEL
DN_DIM = MIX_DIM // 2
POOL_DIM = MIX_DIM - DN_DIM
DN_HEADS = 4
DN_HEAD_DIM = DN_DIM // DN_HEADS
QKV_DIM = 3 * DN_DIM
CONV_W = 4
CHUNK = 64
POOL_WINDOWS = (2, 4, 8, 16)
POOL_GROUPS = len(POOL_WINDOWS)
POOL_GROUP_DIM = POOL_DIM // POOL_GROUPS
POOL_BUF = max(POOL_WINDOWS) - 1
D_FF = 2816
IN_DIM = QKV_DIM + DN_DIM + 2 * DN_HEADS + POOL_DIM
DN_ALPHA = (2.0 * DEPTH) ** 0.25
DN_BETA = (8.0 * DEPTH) ** -0.25
LN_EPS = 1e-5
RMS_EPS = 1e-6
L2_EPS = 1e-6

kernel_name = "hymba_gdn_pool_macaron_deepnorm_step"


def layer_norm(x, g, b):
    xf = x.astype(jnp.float32)
    mu = jnp.mean(xf, axis=-1, keepdims=True)
    xc = xf - mu
    var = jnp.mean(xc * xc, axis=-1, keepdims=True)
    return (xc * lax.rsqrt(var + LN_EPS) * g.astype(jnp.float32) + b.astype(jnp.float32)).astype(x.dtype)


def swiglu(x, wg, wu, wd):
    return (jax.nn.silu(x @ wg) * (x @ wu)) @ wd


def l2norm(x):
    return x * lax.rsqrt(jnp.sum(x * x, axis=-1, keepdims=True) + L2_EPS)


def causal_conv(xin, buf, w):
    L = xin.shape[1]
    ext = jnp.concatenate([buf.astype(xin.dtype), xin], axis=1)
    out = sum(w[i] * ext[:, i:i + L] for i in range(CONV_W))
    return jax.nn.silu(out), ext[:, -(CONV_W - 1):]


def gated_delta_chunked(q, k, v, beta, g, s0, chunk):
    B, L, H, DK = q.shape
    DV = v.shape[-1]
    N = L // chunk

    def blk(t):
        t = t.reshape((B, N, chunk, H) + t.shape[3:])
        return jnp.moveaxis(t, (1, 3), (0, 2))

    qb, kb, vb, bb, gb = blk(q), blk(k), blk(v), blk(beta), blk(g)
    gc = jnp.cumsum(gb, axis=-1)
    idx = jnp.arange(chunk)
    incl = idx[:, None] >= idx[None, :]
    strict = idx[:, None] > idx[None, :]
    diff = gc[..., :, None] - gc[..., None, :]
    dec = jnp.where(incl, jnp.exp(jnp.where(incl, diff, 0.0)), 0.0)
    kk = jnp.einsum('nbhcd,nbhsd->nbhcs', kb, kb)
    m = jnp.where(strict, bb[..., :, None] * kk * dec, 0.0)
    a_mat = jnp.eye(chunk, dtype=jnp.float32) + m
    rhs = jnp.concatenate([bb[..., None] * vb, (bb * jnp.exp(gc))[..., None] * kb], axis=-1)
    sol = lax.linalg.triangular_solve(a_mat, rhs, left_side=True, lower=True, unit_diagonal=True)
    uv, wk = sol[..., :DV], sol[..., DV:]
    qk = jnp.einsum('nbhcd,nbhsd->nbhcs', qb, kb) * dec
    q_dec = qb * jnp.exp(gc)[..., None]
    k_end = kb * jnp.exp(gc[..., -1:] - gc)[..., None]
    g_end = jnp.exp(gc[..., -1])

    def step(s, xs):
        uv_c, w_c, qk_c, qd_c, ke_c, ge_c = xs
        u = uv_c - jnp.einsum('bhcd,bhde->bhce', w_c, s)
        o = jnp.einsum('bhcd,bhde->bhce', qd_c, s) + jnp.einsum('bhcs,bhse->bhce', qk_c, u)
        s = ge_c[..., None, None] * s + jnp.einsum('bhcd,bhce->bhde', ke_c, u)
        return s, o

    s_fin, o = lax.scan(step, s0, (uv, wk, qk, q_dec, k_end, g_end))
    o = jnp.moveaxis(o, (0, 2), (1, 3)).reshape(B, L, H, DV)
    return o, s_fin


def pool_mix(p, buf, start_pos, pool_w, pool_scale):
    B, L, _ = p.shape
    ext = jnp.concatenate([buf.astype(jnp.float32), p.astype(jnp.float32)], axis=1)
    cs = jnp.concatenate([jnp.zeros((B, 1, POOL_DIM), jnp.float32), jnp.cumsum(ext, axis=1)], axis=1)
    end = cs[:, POOL_BUF + 1:]
    pos = start_pos + jnp.arange(L)
    means = []
    for gi, w in enumerate(POOL_WINDOWS):
        sl = slice(gi * POOL_GROUP_DIM, (gi + 1) * POOL_GROUP_DIM)
        s = end[:, :, sl] - cs[:, POOL_BUF + 1 - w:POOL_BUF + 1 - w + L, sl]
        cnt = jnp.minimum(w, pos + 1).astype(jnp.float32)
        means.append(s / cnt[None, :, None])
    d = jnp.concatenate(means, axis=-1) - ext[:, POOL_BUF:]
    d = d.reshape(B, L, POOL_GROUPS, POOL_GROUP_DIM).astype(p.dtype)
    y = jnp.einsum('blgc,gcd->blgd', d, pool_w).reshape(B, L, POOL_DIM) * pool_scale
    return y, ext[:, -POOL_BUF:].astype(p.dtype)


def mixer(h, s0, conv_buf, pool_buf, start_pos, chunk, w_in, conv_w, a_log, dt_bias, onorm_g, pool_w, pool_scale, w_out):
    B, L, _ = h.shape
    proj = h @ w_in
    qkv, z, b_raw, a_raw, p = jnp.split(
        proj, [QKV_DIM, QKV_DIM + DN_DIM, QKV_DIM + DN_DIM + DN_HEADS, QKV_DIM + DN_DIM + 2 * DN_HEADS], axis=-1)
    qkv_c, new_conv = causal_conv(qkv, conv_buf, conv_w)
    q, k, v = jnp.split(qkv_c.astype(jnp.float32), 3, axis=-1)
    q = l2norm(q.reshape(B, L, DN_HEADS, DN_HEAD_DIM)) * (DN_HEAD_DIM ** -0.5)
    k = l2norm(k.reshape(B, L, DN_HEADS, DN_HEAD_DIM))
    v = v.reshape(B, L, DN_HEADS, DN_HEAD_DIM)
    beta = jax.nn.sigmoid(b_raw.astype(jnp.float32))
    g = -jnp.exp(a_log.astype(jnp.float32)) * jax.nn.softplus(a_raw.astype(jnp.float32) + dt_bias.astype(jnp.float32))
    o, s_new = gated_delta_chunked(q, k, v, beta, g, s0.astype(jnp.float32), chunk)
    zf = z.astype(jnp.float32).reshape(B, L, DN_HEADS, DN_HEAD_DIM)
    o = o * lax.rsqrt(jnp.mean(o * o, axis=-1, keepdims=True) + RMS_EPS) * onorm_g.astype(jnp.float32) * jax.nn.silu(zf)
    o_dn = o.reshape(B, L, DN_DIM).astype(h.dtype)
    o_pool, new_pool = pool_mix(p, pool_buf, start_pos, pool_w, pool_scale)
    out = jnp.concatenate([o_dn, o_pool], axis=-1) @ w_out
    return out, s_new.astype(s0.dtype), new_conv.astype(conv_buf.dtype), new_pool.astype(pool_buf.dtype)


def layer(x, s0, conv_buf, pool_buf, start_pos, chunk, ln1_g, ln1_b, f1g, f1u, f1d, w_in, conv_w, a_log, dt_bias,
          onorm_g, pool_w, pool_scale, w_out, ln2_g, ln2_b, f2g, f2u, f2d, ln3_g, ln3_b):
    h = layer_norm(DN_ALPHA * x + 0.5 * swiglu(x, f1g, f1u, f1d), ln1_g, ln1_b)
    mix, s_new, c_new, p_new = mixer(h, s0, conv_buf, pool_buf, start_pos, chunk, w_in, conv_w, a_log, dt_bias,
                                     onorm_g, pool_w, pool_scale, w_out)
    h = layer_norm(DN_ALPHA * h + mix, ln2_g, ln2_b)
    y = layer_norm(DN_ALPHA * h + 0.5 * swiglu(h, f2g, f2u, f2d), ln3_g, ln3_b)
    return y, s_new, c_new, p_new


def setup_inputs(seed: int = 0) -> dict:
    key = jax.random.key(seed)
    ks = iter(jax.random.split(key, 40))
    f32 = jnp.float32
    nrm = lambda shape, scale: jax.random.normal(next(ks), shape, f32) * scale
    inp = {}
    inp['x_prompt'] = nrm((BATCH, SEQ, D_MODEL), 1.0)
    inp['x_sample'] = nrm((DEC_BATCH, DEC_SEQ, D_MODEL), 1.0)
    inp['state_delta'] = nrm((DEPTH, DEC_BATCH, DN_HEADS, DN_HEAD_DIM, DN_HEAD_DIM), 0.05)
    inp['state_conv'] = nrm((DEPTH, DEC_BATCH, CONV_W - 1, QKV_DIM), 1.0)
    inp['state_pool'] = nrm((DEPTH, DEC_BATCH, POOL_BUF, POOL_DIM), 1.0)
    inp['ln1_g'] = 1.0 + nrm((DEPTH, D_MODEL), 0.02)
    inp['ln1_b'] = nrm((DEPTH, D_MODEL), 0.02)
    inp['ffn1_w_gate'] = nrm((DEPTH, D_MODEL, D_FF), D_MODEL ** -0.5)
    inp['ffn1_w_up'] = nrm((DEPTH, D_MODEL, D_FF), D_MODEL ** -0.5)
    inp['ffn1_w_down'] = nrm((DEPTH, D_FF, D_MODEL), DN_BETA * D_FF ** -0.5)
    inp['w_in'] = nrm((DEPTH, D_MODEL, IN_DIM), D_MODEL ** -0.5)
    inp['conv_w'] = nrm((DEPTH, CONV_W, QKV_DIM), CONV_W ** -0.5)
    inp['a_log'] = jnp.log(jax.random.uniform(next(ks), (DEPTH, DN_HEADS), f32, 1.0, 16.0))
    dt = jnp.exp(jax.random.uniform(next(ks), (DEPTH, DN_HEADS), f32, math.log(1e-3), math.log(1e-1)))
    inp['dt_bias'] = dt + jnp.log(-jnp.expm1(-dt))
    inp['onorm_g'] = 1.0 + nrm((DEPTH, DN_HEAD_DIM), 0.02)
    inp['pool_w'] = nrm((DEPTH, POOL_GROUPS, POOL_GROUP_DIM, POOL_GROUP_DIM), POOL_GROUP_DIM ** -0.5)
    inp['pool_scale'] = 1.0 + nrm((DEPTH, POOL_DIM), 0.05)
    inp['w_out'] = nrm((DEPTH, MIX_DIM, D_MODEL), DN_BETA * MIX_DIM ** -0.5)
    inp['ln2_g'] = 1.0 + nrm((DEPTH, D_MODEL), 0.02)
    inp['ln2_b'] = nrm((DEPTH, D_MODEL), 0.02)
    inp['ffn2_w_gate'] = nrm((DEPTH, D_MODEL, D_FF), D_MODEL ** -0.5)
    inp['ffn2_w_up'] = nrm((DEPTH, D_MODEL, D_FF), D_MODEL ** -0.5)
    inp['ffn2_w_down'] = nrm((DEPTH, D_FF, D_MODEL), DN_BETA * D_FF ** -0.5)
    inp['ln3_g'] = 1.0 + nrm((DEPTH, D_MODEL), 0.02)
    inp['ln3_b'] = nrm((DEPTH, D_MODEL), 0.02)
    return inp


def reference(x_prompt, x_sample, state_delta, state_conv, state_pool, ln1_g, ln1_b, ffn1_w_gate, ffn1_w_up,
              ffn1_w_down, w_in, conv_w, a_log, dt_bias, onorm_g, pool_w, pool_scale, w_out, ln2_g, ln2_b,
              ffn2_w_gate, ffn2_w_up, ffn2_w_down, ln3_g, ln3_b):
    dt = x_prompt.dtype
    chunk_prompt = math.gcd(SEQ, CHUNK)
    chunk_sample = math.gcd(DEC_SEQ, CHUNK)
    xp, xs = x_prompt, x_sample
    dp, cp, pp, ds, cs_, ps = [], [], [], [], [], []
    for l in range(DEPTH):
        w = (ln1_g[l], ln1_b[l], ffn1_w_gate[l], ffn1_w_up[l], ffn1_w_down[l], w_in[l], conv_w[l], a_log[l],
             dt_bias[l], onorm_g[l], pool_w[l], pool_scale[l], w_out[l], ln2_g[l], ln2_b[l], ffn2_w_gate[l],
             ffn2_w_up[l], ffn2_w_down[l], ln3_g[l], ln3_b[l])
        s0 = jnp.zeros((BATCH, DN_HEADS, DN_HEAD_DIM, DN_HEAD_DIM), dt)
        c0 = jnp.zeros((BATCH, CONV_W - 1, QKV_DIM), dt)
        p0 = jnp.zeros((BATCH, POOL_BUF, POOL_DIM), dt)
        xp, s_p, c_p, p_p = layer(xp, s0, c0, p0, 0, chunk_prompt, *w)
        xs, s_s, c_s, p_s = layer(xs, state_delta[l], state_conv[l], state_pool[l], PAST_LEN, chunk_sample, *w)
        dp.append(s_p); cp.append(c_p); pp.append(p_p)
        ds.append(s_s); cs_.append(c_s); ps.append(p_s)
    delta_prompt = jnp.stack(dp)
    conv_prompt = jnp.stack(cp)
    pool_prompt = jnp.stack(pp)
    delta_sample = jnp.stack(ds)
    conv_sample = jnp.stack(cs_)
    pool_sample = jnp.stack(ps)
    return (xp, xs, delta_prompt, conv_prompt, pool_prompt, delta_sample, conv_sample, pool_sample)
```

```python
import bisect
from contextlib import ExitStack
from functools import reduce
import numpy as np
import concourse.bass as bass
import concourse.mybir as mybir
from concourse.bass_utils import run_bass_kernel_spmd

F32 = mybir.dt.float32
F32R = mybir.dt.float32r
AF = mybir.ActivationFunctionType
ALU = mybir.AluOpType
AX = mybir.AxisListType
AF_SILU = AF.Silu

NCORES = 8
D = 1024
DC = 8
DFF = 2816
NF = 22
T = 2048
NS = 16
NTOK = T + NS
DEPTH = 4
QKV = 1536
INDIM = 2568
ALPHA = float((2.0 * DEPTH) ** 0.25)
LN_EPS = 1e-5
RMS_EPS = 1e-6
L2_EPS = 1e-6
POOL_W = (2, 4, 8, 16)
SC0 = 1024


def colp(t):
    return t if t < 1024 else t + NS

ENG_NAMES = ("pe", "act", "dve", "pool", "sp")
SEM_EPOCH = 30000
N_DMA_SEMS = 12


class _Op:
    __slots__ = ("eng", "fn", "deps", "is_dma", "sem", "val", "signaled")

    def __init__(self, eng, fn, deps, is_dma):
        self.eng = eng
        self.fn = fn
        self.deps = deps
        self.is_dma = is_dma
        self.sem = None
        self.val = 0
        self.signaled = is_dma


class _IMap:
    def __init__(self, size, excl_read=False):
        self.b = [0, size]
        self.w = [None]
        self.r = [{}]
        self.excl_read = excl_read

    def _split(self, x):
        i = bisect.bisect_right(self.b, x) - 1
        if self.b[i] == x:
            return i
        self.b.insert(i + 1, x)
        self.w.insert(i + 1, self.w[i])
        self.r.insert(i + 1, dict(self.r[i]))
        return i + 1

    def access(self, lo, hi, op, write, deps):
        i0 = self._split(lo)
        i1 = self._split(hi)
        for i in range(i0, i1):
            w = self.w[i]
            if w is not None:
                deps.append(w)
            if write:
                for v in self.r[i].values():
                    if isinstance(v, list):
                        deps.extend(v)
                    else:
                        deps.append(v)
                self.w[i] = op
                self.r[i] = {}
            else:
                if self.excl_read:
                    for k, v in self.r[i].items():
                        if k != op.eng and not isinstance(v, list):
                            deps.append(v)
                if op.is_dma:
                    self.r[i].setdefault("dma_" + op.eng, []).append(op)
                else:
                    self.r[i][op.eng] = op


class Sched:
    def __init__(self, nc, es, sizes):
        self.nc = nc
        self.es = es
        self.ops = {e: [] for e in ENG_NAMES}
        self.maps = {k: _IMap(v, excl_read=(k == "ps")) for k, v in sizes.items()}
        self.nsem = 0

    def new_sem(self, name):
        self.nsem += 1
        return self.es.enter_context(self.nc.semaphore(name))

    def op(self, eng, fn, reads=(), writes=(), dma=False):
        o = _Op(eng, fn, [], dma)
        deps = o.deps
        for ref in reads:
            for (sp, lo, hi) in ref.ivs:
                if sp == "ps":
                    lo, hi = lo // 512 * 512, (hi + 511) // 512 * 512
                self.maps[sp].access(lo, hi, o, False, deps)
        for ref in writes:
            for (sp, lo, hi) in ref.ivs:
                if sp == "ps":
                    lo, hi = lo // 512 * 512, (hi + 511) // 512 * 512
                self.maps[sp].access(lo, hi, o, True, deps)
        self.ops[eng].append(o)
        return o

    def emit(self):
        nc = self.nc
        for e in ENG_NAMES:
            for o in self.ops[e]:
                for d in o.deps:
                    if d is o:
                        continue
                    if o.eng == "pe" and d.eng == "pe":
                        continue
                    d.signaled = True
        dma_tail = {}
        for e in ENG_NAMES:
            cnt = 0
            sem = None
            dma_sems, dma_cnt, dma_last = [], [], []
            nd = 0
            for o in self.ops[e]:
                if o.is_dma:
                    if len(dma_sems) < N_DMA_SEMS:
                        dma_sems.append(self.new_sem("d_%s_%d" % (e, len(dma_sems))))
                        dma_cnt.append(0)
                        dma_last.append(None)
                    j = nd % N_DMA_SEMS
                    nd += 1
                    if dma_last[j] is not None:
                        o.deps.append(dma_last[j])
                    dma_cnt[j] += 16
                    o.sem = dma_sems[j]
                    o.val = dma_cnt[j]
                    dma_last[j] = o
                elif o.signaled:
                    if sem is None or cnt >= SEM_EPOCH:
                        sem = self.new_sem("e_%s_%d" % (e, self.nsem))
                        cnt = 0
                    cnt += 1
                    o.sem = sem
                    o.val = cnt
            dma_tail[e] = [x for x in dma_last if x is not None]
        sched = self

        def run_engine(e, h):
            waited = {}
            for o in sched.ops[e]:
                need = {}
                for d in o.deps:
                    if d.sem is None or d is o:
                        continue
                    if e == "pe" and d.eng == "pe":
                        continue
                    key = id(d.sem)
                    if waited.get(key, 0) >= d.val:
                        continue
                    if key not in need or need[key][1] < d.val:
                        need[key] = (d.sem, d.val)
                for key, (sem, val) in need.items():
                    h.wait_ge(sem, val)
                    waited[key] = val
                inst = o.fn(h)
                if o.sem is not None:
                    inst.then_inc(o.sem, 16 if o.is_dma else 1)
            for d in dma_tail[e]:
                if waited.get(id(d.sem), 0) < d.val:
                    h.wait_ge(d.sem, d.val)

        with nc.Block() as block:
            @block.tensor
            def _(h):
                run_engine("pe", h)

            @block.scalar
            def _(h):
                run_engine("act", h)

            @block.vector
            def _(h):
                run_engine("dve", h)

            @block.gpsimd
            def _(h):
                run_engine("pool", h)

            @block.sync
            def _(h):
                run_engine("sp", h)


class Ref:
    __slots__ = ("ap", "ivs")

    def __init__(self, ap, ivs):
        self.ap = ap
        self.ivs = ivs

    @property
    def r(self):
        return self.ap.bitcast(F32R)


def _runs(dims, rng):
    if len(dims) == 1:
        return [(rng[0][0], rng[0][1])]
    inner = 1
    for d in dims[1:]:
        inner *= d
    sub = _runs(dims[1:], rng[1:])
    if len(sub) == 1 and sub[0] == (0, inner):
        return [(rng[0][0] * inner, rng[0][1] * inner)]
    out = []
    for i in range(rng[0][0], rng[0][1]):
        for (a, b) in sub:
            out.append((i * inner + a, i * inner + b))
    return out


class Buf:
    def __init__(self, mem, space, off, shape):
        self.space = space
        self.off = off
        self.shape = tuple(shape)
        n = 1
        for s in shape:
            n *= s
        self.n = n
        base = mem[:, off:off + n]
        if len(shape) == 2:
            base = base.rearrange("p (a b) -> p a b", a=shape[0])
        elif len(shape) == 3:
            base = base.rearrange("p (a b c) -> p a b c", a=shape[0], b=shape[1])
        self.base = base

    def _norm(self, idx):
        if not isinstance(idx, tuple):
            idx = (idx,)
        idx = tuple(idx) + (slice(None),) * (len(self.shape) - len(idx))
        rng = []
        for i, s in zip(idx, self.shape):
            if isinstance(i, slice):
                lo = 0 if i.start is None else i.start
                hi = s if i.stop is None else i.stop
            else:
                lo, hi = i, i + 1
            assert 0 <= lo < hi <= s, (idx, self.shape)
            rng.append((lo, hi))
        return idx, rng

    def ref(self, p0, p1, idx):
        idx, rng = self._norm(idx)
        ap = self.base[(slice(p0, p1),) + idx]
        runs = _runs(self.shape, rng)
        if len(runs) > 24:
            runs = [(runs[0][0], runs[-1][1])]
        return Ref(ap, [(self.space, self.off + a, self.off + b) for (a, b) in runs])

    def __getitem__(self, idx):
        return self.ref(0, 128, idx)

    def p(self, p0, p1, *idx):
        return self.ref(p0, p1, tuple(idx) if idx else (slice(None),))


class Alloc:
    def __init__(self, mem, space, lo, hi):
        self.mem, self.space, self.cur, self.hi = mem, space, lo, hi

    def __call__(self, *shape):
        n = 1
        for s in shape:
            n *= s
        n2 = (n + 1) // 2 * 2
        b = Buf(self.mem, self.space, self.cur, shape)
        self.cur += n2
        assert self.cur <= self.hi, ("arena overflow", self.space, self.cur, self.hi)
        return b


class Ring:
    def __init__(self, bufs):
        self.bufs = bufs
        self.i = 0

    def next(self):
        b = self.bufs[self.i % len(self.bufs)]
        self.i += 1
        return b


def _dram_ref(ap):
    return Ref(ap, [])


NCONST = 128 * 6 + 256 + 64
C_ID, C_TRI, C_BLK, C_MSTR, C_ONES, C_ONESD = 0, 128, 256, 384, 512, 640
C_EYE16 = 768
C_RC16 = 1024


def make_consts():
    c = np.zeros((128, NCONST), np.float32)
    idx = np.arange(128)
    same = (idx[:, None] // 64) == (idx[None, :] // 64)
    c[:, C_ID:C_ID + 128] = np.eye(128, dtype=np.float32)
    c[:, C_TRI:C_TRI + 128] = (same & (idx[:, None] <= idx[None, :])).astype(np.float32)
    c[:, C_BLK:C_BLK + 128] = same.astype(np.float32)
    c[:, C_MSTR:C_MSTR + 128] = (same & (idx[:, None] < idx[None, :])).astype(np.float32)
    c[:, C_ONES:C_ONES + 128] = 1.0
    c[:, C_ONESD:C_ONESD + 128] = 1.0 / D
    e16 = np.eye(16, dtype=np.float32).reshape(1, 256)
    c[:, C_EYE16:C_EYE16 + 256] = e16
    rc = np.zeros((4, 16), np.float32)
    for gi, w in enumerate(POOL_W):
        rc[gi] = 1.0 / np.minimum(w, np.arange(16) + 1)
    c[:, C_RC16:C_RC16 + 64] = rc.reshape(1, 64)
    return c


class Kern:
    def __init__(self, nlayers=DEPTH, dbg=None, do_sample=True):
        self.nlayers = nlayers
        self.dbg = dbg
        self.do_sample = do_sample
        nc = bass.Bass("TRN2", target_bir_lowering=False)
        nc.dge_precook = False
        self.nc = nc
        self.es = ExitStack()

    def dram_in(self, name, shape, dt=F32):
        return self.nc.dram_tensor(name, list(shape), dt, kind="ExternalInput").ap()

    def dram_out(self, name, shape):
        return self.nc.dram_tensor(name, list(shape), F32, kind="ExternalOutput").ap()

    def mm(self, out, lhsT, rhs, start=True, stop=True, f32=False):
        o = out.ap
        l = lhsT.ap if f32 else lhsT.r
        r = rhs.ap if f32 else rhs.r
        self.S.op("pe", lambda h: h.matmul(o, l, r, start=start, stop=stop), [lhsT, rhs], [out])

    def tr(self, out, in_):
        o, i, idn = out.ap, in_.ap, self.ident.ap
        np_ = in_.ap.shape[0]
        idn = self.identb.p(0, np_, slice(0, np_)).ap
        self.S.op("pe", lambda h: h.transpose(o, i, idn), [in_, self.ident], [out])

    def act(self, out, in_, func, scale=1.0, bias=0.0, r=False, eng="act"):
        o = out.r if r else out.ap
        i = in_.ap
        reads = [in_]
        kw = {}
        if isinstance(scale, Ref):
            reads.append(scale)
            kw["scale"] = scale.ap
        elif scale != 1.0:
            kw["scale"] = float(scale)
        if isinstance(bias, Ref):
            reads.append(bias)
            kw["bias"] = bias.ap
        elif bias != 0.0:
            kw["bias"] = float(bias)
        self.S.op("act", lambda h: h.activation(o, i, func, **kw), reads, [out])

    def tt(self, out, in0, in1, op, r=False, eng="dve"):
        o = out.r if r else out.ap
        a, b = in0.ap, in1.ap
        self.S.op(eng, lambda h: h.tensor_tensor(o, a, b, op), [in0, in1], [out])

    def ts(self, out, in0, s1, op0, s2=None, op1=None, r=False, eng="dve"):
        o = out.r if r else out.ap
        a = in0.ap
        reads = [in0]
        v1 = s1
        if isinstance(s1, Ref):
            reads.append(s1)
            v1 = s1.ap
        v2 = s2
        if isinstance(s2, Ref):
            reads.append(s2)
            v2 = s2.ap
        if op1 is None:
            self.S.op(eng, lambda h: h.tensor_scalar(o, a, v1, None, op0), reads, [out])
        else:
            self.S.op(eng, lambda h: h.tensor_scalar(o, a, v1, v2, op0, op1), reads, [out])

    def stt(self, out, in0, scalar, in1, op0, op1, r=False):
        o = out.r if r else out.ap
        a, b = in0.ap, in1.ap
        reads = [in0, in1]
        sv = scalar
        if isinstance(scalar, Ref):
            reads.append(scalar)
            sv = scalar.ap
        self.S.op("dve", lambda h: h.scalar_tensor_tensor(o, a, sv, b, op0, op1), reads, [out])

    def cp(self, out, in_, r=False, eng="dve"):
        o = out.r if r else out.ap
        i = in_.ap
        if eng == "act":
            self.S.op("act", lambda h: h.copy(o, i), [in_], [out])
        else:
            self.S.op(eng, lambda h: h.tensor_copy(o, i), [in_], [out])

    def dma(self, out, in_, q="sp"):
        o, i = out.ap, in_.ap
        if i.dtype == F32R and o.dtype != F32R:
            o = o.bitcast(F32R)
        if o.dtype == F32R and i.dtype != F32R:
            i = i.bitcast(F32R)
        self.S.op(q, lambda h: h.dma_start(out=o, in_=i), [in_], [out], dma=True)

    def bc(self, ref, shape, axis):
        ap = ref.ap.unsqueeze(axis).broadcast_to(list(shape))
        return Ref(ap, ref.ivs)

    def sub(self, ref, *idx):
        return Ref(ref.ap[idx], ref.ivs)

    def build(self):
        nc, es = self.nc, self.es
        L = self.nlayers
        di, do = self.dram_in, self.dram_out
        self.d = d = {}
        d["xp"] = di("xp", (T, D))
        d["xs"] = di("xs", (NS, D))
        d["sdelta"] = di("sdelta", (DEPTH, NS, 4, 128, 128), F32R)
        d["sconv"] = di("sconv", (DEPTH, NS, 3, QKV))
        d["spool"] = di("spool", (DEPTH, NS, 15, 512))
        for nm, shp in (("wg1", (DEPTH, D, DFF)), ("wu1", (DEPTH, D, DFF)), ("wd1", (DEPTH, DFF, D)),
                        ("win", (DEPTH, D, INDIM)), ("wout", (DEPTH, D, D)),
                        ("wg2", (DEPTH, D, DFF)), ("wu2", (DEPTH, D, DFF)), ("wd2", (DEPTH, DFF, D)),
                        ("poolw", (DEPTH, 4, 128, 128))):
            d[nm] = di(nm, shp, F32R)
        for nm in ("ln1g", "ln1b", "ln2g", "ln2b", "ln3g", "ln3b"):
            d[nm] = di(nm, (DEPTH, D))
        d["convw"] = di("convw", (DEPTH, 4, QKV))
        d["alog"] = di("alog", (DEPTH, 4))
        d["dtb"] = di("dtb", (DEPTH, 4))
        d["onorm"] = di("onorm", (DEPTH, 128))
        d["pscale"] = di("pscale", (DEPTH, 512))
        d["constf"] = di("constf", (128, NCONST))
        d["constr"] = di("constr", (128, 384), F32R)
        d["yp"] = do("yp", (T, D))
        d["ys"] = do("ys", (NS, D))
        d["dprm"] = do("dprm", (DEPTH, 4, 128, 128))
        d["cprm"] = do("cprm", (DEPTH, 3, QKV))
        d["pprm"] = do("pprm", (DEPTH, 15, 512))
        d["dsmp"] = do("dsmp", (DEPTH, NS, 4, 128, 128))
        d["csmp"] = do("csmp", (DEPTH, NS, 3, QKV))
        d["psmp"] = do("psmp", (DEPTH, NS, 15, 512))
        if self.dbg:
            d["dbg"] = do("dbg", self.dbg)

        AWF, AWR = 10000, 43000
        arenaF = es.enter_context(nc.sbuf_tensor("arenaF", [128, AWF], F32))
        arenaR = es.enter_context(nc.sbuf_tensor("arenaR", [128, AWR], F32))
        psum = es.enter_context(nc.psum_tensor("psum", [128, 4096], F32))
        self.S = S = Sched(nc, es, {"sbF": AWF, "sbR": AWR, "ps": 4096})
        A = Alloc(arenaF, "sbF", 0, AWF)
        AR = Alloc(arenaR, "sbR", 0, AWR)
        self.arenaF, self.arenaR = arenaF, arenaR
        self.PS = [Buf(psum, "ps", 512 * i, (512,)) for i in range(8)]

        self.X = AR(DC, NTOK)
        self.cr = AR(384)
        self.poolw = AR(DEPTH * 4, 128)
        self.Sst = AR(4, 128)
        self.cf = A(NCONST)
        self.lnp = A(192)
        self.convw = A(192)
        self.pscale = A(16)
        self.onorm = A(4)
        self.dtb = A(16)
        self.nea = A(16)
        self.onorm_bc = A(DEPTH * 128)
        self.halo_c = A(12, 4)
        self.halo_p = A(4, 16)
        self.FZ = (A.cur, AWF)
        self.RZ = (AR.cur, AWR)

        cf, cr = self.cf, self.cr
        self.ident = cf[C_ID:C_ID + 128]
        self.identb = Buf(arenaF, "sbF", cf.off + C_ID, (128,))
        self.tri = cf[C_TRI:C_TRI + 128]
        self.blk = cf[C_BLK:C_BLK + 128]
        self.mstr = cf[C_MSTR:C_MSTR + 128]
        self.onesf = cf[C_ONES:C_ONES + 128]
        self.eye16 = cf[C_EYE16:C_EYE16 + 256]
        self.rc16 = cf[C_RC16:C_RC16 + 64]
        self.identr = cr[0:128]
        self.onesr = cr[128:256]
        self.onesdr = cr[256:384]

        self.load_consts()
        self.load_x()
        for l in range(L):
            self.layer(l)
        self.store_y()
        S.emit()
        return nc

    def load_consts(self):
        d = self.d
        self.dma(self.cf[:], _dram_ref(d["constf"]))
        self.dma(self.cr[:], _dram_ref(d["constr"]))
        A = Alloc(self.arenaF, "sbF", *self.FZ)
        stg = [A(128) for _ in range(5)]
        names = ("ln1g", "ln1b", "ln2g", "ln2b", "ln3g", "ln3b")
        for i, nm in enumerate(names):
            self.dma(stg[i // 3].p(32 * (i % 3), 32 * (i % 3) + 32), _dram_ref(d[nm].rearrange("l (c p) -> (l c) p", p=128)))
        cwv = d["convw"].rearrange("l i (c p) -> (l i c) p", p=128)
        self.dma(stg[2].p(0, 96), _dram_ref(cwv[0:96, :]))
        self.dma(stg[3].p(0, 96), _dram_ref(cwv[96:192, :]))
        self.dma(stg[4].p(0, 16), _dram_ref(d["pscale"].rearrange("l (c p) -> (l c) p", p=128)))
        self.dma(stg[4].p(16, 20), _dram_ref(d["onorm"]))
        ps = self.PS[0]
        for i in range(4):
            self.tr(ps[96 * i:96 * i + 96], stg[i].p(0, 96))
        self.tr(self.PS[1][0:20], stg[4].p(0, 20))
        self.cp(self.lnp[:], ps[0:192])
        self.cp(self.convw[:], ps[192:384])
        self.cp(self.pscale[:], self.PS[1][0:16])
        self.cp(self.onorm[:], self.PS[1][16:20])
        self.dma(self.dtb[:], _dram_ref(d["dtb"].rearrange("l h -> (l h)").partition_broadcast(128)))
        self.dma(self.nea[:], _dram_ref(d["alog"].rearrange("l h -> (l h)").partition_broadcast(128)))
        self.dma(self.onorm_bc[:], _dram_ref(d["onorm"].rearrange("l e -> (l e)").partition_broadcast(128)))
        for l in range(DEPTH):
            self.dma(self.poolw[4 * l:4 * l + 4], _dram_ref(d["poolw"][l].rearrange("g c e -> c g e")))
        self.act(self.nea[:], self.nea[:], AF.Exp)
        self.ts(self.nea[:], self.nea[:], -1.0, ALU.mult)

    def load_x(self):
        d = self.d
        A = Alloc(self.arenaF, "sbF", *self.FZ)
        stg = Ring([A(D) for _ in range(6)])
        k = 0
        for tb in range(T // 128 + 1):
            s = stg.next()
            if tb < T // 128:
                npart, col0 = 128, colp(tb * 128)
                self.dma(s.p(0, 128), _dram_ref(d["xp"][tb * 128:(tb + 1) * 128, :]))
            else:
                npart, col0 = NS, SC0
                self.dma(s.p(0, NS), _dram_ref(d["xs"]))
            for half in range(2):
                ps = self.PS[k % 8]
                k += 1
                for c4 in range(4):
                    c = half * 4 + c4
                    self.tr(ps.p(0, 128, slice(c4 * 128, c4 * 128 + npart)), s.p(0, npart, slice(c * 128, (c + 1) * 128)))
                src = Ref(ps.base[:, 0:512].rearrange("p (c t) -> p c t", c=4)[:, :, 0:npart], ps[:].ivs)
                dst = self.X[half * 4:(half + 1) * 4, col0:col0 + npart]
                if (k % 2) == 0:
                    self.cp(dst, src, r=True, eng="act")
                else:
                    self.cp(dst, src, r=True, eng="dve")

    def store_y(self):
        d = self.d
        A = Alloc(self.arenaF, "sbF", *self.FZ)
        stg = Ring([A(D) for _ in range(6)])
        k = 0
        for tb in range(T // 128 + 1):
            s = stg.next()
            if tb < T // 128:
                npart, col0 = 128, colp(tb * 128)
            else:
                npart, col0 = NS, SC0
            for half in range(2):
                ps = self.PS[k % 8]
                k += 1
                for c4 in range(4):
                    c = half * 4 + c4
                    self.tr(ps.p(0, npart, slice(c4 * 128, (c4 + 1) * 128)), self.X[c, col0:col0 + npart])
                dst = s.p(0, npart, slice(half * 512, (half + 1) * 512))
                src = ps.p(0, npart)
                if (k % 2) == 0:
                    self.cp(dst, src, eng="act")
                else:
                    self.cp(dst, src, eng="dve")
            if tb < T // 128:
                self.dma(_dram_ref(d["yp"][tb * 128:(tb + 1) * 128, :]), s.p(0, 128))
            else:
                self.dma(_dram_ref(d["ys"]), s.p(0, NS))

    def layer_norm(self, rb, loc, n, col0, gi, bi, l, eps, tmp, par=0):
        sqr, mean_sb, m2, rstd, tt_ = tmp
        pm, pq = (self.PS[6], self.PS[7]) if par == 0 else (self.PS[4], self.PS[5])
        for c in range(DC):
            sq = sqr.next()
            self.act(sq[0:n], rb[c, loc:loc + n], AF.Square, r=True)
            self.mm(pm[0:n], self.onesdr, rb[c, loc:loc + n], start=(c == 0), stop=(c == DC - 1))
            self.mm(pq[0:n], self.onesdr, sq[0:n], start=(c == 0), stop=(c == DC - 1))
        self.cp(mean_sb[0:n], pm[0:n], eng="act")
        self.tt(m2[0:n], mean_sb[0:n], mean_sb[0:n], ALU.mult)
        self.tt(m2[0:n], pq[0:n], m2[0:n], ALU.subtract)
        self.ts(m2[0:n], m2[0:n], 0.0, ALU.max)
        self.act(rstd[0:n], m2[0:n], AF.Ln, bias=self.eps_ref(eps))
        self.act(rstd[0:n], rstd[0:n], AF.Exp, scale=-0.5)
        for c in range(DC):
            t = tt_.next()
            self.tt(t[0:n], rb[c, loc:loc + n], mean_sb[0:n], ALU.subtract)
            self.tt(t[0:n], t[0:n], rstd[0:n], ALU.mult)
            self.act(self.X[c, col0:col0 + n], t[0:n], AF.Identity,
                     scale=self.lnp[gi * 32 + l * 8 + c:gi * 32 + l * 8 + c + 1], bias=self.lnp[bi * 32 + l * 8 + c:bi * 32 + l * 8 + c + 1], r=True)

    def eps_ref(self, eps):
        return float(eps)

    def ffn(self, l, which, tiles):
        self.mark('ffn%d_%d_%d' % (l, which, len(tiles)))
        d = self.d
        wg, wu, wd = (d["wg1"], d["wu1"], d["wd1"]) if which == 1 else (d["wg2"], d["wu2"], d["wd2"])
        gi, bi = (0, 1) if which == 1 else (4, 5)
        G = sum(n for _, n in tiles)
        locs = []
        o = 0
        for (_, n) in tiles:
            locs.append(o)
            o += n
        groups = [(0, 5), (5, 10), (10, 14), (14, 18), (18, 22)]
        AR = Alloc(self.arenaR, "sbR", *self.RZ)
        AF_ = Alloc(self.arenaF, "sbF", *self.FZ)
        GM = 1040
        rb = AR(DC, GM)
        hid = AR(5, GM)
        wgu = Ring([AR(2, DC, 128) for _ in range(3)])
        wdr = Ring([AR(5, 128) for _ in range(3)])
        sgr = Ring([AF_(512) for _ in range(2)])
        tmp = (Ring([AR(512) for _ in range(2)]), AF_(512), AF_(512), AF_(512), Ring([AF_(512) for _ in range(2)]))
        wgv = wg[l].rearrange("(kc p) f -> p kc f", p=128)
        wuv = wu[l].rearrange("(kc p) f -> p kc f", p=128)
        pgi = 0
        pdi = 0
        for g, (f0, f1) in enumerate(groups):
            nj = f1 - f0
            for j in range(f0, f1):
                w = wgu.next()
                self.dma(Ref(w[0].r, w[0].ivs), _dram_ref(wgv[:, :, j * 128:(j + 1) * 128]))
                self.dma(Ref(w[1].r, w[1].ivs), _dram_ref(wuv[:, :, j * 128:(j + 1) * 128]))
                for ti, (col0, n) in enumerate(tiles):
                    pg = self.PS[pgi % 2]
                    pu = self.PS[2 + pgi % 2]
                    pgi += 1
                    for k in range(DC):
                        self.mm(pg[0:n], w[0, k], self.X[k, col0:col0 + n], start=(k == 0), stop=(k == DC - 1))
                    for k in range(DC):
                        self.mm(pu[0:n], w[1, k], self.X[k, col0:col0 + n], start=(k == 0), stop=(k == DC - 1))
                    sg = sgr.next()
                    self.act(sg[0:n], pg[0:n], AF.Silu)
                    self.tt(hid[j - f0, locs[ti]:locs[ti] + n], sg[0:n], pu[0:n], ALU.mult, r=True)
            for m in range(DC):
                wb = wdr.next()
                src = wd[l][f0 * 128:f1 * 128, m * 128:(m + 1) * 128].rearrange("(j p) c -> p j c", p=128)
                dst = wb[0:nj]
                self.dma(Ref(dst.r, dst.ivs), _dram_ref(src))
                for ti, (col0, n) in enumerate(tiles):
                    pa = self.PS[4 + pdi % 2]
                    pdi += 1
                    for j in range(nj):
                        self.mm(pa[0:n], wb[j], hid[j, locs[ti]:locs[ti] + n], start=(j == 0), stop=(j == nj - 1))
                    rr = rb[m, locs[ti]:locs[ti] + n]
                    if g == 0:
                        self.stt(rr, self.X[m, col0:col0 + n], 2.0 * ALPHA, pa[0:n], ALU.mult, ALU.add, r=True)
                    else:
                        self.tt(rr, rr, pa[0:n], ALU.add, r=True)
        if getattr(self, "skip_ln", False):
            return
        for ti, (col0, n) in enumerate(tiles):
            self.layer_norm(rb, locs[ti], n, col0, gi, bi, l, 4.0 * LN_EPS, tmp, par=ti % 2)

    def layer(self, l):
        ffn_tiles = [[(0, 348), (348, 348), (696, 344)], [(1040, 512), (1552, 512)]]
        if getattr(self, "only_sample", False):
            self.mixer_sample(l)
            return
        if getattr(self, "only_mix", False):
            self.mixer_prompt(l, 0)
            return
        for p in range(2):
            self.ffn(l, 1, ffn_tiles[p])
        if getattr(self, "ffn_only", False):
            return
        for c0 in (0, 512):
            self.mixer_prompt(l, c0)
        if self.do_sample:
            self.mixer_sample(l)
        for c0 in (1024, 1536):
            self.mixer_prompt(l, c0)
        for p in range(2):
            self.ffn(l, 2, ffn_tiles[p])

    def ms(self, out, val, r=False):
        o = out.r if r else out.ap
        self.S.op("dve", lambda h: h.memset(o, val), [], [out])

    def v4(self, ref, a=4):
        return Ref(ref.ap.rearrange("p (a b) -> p a b", a=a), ref.ivs)

    def mark(self, name):
        if not hasattr(self, 'marks'):
            self.marks = []
        self.marks.append((name, len(self.S.ops['pe'])))

    def mixer_prompt(self, l, c0):
        d = self.d
        self.mark('mix%d_%d_p1pool' % (l, c0))
        N = 512
        first = (c0 == 0)
        last = (c0 + N == T)
        PS = self.PS
        winv = d["win"][l].rearrange("(kc p) f -> p kc f", p=128)
        woutv = d["wout"][l].rearrange("(kc p) f -> p kc f", p=128)
        xc = colp(c0)
        Xs = lambda k: self.X[k, xc:xc + N]
        AR = Alloc(self.arenaR, "sbR", *self.RZ)
        qkvc = AR(12, N)
        mix = AR(8, N)
        wblk = Ring([AR(DC, 128) for _ in range(3)])
        wba = AR(DC, 8)
        RP = AR.cur
        AFt = Alloc(self.arenaF, "sbF", self.FZ[1] - 96, self.FZ[1])
        ba = AFt(4, 8)
        beta = AFt(4, 4)
        yv = AFt(4, 4)
        gv = AFt(4, 4)
        self.dma(wba[:], _dram_ref(winv[:, :, 2048:2056]))
        for i in range(4):
            for k in range(DC):
                self.mm(PS[4][8 * i:8 * i + 8], self.X[k, xc + 128 * i:xc + 128 * (i + 1)], wba[k], start=(k == 0), stop=(k == DC - 1))
        self.cp(ba[:], self.v4(PS[4][0:32]), eng="act")
        self.act(beta[:], ba[:, 0:4], AF.Sigmoid)
        dtb_l = self.dtb[l * 4:(l + 1) * 4]
        nea_l = self.nea[l * 4:(l + 1) * 4]
        self.tt(yv[:], ba[:, 4:8], self.bc(dtb_l, [128, 4, 4], 1), ALU.add)
        self.ts(yv[:], yv[:], 30.0, ALU.min)
        self.act(yv[:], yv[:], AF.Exp)
        self.act(yv[:], yv[:], AF.Ln, bias=1.0)
        self.tt(gv[:], yv[:], self.bc(nea_l, [128, 4, 4], 1), ALU.mult)
        AR1 = Alloc(self.arenaR, "sbR", RP, self.RZ[1])
        dd = AR1(4, N)
        pbr = Ring([(AR1(528), AR1(528)) for _ in range(3)])
        dgp = Ring([AR1(2, 128) for _ in range(3)])
        AR1b = Alloc(self.arenaR, "sbR", RP, self.RZ[1])
        sqr = Ring([AR1b(N) for _ in range(2)])
        preR = Ring([(AR1b(516), AR1b(516)) for _ in range(3)])
        dgr = Ring([AR1b(4, 128) for _ in range(3)])
        AF1 = Alloc(self.arenaF, "sbF", *self.FZ)
        acc = Ring([AF1(N) for _ in range(2)])
        rn = Ring([AF1(N) for _ in range(2)])
        z16 = AF1(16)
        t16 = AF1(16)
        pst = AF1(512)
        cst = AF1(QKV)
        if first:
            self.ms(z16[:], 0.0)
        pool_ctx = {}

        def poolA(g):
            w = POOL_W[g]
            wb = wblk.next()
            self.dma(wb[:], _dram_ref(winv[:, :, 2056 + 128 * g:2056 + 128 * (g + 1)]))
            ps = PS[g % 2]
            for k in range(DC):
                self.mm(ps[:], wb[k], Xs(k), start=(k == 0), stop=(k == DC - 1))
            pbA, pbB = pbr.next()
            if first:
                self.cp(pbA[0:16], z16[:], r=True)
                self.cp(pbB[0:15], z16[0:15], r=True)
            else:
                self.cp(pbA[1:16], self.halo_p[g, 1:16], r=True)
                self.cp(pbB[0:15], self.halo_p[g, 1:16], r=True)
            self.cp(pbA[16:528], ps[:], r=True, eng="act")
            self.cp(pbB[15:527], pbA[16:528], r=True)
            self.cp(self.halo_p[g, 1:16], pbA[513:528])
            dg = dgp.next()
            self.ts(dg[0], self.ident, 1.0 / w - 1.0, ALU.mult, r=True)
            self.ts(dg[1], self.ident, 1.0 / w, ALU.mult, r=True)
            pool_ctx[g] = (pbA, pbB, dg)

        def poolB(g):
            w = POOL_W[g]
            pbA, pbB, dg = pool_ctx[g]
            pd = PS[2 + g % 2]
            for i in range(w):
                win = pbA[16 - i:528 - i] if i % 2 == 0 else pbB[15 - i:527 - i]
                self.mm(pd[:], dg[0 if i == 0 else 1], win, start=(i == 0), stop=(i == w - 1))
            self.cp(dd[g], pd[:], r=True, eng="act")
            if first:
                self.tt(t16[:], pd[0:16], pbA[16:32], ALU.add)
                self.tt(t16[:], t16[:], self.sub(self.rc16, slice(None), slice(g * 16, (g + 1) * 16)), ALU.mult)
                self.stt(dd[g, 0:16], t16[:], float(w), pbA[16:32], ALU.mult, ALU.subtract, r=True)
            ps2 = PS[4 + g % 2]
            self.mm(ps2[:], self.poolw[4 * l + g], dd[g])
            self.act(mix[4 + g], ps2[:], AF.Identity, scale=self.pscale[l * 4 + g:l * 4 + g + 1], r=True)
            if last:
                self.tr(PS[6].p(0, 15, slice(g * 128, (g + 1) * 128)), pbA[513:528])

        for st_ in range(5):
            if st_ < 4:
                poolA(st_)
            if st_ >= 1:
                poolB(st_ - 1)
        if last:
            self.cp(pst.p(0, 15), PS[6].p(0, 15), eng="act")
            self.dma(_dram_ref(d["pprm"][l]), pst.p(0, 15))
        self.mark('mix%d_%d_p1qkv' % (l, c0))
        qctx = {}

        def qkvA(oc):
            wb = wblk.next()
            self.dma(wb[:], _dram_ref(winv[:, :, 128 * oc:128 * (oc + 1)]))
            ps = PS[oc % 2]
            for k in range(DC):
                self.mm(ps[:], wb[k], Xs(k), start=(k == 0), stop=(k == DC - 1))
            pr, prB = preR.next()
            if first:
                self.cp(pr[0:4], z16[0:4], r=True)
                self.cp(prB[0:3], z16[0:3], r=True)
            else:
                self.cp(pr[1:4], self.halo_c[oc, 1:4], r=True)
                self.cp(prB[0:3], self.halo_c[oc, 1:4], r=True)
            self.cp(pr[4:516], ps[:], r=True, eng="act")
            self.cp(prB[3:515], pr[4:516], r=True)
            self.cp(self.halo_c[oc, 1:4], pr[513:516])
            dg = dgr.next()
            for i in range(4):
                self.ts(dg[i], self.ident, self.convw[l * 48 + i * 12 + oc:l * 48 + i * 12 + oc + 1], ALU.mult, r=True)
            qctx[oc] = [pr, prB, dg, None]

        def qkvB(oc):
            pr, prB, dg, _ = qctx[oc]
            pc = PS[2 + oc % 2]
            for i in range(4):
                self.mm(pc[:], dg[i], (pr[1 + i:513 + i] if i % 2 else prB[i:512 + i]), start=(i == 0), stop=(i == 3))
            if oc >= 8:
                self.act(qkvc[oc], pc[:], AF.Silu, r=True)
            else:
                a = acc.next()
                qctx[oc][3] = a
                self.act(a[:], pc[:], AF.Silu)
                sq = sqr.next()
                self.act(sq[:], a[:], AF.Square, r=True)
                self.mm(PS[4 + oc % 2][:], self.onesr, sq[:])

        def qkvC(oc):
            pr, prB, dg, a = qctx[oc]
            if oc < 8:
                r_ = rn.next()
                self.act(r_[:], PS[4 + oc % 2][:], AF.Ln, bias=L2_EPS)
                self.act(r_[:], r_[:], AF.Exp, scale=-0.5, bias=(-0.5 * float(np.log(128.0)) if oc < 4 else 0.0))
                self.tt(qkvc[oc], a[:], r_[:], ALU.mult, r=True)
            if last:
                self.tr(PS[7].p(0, 3, slice((oc % 4) * 128, (oc % 4 + 1) * 128)), pr[513:516])
                if oc % 4 == 3:
                    b3 = oc // 4
                    self.cp(cst.p(0, 3, slice(b3 * 512, (b3 + 1) * 512)), PS[7].p(0, 3), eng="act")

        for st_ in range(14):
            if st_ < 12:
                qkvA(st_)
            if 1 <= st_ <= 12:
                qkvB(st_ - 1)
            if st_ >= 2:
                qkvC(st_ - 2)
        if last:
            self.dma(_dram_ref(d["cprm"][l]), cst.p(0, 3))
        if getattr(self, 'mix_stop', 99) <= 2:
            return
        self.mark('mix%d_%d_p2' % (l, c0))
        AR2 = Alloc(self.arenaR, "sbR", RP, self.RZ[1])
        ctxs = []
        for _p in range(2):
            ctxs.append(dict(TI5=AR2(N), QKm=AR2(N), kend=AR2(N), nwk=AR2(N), qdec=AR2(N), X5=AR2(4, 256)))
        Xtmp = AR2(4, 256)
        Tb = AR2(N)
        Pb = AR2(N)
        usb = AR2(N)
        AF2 = Alloc(self.arenaF, "sbF", *self.FZ)
        gcs = AF2(8)
        egc = AF2(4)
        kes = AF2(4)
        nbe = AF2(4)
        grhs = AF2(N)
        brhs = AF2(N)
        diff = AF2(N)
        dinc = AF2(N)
        W1 = AF2(N)
        for _p in range(2):
            ctxs[_p]["egr"] = AF2(N)
        if first:
            self.ms(grhs[:], 0.0)
            self.cp(self.Sst[:], self.v4(grhs[:]), r=True)
        H = lambda buf, h: buf[128 * h:128 * (h + 1)]

        def stageA(i):
            cx = ctxs[i % 2]
            TI5, QKm, kend, nwk, qdec, X5, egr = cx["TI5"], cx["QKm"], cx["kend"], cx["nwk"], cx["qdec"], cx["X5"], cx["egr"]
            tc0 = 128 * i
            qh = lambda h: qkvc[h, tc0:tc0 + 128]
            kh = lambda h: qkvc[4 + h, tc0:tc0 + 128]
            vh = lambda h: qkvc[8 + h, tc0:tc0 + 128]

            def s1():
                self.mm(PS[4][32:36], self.tri, gv[i], f32=True)
                self.mm(PS[4][36:40], self.blk, gv[i], f32=True)
                self.cp(gcs[:], PS[4][32:40], eng="act")
                self.act(egc[:], gcs[0:4], AF.Exp)
                self.tt(kes[:], gcs[4:8], gcs[0:4], ALU.subtract)
                self.act(kes[:], kes[:], AF.Exp)
                self.stt(nbe[:], beta[i], -1.0, egc[:], ALU.mult, ALU.mult)
                self.tt(self.v4(grhs[:]), self.bc(self.tri, [128, 4, 128], 1), self.bc(gv[i], [128, 4, 128], 2), ALU.mult)
                self.tt(self.v4(brhs[:]), self.bc(self.ident, [128, 4, 128], 1), self.bc(beta[i], [128, 4, 128], 2), ALU.mult)
                self.mm(PS[0][:], self.onesf, grhs[:], f32=True)
                self.mm(PS[1][:], self.onesf, brhs[:], f32=True)
                for h in range(4):
                    self.ts(H(diff, h), H(PS[0], h), gcs[h:h + 1], ALU.subtract, 0.0, ALU.min)
                self.act(egr[:], PS[0][:], AF.Exp)
                self.act(diff[:], diff[:], AF.Exp)
                self.tt(self.v4(dinc[:]), self.v4(diff[:]), self.bc(self.tri, [128, 4, 128], 1), ALU.mult)
                self.tt(self.v4(W1[:]), self.v4(diff[:]), self.bc(self.mstr, [128, 4, 128], 1), ALU.mult)
                self.stt(W1[:], W1[:], -1.0, PS[1][:], ALU.mult, ALU.mult)
                self.tt(self.v4(qdec[:]), qkvc[0:4, tc0:tc0 + 128], self.v4(egr[:]), ALU.mult, r=True)

            def s2():
                for h in range(4):
                    self.tr(H(PS[0], h), kh(h))
                for h in range(4):
                    self.tr(H(PS[1], h), vh(h))
                self.tt(Xtmp[:, 0:128], self.v4(PS[1][:]), self.bc(beta[i], [128, 4, 128], 2), ALU.mult, r=True)
                self.tt(Xtmp[:, 128:256], self.v4(PS[0][:]), self.bc(nbe[:], [128, 4, 128], 2), ALU.mult, r=True)
                self.tt(self.v4(kend[:]), self.v4(PS[0][:]), self.bc(kes[:], [128, 4, 128], 2), ALU.mult, r=True)

            def s3():
                for h in range(4):
                    self.mm(H(PS[2], h), kh(h), kh(h))
                for h in range(4):
                    self.mm(H(PS[3], h), kh(h), qh(h))
                self.tt(Tb[:], PS[2][:], W1[:], ALU.mult, r=True)
                self.tt(QKm[:], PS[3][:], dinc[:], ALU.mult, r=True)
                for h in range(4):
                    self.tr(H(PS[0], h), H(Tb, h))
                self.cp(Pb[:], PS[0][:], r=True, eng="act")

            def level(kx):
                def f():
                    Xk, Xn = (Xtmp, X5) if kx % 2 == 0 else (X5, Xtmp)
                    for h in range(4):
                        bank = PS[h // 2]
                        self.mm(bank[256 * (h % 2):256 * (h % 2 + 1)], H(Tb, h), Xk[h])
                    for h in range(4):
                        self.mm(H(PS[2], h), H(Pb, h), H(Tb, h))
                    if kx < 4:
                        for h in range(4):
                            self.mm(H(PS[3], h), H(Tb, h), H(Pb, h))
                    for hp in range(2):
                        self.tt(Xn[2 * hp:2 * hp + 2], Xk[2 * hp:2 * hp + 2], self.v4(PS[hp][:], 2), ALU.add, r=True)
                    if kx < 4:
                        self.cp(Tb[:], PS[2][:], r=True, eng="act")
                        self.cp(Pb[:], PS[3][:], r=True, eng="act")
                    else:
                        self.tt(self.v4(TI5[:]), self.v4(PS[2][:]), self.bc(self.ident, [128, 4, 128], 1), ALU.add, r=True)
                        for h in range(4):
                            self.mm(H(PS[4], h), X5[h, 128:256], H(TI5, h))
                        self.cp(nwk[:], PS[4][:], r=True, eng="act")
                return f

            return [s1, s2, s3] + [level(kx) for kx in range(5)]

        def stageB(i):
            cx = ctxs[i % 2]
            TI5, QKm, kend, nwk, qdec, X5, egr = cx["TI5"], cx["QKm"], cx["kend"], cx["nwk"], cx["qdec"], cx["X5"], cx["egr"]
            tc0 = 128 * i
            steps = []
            for c in range(2):
                p0, p1 = 64 * c, 64 * c + 64

                def a_(c=c, p0=p0, p1=p1):
                    for h in range(4):
                        self.mm(H(PS[7], h), H(TI5, h), X5[h, 0:128], start=True, stop=False)
                        self.mm(H(PS[7], h), H(nwk, h), self.Sst[h], start=False, stop=True)

                def b_(c=c, p0=p0, p1=p1):
                    self.cp(usb.p(p0, p1), PS[7].p(p0, p1), r=True, eng="act")

                def c_(c=c, p0=p0, p1=p1):
                    for h in range(4):
                        oc_ = slice(128 * h + 64 * c, 128 * h + 64 * c + 64)
                        hs = slice(128 * h, 128 * h + 128)
                        self.mm(PS[5][oc_], self.Sst[h], qdec[oc_], start=True, stop=False)
                        self.mm(PS[5][oc_], usb.p(p0, p1, hs), QKm.p(p0, p1, oc_), start=False, stop=True)
                    for h in range(4):
                        hs = slice(128 * h, 128 * h + 128)
                        self.mm(H(PS[6], h), kend.p(p0, p1, hs), usb.p(p0, p1, hs))

                def d_(c=c, p0=p0, p1=p1):
                    for h in range(4):
                        ge = egr[128 * h + 64 * c + 63:128 * h + 64 * c + 64]
                        self.stt(self.Sst[h], self.Sst[h], ge, H(PS[6], h), ALU.mult, ALU.add, r=True)

                steps += [a_, b_, c_, d_]

            def fin():
                self.cp(mix[0:4, tc0:tc0 + 128], self.v4(PS[5][:]), r=True, eng="act")
            return steps, fin

        zbuf = [Buf(self.arenaF, "sbF", self.FZ[0] + 4224 + N * h_, (N,)) for h_ in range(4)]

        def zA1(h):
            def f():
                wb = wblk.next()
                self.dma(wb[:], _dram_ref(winv[:, :, QKV + 128 * h:QKV + 128 * (h + 1)]))
                ps = PS[h % 2]
                for k in range(DC):
                    self.mm(ps[:], wb[k], Xs(k), start=(k == 0), stop=(k == DC - 1))
                self.act(zbuf[h][:], ps[:], AF.Silu)
            return f

        for f in stageA(0):
            f()
        for i in range(4):
            self.mark('mix%d_%d_p2tile%d' % (l, c0, i))
            segsA = stageA(i + 1) if i < 3 else [zA1(h_) for h_ in range(4)]
            stepsB, fin = stageB(i)
            order = []
            ai = 0
            for bi in range(0, 8, 2):
                if ai < len(segsA):
                    order.append(segsA[ai])
                    ai += 1
                order += stepsB[bi:bi + 2]
            order += segsA[ai:]
            for f in order:
                f()
            fin()
        if last:
            for h in range(4):
                self.dma(_dram_ref(d["dprm"][l, h]), self.Sst[h])
        self.mark('mix%d_%d_p3' % (l, c0))
        AR3 = Alloc(self.arenaR, "sbR", RP, self.RZ[1])
        rb = AR3(DC, N)
        sqr3 = Ring([AR3(N) for _ in range(2)])
        AF3 = Alloc(self.arenaF, "sbF", *self.FZ)
        zt = Ring([AF3(N) for _ in range(2)])
        rt = Ring([AF3(N) for _ in range(2)])
        tmp = (sqr3, AF3(N), AF3(N), AF3(N), Ring([AF3(N) for _ in range(2)]))
        zctx = {}

        def zA(h):
            z = zbuf[h]
            sq = sqr3.next()
            self.act(sq[:], mix[h], AF.Square, r=True)
            self.mm(PS[4 + h % 2][:], self.onesr, sq[:])
            zctx[h] = z

        def zB(h):
            z = zctx[h]
            r_ = rt.next()
            self.act(r_[:], PS[4 + h % 2][:], AF.Ln, scale=1.0 / 128.0, bias=RMS_EPS)
            self.act(r_[:], r_[:], AF.Exp, scale=-0.5)
            self.tt(z[:], z[:], r_[:], ALU.mult)
            self.stt(mix[h], mix[h], self.onorm[l:l + 1], z[:], ALU.mult, ALU.mult, r=True)

        for st_ in range(5):
            if st_ < 4:
                zA(st_)
            if st_ >= 1:
                zB(st_ - 1)
        for m in range(DC):
            wb = wblk.next()
            self.dma(wb[:], _dram_ref(woutv[:, :, 128 * m:128 * (m + 1)]))
            ps = PS[m % 4]
            for k in range(DC):
                self.mm(ps[:], wb[k], mix[k], start=(k == 0), stop=(k == DC - 1))
            self.stt(rb[m], Xs(m), ALPHA, ps[:], ALU.mult, ALU.add, r=True)
        self.layer_norm(rb, 0, N, xc, 2, 3, l, LN_EPS, tmp)

    def red(self, out, in_, op=None):
        o, i = out.ap, in_.ap
        op = op or ALU.add
        self.S.op("dve", lambda h: h.tensor_reduce(o, i, AX.X, op), [in_], [out])

    def mixer_sample(self, l):
        d = self.d
        self.mark('sample%d' % l)
        PS = self.PS
        P = NS
        winv = d["win"][l].rearrange("(kc p) f -> p kc f", p=128)
        woutv = d["wout"][l].rearrange("(kc p) f -> p kc f", p=128)
        Xs = lambda k: self.X[k, SC0:SC0 + NS]
        ident16 = Ref(self.ident.ap[0:P, 0:P], self.ident.ivs)
        ones16 = Ref(self.onesf.ap[0:P, 0:128], self.onesf.ivs)
        v3 = lambda ref, a: Ref(ref.ap.rearrange("p (a b) -> p a b", a=a), ref.ivs)
        AR = Alloc(self.arenaR, "sbR", *self.RZ)
        wblk = Ring([AR(DC, 128) for _ in range(3)])
        wba = AR(DC, 8)
        Sall = AR(4 * NS, 128)
        kqT = AR(8, P)
        km = AR(4, P, P)
        qm = AR(4, P, P)
        qkn = AR(1024)
        umr = Ring([AR(512) for _ in range(2)])
        rhsr = AR(64)
        mixs = AR(8, P)
        ddT = AR(4, P)
        rbs = AR(DC, P)
        sqs = Ring([AR(P) for _ in range(2)])
        wblk = Ring(wblk.bufs + [AR(DC, 128) for _ in range(6)])
        AFa = Alloc(self.arenaF, "sbF", *self.FZ)
        qkv = AFa(QKV)
        zt = AFa(512)
        pt = AFa(512)
        bat = AFa(8)
        F2 = AFa.cur
        cst = AFa(3, 512)
        cw = AFa(4, 512)
        sdv = d["sdelta"][l].rearrange("s h d e -> d (s h) e")
        for j in range(8):
            self.dma(Sall[8 * j:8 * j + 8], _dram_ref(sdv[:, 8 * j:8 * j + 8, :]))
        cols = [128 * b for b in range(16)] + [2056 + 128 * g for g in range(4)]
        dsts = [qkv.p(0, P, slice(0, 512)), qkv.p(0, P, slice(512, 1024)), qkv.p(0, P, slice(1024, 1536)), zt.p(0, P), pt.p(0, P)]
        for grp in range(5):
            bank = PS[grp % 4]
            for b4 in range(4):
                wb = wblk.next()
                c = cols[4 * grp + b4]
                self.dma(wb[:], _dram_ref(winv[:, :, c:c + 128]))
                for k in range(DC):
                    self.mm(bank.p(0, P, slice(128 * b4, 128 * b4 + 128)), Xs(k), wb[k], start=(k == 0), stop=(k == DC - 1))
            self.cp(dsts[grp], bank.p(0, P), eng="act")
        self.dma(wba[:], _dram_ref(winv[:, :, 2048:2056]))
        for k in range(DC):
            self.mm(PS[4].p(0, P, slice(0, 8)), Xs(k), wba[k], start=(k == 0), stop=(k == DC - 1))
        self.cp(bat.p(0, P), PS[4].p(0, P, slice(0, 8)), eng="act")
        if getattr(self, 'smp_stop', 99) <= 1:
            return
        self.mark('smp%d_1' % l)
        self.dma(_dram_ref(d["csmp"][l, :, 0:2, :]), _dram_ref(d["sconv"][l, :, 1:3, :]))
        self.dma(_dram_ref(d["csmp"][l, :, 2, :]), qkv.p(0, P))
        self.dma(_dram_ref(d["psmp"][l, :, 0:14, :]), _dram_ref(d["spool"][l, :, 1:15, :]))
        self.dma(_dram_ref(d["psmp"][l, :, 14, :]), pt.p(0, P))
        if getattr(self, 'smp_stop', 99) <= 2:
            return
        self.mark('smp%d_2' % l)
        for j in range(3):
            self.dma(cst.p(0, P), _dram_ref(d["sconv"][l][:, :, 512 * j:512 * (j + 1)]))
            for i in range(4):
                self.dma(cw.p(0, P, i), _dram_ref(d["convw"][l, i, 512 * j:512 * (j + 1)].partition_broadcast(P)))
            pre = qkv.p(0, P, slice(512 * j, 512 * (j + 1)))
            tA = Buf(self.arenaF, "sbF", AFa.cur, (512,)).p(0, P)
            tB = Buf(self.arenaF, "sbF", AFa.cur + 512, (512,)).p(0, P)
            self.tt(tA, pre, cw.p(0, P, 3), ALU.mult)
            for i in range(3):
                self.tt(tB, cst.p(0, P, i), cw.p(0, P, i), ALU.mult)
                self.tt(tA, tA, tB, ALU.add)
            self.act(pre, tA, AF_SILU)
        if getattr(self, 'smp_stop', 99) <= 3:
            return
        self.mark('smp%d_3' % l)
        AFb = Alloc(self.arenaF, "sbF", F2, self.FZ[1])
        t1024 = AFb(1024)
        ss8 = AFb(8)
        rn8 = AFb(8)
        beta = AFb(4)
        yv = AFb(4)
        eg = AFb(4)
        nbe = AFb(4)
        qkd = AFb(4)
        rhse = AFb(P, 4)
        egb = AFb(64)
        u = AFb(512)
        t2 = AFb(512)
        o = AFb(512)
        ss4 = AFb(4)
        dtok = AFb(512)
        sumg = AFb(128)
        pst1 = Buf(self.arenaF, "sbF", F2, (15, 128))
        self.ms(t1024[:], 0.0)
        self.cp(qkn[:], t1024[:], r=True)
        self.cp(rhsr[:], t1024[0:64], r=True)
        for _ in range(2):
            umz = umr.next()
            self.cp(umz[:], t1024[0:512], r=True)
        qk = qkv.p(0, P, slice(0, 1024))
        self.tt(t1024.p(0, P), qk, qk, ALU.mult)
        self.red(ss8.p(0, P), v3(t1024.p(0, P), 8))
        self.act(rn8.p(0, P), ss8.p(0, P), AF.Ln, bias=L2_EPS)
        self.act(rn8.p(0, P), rn8.p(0, P), AF.Exp, scale=-0.5)
        self.ts(rn8.p(0, P, slice(0, 4)), rn8.p(0, P, slice(0, 4)), float(128.0 ** -0.5), ALU.mult)
        self.tt(v3(qkn.p(0, P), 8), v3(qk, 8), self.bc(rn8.p(0, P), [P, 8, 128], 2), ALU.mult, r=True)
        qn = qkn.p(0, P, slice(0, 512))
        kn = qkn.p(0, P, slice(512, 1024))
        vt = qkv.p(0, P, slice(1024, 1536))
        if getattr(self, 'smp_stop', 99) <= 4:
            return
        self.mark('smp%d_4' % l)
        self.act(beta.p(0, P), bat.p(0, P, slice(0, 4)), AF.Sigmoid)
        self.tt(yv.p(0, P), bat.p(0, P, slice(4, 8)), self.dtb.p(0, P, slice(4 * l, 4 * l + 4)), ALU.add)
        self.ts(yv.p(0, P), yv.p(0, P), 30.0, ALU.min)
        self.act(yv.p(0, P), yv.p(0, P), AF.Exp)
        self.act(yv.p(0, P), yv.p(0, P), AF.Ln, bias=1.0)
        self.tt(yv.p(0, P), yv.p(0, P), self.nea.p(0, P, slice(4 * l, 4 * l + 4)), ALU.mult)
        self.act(eg.p(0, P), yv.p(0, P), AF.Exp)
        self.stt(nbe.p(0, P), beta.p(0, P), -1.0, eg.p(0, P), ALU.mult, ALU.mult)
        if getattr(self, 'smp_stop', 99) <= 5:
            return
        self.mark('smp%d_5' % l)
        for j in range(8):
            self.tr(PS[5][P * j:P * j + P], qkn.p(0, P, slice(128 * j, 128 * j + 128)))
        self.cp(kqT[:], v3(PS[5][0:8 * P], 8), r=True, eng="act")
        eye = v3(self.eye16, P)
        self.tt(km[:], self.bc(kqT[4:8], [128, 4, P, P], 2), self.bc(eye, [128, 4, P, P], 1), ALU.mult, r=True)
        self.tt(qm[:], self.bc(kqT[0:4], [128, 4, P, P], 2), self.bc(eye, [128, 4, P, P], 1), ALU.mult, r=True)
        for (msk, bank) in ((km, PS[6]), (qm, PS[7])):
            for h in range(4):
                for i in range(P):
                    self.mm(bank.p(0, P, slice(128 * h, 128 * h + 128)), msk[h, i], Sall[4 * i + h], start=(i == 0), stop=(i == P - 1))
        if getattr(self, 'smp_stop', 99) <= 6:
            return
        self.mark('smp%d_6' % l)
        b3 = lambda x: self.bc(x.p(0, P), [P, 4, 128], 2)
        self.tt(v3(u.p(0, P), 4), v3(vt, 4), b3(beta), ALU.mult)
        self.tt(v3(t2.p(0, P), 4), v3(PS[6].p(0, P), 4), b3(nbe), ALU.mult)
        self.tt(u.p(0, P), u.p(0, P), t2.p(0, P), ALU.add)
        self.tt(t2.p(0, P), qn, kn, ALU.mult)
        self.red(qkd.p(0, P), v3(t2.p(0, P), 4))
        self.tt(v3(o.p(0, P), 4), v3(PS[7].p(0, P), 4), b3(eg), ALU.mult)
        self.tt(v3(t2.p(0, P), 4), v3(u.p(0, P), 4), b3(qkd), ALU.mult)
        self.tt(o.p(0, P), o.p(0, P), t2.p(0, P), ALU.add)
        if getattr(self, 'smp_stop', 99) <= 7:
            return
        self.mark('smp%d_7' % l)
        self.tt(Ref(rhsr.p(0, P).ap.rearrange("p (a b) -> p a b", a=P), rhsr[:].ivs), self.bc(eg.p(0, P), [P, P, 4], 1), self.bc(ident16, [P, P, 4], 2), ALU.mult, r=True)
        self.mm(PS[4][64:128], self.onesr, rhsr[:])
        self.cp(egb[:], PS[4][64:128], eng="act")
        if getattr(self, 'smp_stop', 99) <= 7.1:
            return
        import os as _os
        ums = {}

        def mk_um(i):
            ums[i] = umr.next()
            self.ts(ums[i].p(0, P), u.p(0, P), Ref(self.ident.ap[0:P, i:i + 1], self.ident.ivs), ALU.mult, r=True)

        mk_um(0)
        for i in range(P if not _os.environ.get('DBG_SKIP_SUPD') else 0):
            if getattr(self, 'smp_stop', 99) <= 7.2 and i >= 1:
                break
            um = ums[i]
            import os as _os
            bank = PS[(4 if _os.environ.get('DBG_BANK') else 0) + i % 4]
            if _os.environ.get("DBG_V") == "B":
                continue
            for h in range(4 if not _os.environ.get('DBG_H1') else 1):
                if _os.environ.get("DBG_OP") == "R":
                    self.mm(bank[128 * h:128 * h + 128], self.identr, Sall[1])
                elif _os.environ.get("DBG_OP") == "P":
                    self.mm(bank[128 * h:128 * h + 128], Sall[0], um[128 * h:128 * h + 128])
                elif _os.environ.get("DBG_OP") == "Q":
                    self.mm(bank[128 * h:128 * h + 128], qkn[512 + 128 * h:512 + 128 * h + 128], Sall[1])
                else:
                    self.mm(bank[128 * h:128 * h + 128], qkn[512 + 128 * h:512 + 128 * h + 128], um[128 * h:128 * h + 128])
            if i + 1 < P:
                mk_um(i + 1)
            if _os.environ.get("DBG_V") == "A":
                continue
            for h in range(4 if not _os.environ.get('DBG_H1') else 1):
                pi = 4 * i + h
                if _os.environ.get("DBG_V") == "C":
                    self.stt(Sall[pi], Sall[pi], 0.5, bank[128 * h:128 * h + 128], ALU.mult, ALU.add, r=True)
                elif _os.environ.get("DBG_V") == "K":
                    self.cp(t1024[0:128], Ref(PS[5][128 * h:128 * h + 128].ap, bank[128 * h:128 * h + 128].ivs))
                elif _os.environ.get("DBG_V") == "N":
                    self.cp(t1024.p(0, 32, slice(0, 128)), bank.p(0, 32, slice(128 * h, 128 * h + 128)))
                elif _os.environ.get("DBG_V") == "O":
                    self.cp(t1024.p(64, 128, slice(0, 128)), bank.p(64, 128, slice(128 * h, 128 * h + 128)))
                elif _os.environ.get("DBG_V") == "H":
                    self.cp(t1024[0:128], bank[128 * h:128 * h + 128])
                elif _os.environ.get("DBG_V") == "I":
                    self.cp(t1024[0:128], bank[128 * h:128 * h + 128], eng="act")
                elif _os.environ.get("DBG_V") == "E":
                    self.tt(t1024[0:128], Sall[pi], bank[128 * h:128 * h + 128], ALU.add)
                elif _os.environ.get("DBG_V") == "F":
                    self.cp(Sall[pi], bank[128 * h:128 * h + 128], r=True)
                elif _os.environ.get("DBG_V") == "D":
                    self.tt(Sall[pi], Sall[pi], bank[128 * h:128 * h + 128], ALU.add, r=True)
                else:
                    self.stt(Sall[pi], Sall[pi], egb[pi:pi + 1], bank[128 * h:128 * h + 128], ALU.mult, ALU.add, r=True)
        if getattr(self, 'smp_stop', 99) <= 7.3:
            return
        dov = d["dsmp"][l].rearrange("s h d e -> d (s h) e")
        for j in range(8 if not _os.environ.get('DBG_SKIP_SUPD') else 0):
            self.dma(_dram_ref(dov[:, 8 * j:8 * j + 8, :]), Sall[8 * j:8 * j + 8])
        if getattr(self, 'smp_stop', 99) <= 8:
            return
        self.mark('smp%d_8' % l)
        self.tt(t2.p(0, P), o.p(0, P), o.p(0, P), ALU.mult)
        self.red(ss4.p(0, P), v3(t2.p(0, P), 4))
        self.act(ss4.p(0, P), ss4.p(0, P), AF.Ln, scale=1.0 / 128.0, bias=RMS_EPS)
        self.act(ss4.p(0, P), ss4.p(0, P), AF.Exp, scale=-0.5)
        self.tt(v3(o.p(0, P), 4), v3(o.p(0, P), 4), b3(ss4), ALU.mult)
        self.tt(v3(o.p(0, P), 4), v3(o.p(0, P), 4), self.bc(self.onorm_bc.p(0, P, slice(128 * l, 128 * l + 128)), [P, 4, 128], 1), ALU.mult)
        self.act(zt.p(0, P), zt.p(0, P), AF_SILU)
        self.tt(o.p(0, P), o.p(0, P), zt.p(0, P), ALU.mult)
        for h in range(4):
            self.tr(PS[5][P * h:P * h + P], o.p(0, P, slice(128 * h, 128 * h + 128)))
        self.cp(mixs[0:4], v3(PS[5][0:4 * P], 4), r=True, eng="act")
        if getattr(self, 'smp_stop', 99) <= 9:
            return
        self.mark('smp%d_9' % l)
        for g in range(4):
            w = POOL_W[g]
            pg = pt.p(0, P, slice(128 * g, 128 * g + 128))
            pv = pst1.p(0, P, slice(0, w - 1))
            self.dma(pv, _dram_ref(d["spool"][l][:, 16 - w:15, 128 * g:128 * g + 128]))
            if w == 2:
                self.tt(sumg.p(0, P), pst1.p(0, P, 0), pg, ALU.add)
            else:
                self.red(sumg.p(0, P), Ref(pv.ap.rearrange("p r c -> p c r"), pv.ivs))
                self.tt(sumg.p(0, P), sumg.p(0, P), pg, ALU.add)
            self.stt(dtok.p(0, P, slice(128 * g, 128 * g + 128)), sumg.p(0, P), 1.0 / w, pg, ALU.mult, ALU.subtract)
        for g in range(4):
            self.tr(PS[5][64 + P * g:64 + P * g + P], dtok.p(0, P, slice(128 * g, 128 * g + 128)))
        self.cp(ddT[:], v3(PS[5][64:64 + 4 * P], 4), r=True, eng="act")
        for g in range(4):
            self.mm(PS[4][P * g:P * g + P], self.poolw[4 * l + g], ddT[g])
            self.act(mixs[4 + g], PS[4][P * g:P * g + P], AF.Identity, scale=self.pscale[l * 4 + g:l * 4 + g + 1], r=True)
        if getattr(self, 'smp_stop', 99) <= 10:
            return
        self.mark('smp%d_10' % l)
        for m in range(DC):
            wb = wblk.next()
            self.dma(wb[:], _dram_ref(woutv[:, :, 128 * m:128 * (m + 1)]))
            ps = PS[m % 4]
            for k in range(DC):
                self.mm(ps[0:P], wb[k], mixs[k], start=(k == 0), stop=(k == DC - 1))
            self.stt(rbs[m], Xs(m), ALPHA, ps[0:P], ALU.mult, ALU.add, r=True)
        AF3 = Alloc(self.arenaF, "sbF", AFb.cur, self.FZ[1])
        tmp = (sqs, AF3(P), AF3(P), AF3(P), Ring([AF3(P) for _ in range(2)]))
        self.layer_norm(rbs, 0, P, SC0, 2, 3, l, LN_EPS, tmp)


_CACHE = {}


def _get_program():
    if "nc" not in _CACHE:
        k = Kern()
        _CACHE["nc"] = k.build()
    return _CACHE["nc"]


def make_in_maps(inputs):
    f = lambda a: np.ascontiguousarray(np.asarray(a, dtype=np.float32))
    consts = make_consts()
    shared = {
        "wg1": f(inputs["ffn1_w_gate"]), "wu1": f(inputs["ffn1_w_up"]), "wd1": f(inputs["ffn1_w_down"]),
        "win": f(inputs["w_in"]), "wout": f(inputs["w_out"]),
        "wg2": f(inputs["ffn2_w_gate"]), "wu2": f(inputs["ffn2_w_up"]), "wd2": f(inputs["ffn2_w_down"]),
        "poolw": f(inputs["pool_w"]),
        "ln1g": f(inputs["ln1_g"]), "ln1b": f(inputs["ln1_b"]), "ln2g": f(inputs["ln2_g"]),
        "ln2b": f(inputs["ln2_b"]), "ln3g": f(inputs["ln3_g"]), "ln3b": f(inputs["ln3_b"]),
        "convw": f(inputs["conv_w"]), "alog": f(inputs["a_log"]), "dtb": f(inputs["dt_bias"]),
        "onorm": f(inputs["onorm_g"]), "pscale": f(inputs["pool_scale"]),
        "constf": consts, "constr": np.ascontiguousarray(np.concatenate([consts[:, C_ID:C_ID + 128], consts[:, C_ONES:C_ONES + 128], consts[:, C_ONESD:C_ONESD + 128]], axis=1)),
    }
    xp = f(inputs["x_prompt"])
    xs = f(inputs["x_sample"])
    sd = f(inputs["state_delta"])
    sc = f(inputs["state_conv"])
    sp = f(inputs["state_pool"])
    maps = []
    for c in range(NCORES):
        m = dict(shared)
        m["xp"] = xp[c]
        m["xs"] = np.ascontiguousarray(xs[c * NS:(c + 1) * NS, 0, :])
        m["sdelta"] = np.ascontiguousarray(sd[:, c * NS:(c + 1) * NS])
        m["sconv"] = np.ascontiguousarray(sc[:, c * NS:(c + 1) * NS])
        m["spool"] = np.ascontiguousarray(sp[:, c * NS:(c + 1) * NS])
        maps.append(m)
    return maps


def kernel(**inputs):
    nc = _get_program()
    maps = make_in_maps(inputs)
    res = run_bass_kernel_spmd(nc, maps, core_ids=list(range(NCORES)))
    R = res.results
    yp = np.stack([R[c]["yp"] for c in range(NCORES)]).astype(np.float32)
    ys = np.concatenate([R[c]["ys"] for c in range(NCORES)], axis=0).reshape(NCORES * NS, 1, D).astype(np.float32)
    dprm = np.stack([R[c]["dprm"] for c in range(NCORES)], axis=1).astype(np.float32)
    cprm = np.stack([R[c]["cprm"] for c in range(NCORES)], axis=1).astype(np.float32)
    pprm = np.stack([R[c]["pprm"] for c in range(NCORES)], axis=1).astype(np.float32)
    dsmp = np.concatenate([R[c]["dsmp"] for c in range(NCORES)], axis=1).astype(np.float32)
    csmp = np.concatenate([R[c]["csmp"] for c in range(NCORES)], axis=1).astype(np.float32)
    psmp = np.concatenate([R[c]["psmp"] for c in range(NCORES)], axis=1).astype(np.float32)
    return (yp, ys, dprm, cprm, pprm, dsmp, csmp, psmp)
```

```python
import bisect
from contextlib import ExitStack
from functools import reduce
import numpy as np
import concourse.bass as bass
import concourse.mybir as mybir
from concourse.bass_utils import run_bass_kernel_spmd

F32 = mybir.dt.float32
F32R = mybir.dt.float32r
AF = mybir.ActivationFunctionType
ALU = mybir.AluOpType
AX = mybir.AxisListType
AF_SILU = AF.Silu

NCORES = 8
D = 1024
DC = 8
DFF = 2816
NF = 22
T = 2048
NS = 16
NTOK = T + NS
DEPTH = 4
QKV = 1536
INDIM = 2568
ALPHA = float((2.0 * DEPTH) ** 0.25)
LN_EPS = 1e-5
RMS_EPS = 1e-6
L2_EPS = 1e-6
POOL_W = (2, 4, 8, 16)
SC0 = 1024


def colp(t):
    return t if t < 1024 else t + NS

ENG_NAMES = ("pe", "act", "dve", "pool", "sp")
SEM_EPOCH = 30000
N_DMA_SEMS = 8


class _Op:
    __slots__ = ("eng", "fn", "deps", "is_dma", "sem", "val", "signaled")

    def __init__(self, eng, fn, deps, is_dma):
        self.eng = eng
        self.fn = fn
        self.deps = deps
        self.is_dma = is_dma
        self.sem = None
        self.val = 0
        self.signaled = is_dma


class _IMap:
    def __init__(self, size, excl_read=False):
        self.b = [0, size]
        self.w = [None]
        self.r = [{}]
        self.excl_read = excl_read

    def _split(self, x):
        i = bisect.bisect_right(self.b, x) - 1
        if self.b[i] == x:
            return i
        self.b.insert(i + 1, x)
        self.w.insert(i + 1, self.w[i])
        self.r.insert(i + 1, dict(self.r[i]))
        return i + 1

    def access(self, lo, hi, op, write, deps):
        i0 = self._split(lo)
        i1 = self._split(hi)
        for i in range(i0, i1):
            w = self.w[i]
            if w is not None:
                deps.append(w)
            if write:
                for v in self.r[i].values():
                    if isinstance(v, list):
                        deps.extend(v)
                    else:
                        deps.append(v)
                self.w[i] = op
                self.r[i] = {}
            else:
                if self.excl_read:
                    for k, v in self.r[i].items():
                        if k != op.eng and not isinstance(v, list):
                            deps.append(v)
                if op.is_dma:
                    self.r[i].setdefault("dma_" + op.eng, []).append(op)
                else:
                    self.r[i][op.eng] = op


class Sched:
    def __init__(self, nc, es, sizes):
        self.nc = nc
        self.es = es
        self.ops = {e: [] for e in ENG_NAMES}
        self.maps = {k: _IMap(v, excl_read=(k == "ps")) for k, v in sizes.items()}
        self.nsem = 0

    def new_sem(self, name):
        self.nsem += 1
        return self.es.enter_context(self.nc.semaphore(name))

    def op(self, eng, fn, reads=(), writes=(), dma=False):
        o = _Op(eng, fn, [], dma)
        deps = o.deps
        for ref in reads:
            for (sp, lo, hi) in ref.ivs:
                if sp == "ps":
                    lo, hi = lo // 512 * 512, (hi + 511) // 512 * 512
                self.maps[sp].access(lo, hi, o, False, deps)
        for ref in writes:
            for (sp, lo, hi) in ref.ivs:
                if sp == "ps":
                    lo, hi = lo // 512 * 512, (hi + 511) // 512 * 512
                self.maps[sp].access(lo, hi, o, True, deps)
        self.ops[eng].append(o)
        return o

    def emit(self):
        nc = self.nc
        for e in ENG_NAMES:
            for o in self.ops[e]:
                for d in o.deps:
                    if d is o:
                        continue
                    if o.eng == "pe" and d.eng == "pe":
                        continue
                    d.signaled = True
        dma_tail = {}
        for e in ENG_NAMES:
            cnt = 0
            sem = None
            dma_sems, dma_cnt, dma_last = [], [], []
            nd = 0
            for o in self.ops[e]:
                if o.is_dma:
                    if len(dma_sems) < N_DMA_SEMS:
                        dma_sems.append(self.new_sem("d_%s_%d" % (e, len(dma_sems))))
                        dma_cnt.append(0)
                        dma_last.append(None)
                    j = nd % N_DMA_SEMS
                    nd += 1
                    if dma_last[j] is not None:
                        o.deps.append(dma_last[j])
                    dma_cnt[j] += 16
                    o.sem = dma_sems[j]
                    o.val = dma_cnt[j]
                    dma_last[j] = o
                elif o.signaled:
                    if sem is None or cnt >= SEM_EPOCH:
                        sem = self.new_sem("e_%s_%d" % (e, self.nsem))
                        cnt = 0
                    cnt += 1
                    o.sem = sem
                    o.val = cnt
            dma_tail[e] = [x for x in dma_last if x is not None]
        sched = self

        def run_engine(e, h):
            waited = {}
            for o in sched.ops[e]:
                need = {}
                for d in o.deps:
                    if d.sem is None or d is o:
                        continue
                    if e == "pe" and d.eng == "pe":
                        continue
                    key = id(d.sem)
                    if waited.get(key, 0) >= d.val:
                        continue
                    if key not in need or need[key][1] < d.val:
                        need[key] = (d.sem, d.val)
                for key, (sem, val) in need.items():
                    h.wait_ge(sem, val)
                    waited[key] = val
                inst = o.fn(h)
                if o.sem is not None:
                    inst.then_inc(o.sem, 16 if o.is_dma else 1)
            for d in dma_tail[e]:
                if waited.get(id(d.sem), 0) < d.val:
                    h.wait_ge(d.sem, d.val)

        with nc.Block() as block:
            @block.tensor
            def _(h):
                run_engine("pe", h)

            @block.scalar
            def _(h):
                run_engine("act", h)

            @block.vector
            def _(h):
                run_engine("dve", h)

            @block.gpsimd
            def _(h):
                run_engine("pool", h)

            @block.sync
            def _(h):
                run_engine("sp", h)


class Ref:
    __slots__ = ("ap", "ivs")

    def __init__(self, ap, ivs):
        self.ap = ap
        self.ivs = ivs

    @property
    def r(self):
        return self.ap.bitcast(F32R)


def _runs(dims, rng):
    if len(dims) == 1:
        return [(rng[0][0], rng[0][1])]
    inner = 1
    for d in dims[1:]:
        inner *= d
    sub = _runs(dims[1:], rng[1:])
    if len(sub) == 1 and sub[0] == (0, inner):
        return [(rng[0][0] * inner, rng[0][1] * inner)]
    out = []
    for i in range(rng[0][0], rng[0][1]):
        for (a, b) in sub:
            out.append((i * inner + a, i * inner + b))
    return out


class Buf:
    def __init__(self, mem, space, off, shape):
        self.space = space
        self.off = off
        self.shape = tuple(shape)
        n = 1
        for s in shape:
            n *= s
        self.n = n
        base = mem[:, off:off + n]
        if len(shape) == 2:
            base = base.rearrange("p (a b) -> p a b", a=shape[0])
        elif len(shape) == 3:
            base = base.rearrange("p (a b c) -> p a b c", a=shape[0], b=shape[1])
        self.base = base

    def _norm(self, idx):
        if not isinstance(idx, tuple):
            idx = (idx,)
        idx = tuple(idx) + (slice(None),) * (len(self.shape) - len(idx))
        rng = []
        for i, s in zip(idx, self.shape):
            if isinstance(i, slice):
                lo = 0 if i.start is None else i.start
                hi = s if i.stop is None else i.stop
            else:
                lo, hi = i, i + 1
            assert 0 <= lo < hi <= s, (idx, self.shape)
            rng.append((lo, hi))
        return idx, rng

    def ref(self, p0, p1, idx):
        idx, rng = self._norm(idx)
        ap = self.base[(slice(p0, p1),) + idx]
        runs = _runs(self.shape, rng)
        if len(runs) > 24:
            runs = [(runs[0][0], runs[-1][1])]
        return Ref(ap, [(self.space, self.off + a, self.off + b) for (a, b) in runs])

    def __getitem__(self, idx):
        return self.ref(0, 128, idx)

    def p(self, p0, p1, *idx):
        return self.ref(p0, p1, tuple(idx) if idx else (slice(None),))


class Alloc:
    def __init__(self, mem, space, lo, hi):
        self.mem, self.space, self.cur, self.hi = mem, space, lo, hi

    def __call__(self, *shape):
        n = 1
        for s in shape:
            n *= s
        n2 = (n + 1) // 2 * 2
        b = Buf(self.mem, self.space, self.cur, shape)
        self.cur += n2
        assert self.cur <= self.hi, ("arena overflow", self.space, self.cur, self.hi)
        return b


class Ring:
    def __init__(self, bufs):
        self.bufs = bufs
        self.i = 0

    def next(self):
        b = self.bufs[self.i % len(self.bufs)]
        self.i += 1
        return b


def _dram_ref(ap):
    return Ref(ap, [])


NCONST = 128 * 6 + 256 + 64
C_ID, C_TRI, C_BLK, C_MSTR, C_ONES, C_ONESD = 0, 128, 256, 384, 512, 640
C_EYE16 = 768
C_RC16 = 1024


def make_consts():
    c = np.zeros((128, NCONST), np.float32)
    idx = np.arange(128)
    same = (idx[:, None] // 64) == (idx[None, :] // 64)
    c[:, C_ID:C_ID + 128] = np.eye(128, dtype=np.float32)
    c[:, C_TRI:C_TRI + 128] = (same & (idx[:, None] <= idx[None, :])).astype(np.float32)
    c[:, C_BLK:C_BLK + 128] = same.astype(np.float32)
    c[:, C_MSTR:C_MSTR + 128] = (same & (idx[:, None] < idx[None, :])).astype(np.float32)
    c[:, C_ONES:C_ONES + 128] = 1.0
    c[:, C_ONESD:C_ONESD + 128] = 1.0 / D
    e16 = np.eye(16, dtype=np.float32).reshape(1, 256)
    c[:, C_EYE16:C_EYE16 + 256] = e16
    rc = np.zeros((4, 16), np.float32)
    for gi, w in enumerate(POOL_W):
        rc[gi] = 1.0 / np.minimum(w, np.arange(16) + 1)
    c[:, C_RC16:C_RC16 + 64] = rc.reshape(1, 64)
    return c


class Kern:
    def __init__(self, nlayers=DEPTH, dbg=None, do_sample=True):
        self.nlayers = nlayers
        self.dbg = dbg
        self.do_sample = do_sample
        nc = bass.Bass("TRN2", target_bir_lowering=False)
        nc.dge_precook = False
        self.nc = nc
        self.es = ExitStack()

    def dram_in(self, name, shape, dt=F32):
        return self.nc.dram_tensor(name, list(shape), dt, kind="ExternalInput").ap()

    def dram_out(self, name, shape):
        return self.nc.dram_tensor(name, list(shape), F32, kind="ExternalOutput").ap()

    def mm(self, out, lhsT, rhs, start=True, stop=True, f32=False):
        o = out.ap
        l = lhsT.ap if f32 else lhsT.r
        r = rhs.ap if f32 else rhs.r
        self.S.op("pe", lambda h: h.matmul(o, l, r, start=start, stop=stop), [lhsT, rhs], [out])

    def tr(self, out, in_):
        o, i, idn = out.ap, in_.ap, self.ident.ap
        np_ = in_.ap.shape[0]
        idn = self.identb.p(0, np_, slice(0, np_)).ap
        self.S.op("pe", lambda h: h.transpose(o, i, idn), [in_, self.ident], [out])

    def act(self, out, in_, func, scale=1.0, bias=0.0, r=False, eng="act"):
        o = out.r if r else out.ap
        i = in_.ap
        reads = [in_]
        kw = {}
        if isinstance(scale, Ref):
            reads.append(scale)
            kw["scale"] = scale.ap
        elif scale != 1.0:
            kw["scale"] = float(scale)
        if isinstance(bias, Ref):
            reads.append(bias)
            kw["bias"] = bias.ap
        elif bias != 0.0:
            kw["bias"] = float(bias)
        self.S.op("act", lambda h: h.activation(o, i, func, **kw), reads, [out])

    def tt(self, out, in0, in1, op, r=False, eng="dve"):
        o = out.r if r else out.ap
        a, b = in0.ap, in1.ap
        self.S.op(eng, lambda h: h.tensor_tensor(o, a, b, op), [in0, in1], [out])

    def ts(self, out, in0, s1, op0, s2=None, op1=None, r=False, eng="dve"):
        o = out.r if r else out.ap
        a = in0.ap
        reads = [in0]
        v1 = s1
        if isinstance(s1, Ref):
            reads.append(s1)
            v1 = s1.ap
        v2 = s2
        if isinstance(s2, Ref):
            reads.append(s2)
            v2 = s2.ap
        if op1 is None:
            self.S.op(eng, lambda h: h.tensor_scalar(o, a, v1, None, op0), reads, [out])
        else:
            self.S.op(eng, lambda h: h.tensor_scalar(o, a, v1, v2, op0, op1), reads, [out])

    def stt(self, out, in0, scalar, in1, op0, op1, r=False):
        o = out.r if r else out.ap
        a, b = in0.ap, in1.ap
        reads = [in0, in1]
        sv = scalar
        if isinstance(scalar, Ref):
            reads.append(scalar)
            sv = scalar.ap
        self.S.op("dve", lambda h: h.scalar_tensor_tensor(o, a, sv, b, op0, op1), reads, [out])

    def cp(self, out, in_, r=False, eng="dve"):
        o = out.r if r else out.ap
        i = in_.ap
        if eng == "act":
            self.S.op("act", lambda h: h.copy(o, i), [in_], [out])
        else:
            self.S.op(eng, lambda h: h.tensor_copy(o, i), [in_], [out])

    def dma(self, out, in_, q="sp"):
        o, i = out.ap, in_.ap
        if i.dtype == F32R and o.dtype != F32R:
            o = o.bitcast(F32R)
        if o.dtype == F32R and i.dtype != F32R:
            i = i.bitcast(F32R)
        self.S.op(q, lambda h: h.dma_start(out=o, in_=i), [in_], [out], dma=True)

    def bc(self, ref, shape, axis):
        ap = ref.ap.unsqueeze(axis).broadcast_to(list(shape))
        return Ref(ap, ref.ivs)

    def sub(self, ref, *idx):
        return Ref(ref.ap[idx], ref.ivs)

    def build(self):
        nc, es = self.nc, self.es
        L = self.nlayers
        di, do = self.dram_in, self.dram_out
        self.d = d = {}
        d["xp"] = di("xp", (T, D))
        d["xs"] = di("xs", (NS, D))
        d["sdelta"] = di("sdelta", (DEPTH, NS, 4, 128, 128), F32R)
        d["sconv"] = di("sconv", (DEPTH, NS, 3, QKV))
        d["spool"] = di("spool", (DEPTH, NS, 15, 512))
        for nm, shp in (("wg1", (DEPTH, D, DFF)), ("wu1", (DEPTH, D, DFF)), ("wd1", (DEPTH, DFF, D)),
                        ("win", (DEPTH, D, INDIM)), ("wout", (DEPTH, D, D)),
                        ("wg2", (DEPTH, D, DFF)), ("wu2", (DEPTH, D, DFF)), ("wd2", (DEPTH, DFF, D)),
                        ("poolw", (DEPTH, 4, 128, 128))):
            d[nm] = di(nm, shp, F32R)
        for nm in ("ln1g", "ln1b", "ln2g", "ln2b", "ln3g", "ln3b"):
            d[nm] = di(nm, (DEPTH, D))
        d["convw"] = di("convw", (DEPTH, 4, QKV))
        d["alog"] = di("alog", (DEPTH, 4))
        d["dtb"] = di("dtb", (DEPTH, 4))
        d["onorm"] = di("onorm", (DEPTH, 128))
        d["pscale"] = di("pscale", (DEPTH, 512))
        d["constf"] = di("constf", (128, NCONST))
        d["constr"] = di("constr", (128, 384), F32R)
        d["yp"] = do("yp", (T, D))
        d["ys"] = do("ys", (NS, D))
        d["dprm"] = do("dprm", (DEPTH, 4, 128, 128))
        d["cprm"] = do("cprm", (DEPTH, 3, QKV))
        d["pprm"] = do("pprm", (DEPTH, 15, 512))
        d["dsmp"] = do("dsmp", (DEPTH, NS, 4, 128, 128))
        d["csmp"] = do("csmp", (DEPTH, NS, 3, QKV))
        d["psmp"] = do("psmp", (DEPTH, NS, 15, 512))
        if self.dbg:
            d["dbg"] = do("dbg", self.dbg)

        AWF, AWR = 10000, 43000
        arenaF = es.enter_context(nc.sbuf_tensor("arenaF", [128, AWF], F32))
        arenaR = es.enter_context(nc.sbuf_tensor("arenaR", [128, AWR], F32))
        psum = es.enter_context(nc.psum_tensor("psum", [128, 4096], F32))
        self.S = S = Sched(nc, es, {"sbF": AWF, "sbR": AWR, "ps": 4096})
        A = Alloc(arenaF, "sbF", 0, AWF)
        AR = Alloc(arenaR, "sbR", 0, AWR)
        self.arenaF, self.arenaR = arenaF, arenaR
        self.PS = [Buf(psum, "ps", 512 * i, (512,)) for i in range(8)]

        self.X = AR(DC, NTOK)
        self.cr = AR(384)
        self.poolw = AR(DEPTH * 4, 128)
        self.Sst = AR(4, 128)
        self.cf = A(NCONST)
        self.lnp = A(192)
        self.convw = A(192)
        self.pscale = A(16)
        self.onorm = A(4)
        self.dtb = A(16)
        self.nea = A(16)
        self.onorm_bc = A(DEPTH * 128)
        self.halo_c = A(12, 4)
        self.halo_p = A(4, 16)
        self.FZ = (A.cur, AWF)
        self.RZ = (AR.cur, AWR)

        cf, cr = self.cf, self.cr
        self.ident = cf[C_ID:C_ID + 128]
        self.identb = Buf(arenaF, "sbF", cf.off + C_ID, (128,))
        self.tri = cf[C_TRI:C_TRI + 128]
        self.blk = cf[C_BLK:C_BLK + 128]
        self.mstr = cf[C_MSTR:C_MSTR + 128]
        self.onesf = cf[C_ONES:C_ONES + 128]
        self.eye16 = cf[C_EYE16:C_EYE16 + 256]
        self.rc16 = cf[C_RC16:C_RC16 + 64]
        self.identr = cr[0:128]
        self.onesr = cr[128:256]
        self.onesdr = cr[256:384]

        self.load_consts()
        self.load_x()
        for l in range(L):
            self.layer(l)
        self.store_y()
        S.emit()
        return nc

    def load_consts(self):
        d = self.d
        self.dma(self.cf[:], _dram_ref(d["constf"]))
        self.dma(self.cr[:], _dram_ref(d["constr"]))
        A = Alloc(self.arenaF, "sbF", *self.FZ)
        stg = [A(128) for _ in range(5)]
        names = ("ln1g", "ln1b", "ln2g", "ln2b", "ln3g", "ln3b")
        for i, nm in enumerate(names):
            self.dma(stg[i // 3].p(32 * (i % 3), 32 * (i % 3) + 32), _dram_ref(d[nm].rearrange("l (c p) -> (l c) p", p=128)))
        cwv = d["convw"].rearrange("l i (c p) -> (l i c) p", p=128)
        self.dma(stg[2].p(0, 96), _dram_ref(cwv[0:96, :]))
        self.dma(stg[3].p(0, 96), _dram_ref(cwv[96:192, :]))
        self.dma(stg[4].p(0, 16), _dram_ref(d["pscale"].rearrange("l (c p) -> (l c) p", p=128)))
        self.dma(stg[4].p(16, 20), _dram_ref(d["onorm"]))
        ps = self.PS[0]
        for i in range(4):
            self.tr(ps[96 * i:96 * i + 96], stg[i].p(0, 96))
        self.tr(self.PS[1][0:20], stg[4].p(0, 20))
        self.cp(self.lnp[:], ps[0:192])
        self.cp(self.convw[:], ps[192:384])
        self.cp(self.pscale[:], self.PS[1][0:16])
        self.cp(self.onorm[:], self.PS[1][16:20])
        self.dma(self.dtb[:], _dram_ref(d["dtb"].rearrange("l h -> (l h)").partition_broadcast(128)))
        self.dma(self.nea[:], _dram_ref(d["alog"].rearrange("l h -> (l h)").partition_broadcast(128)))
        self.dma(self.onorm_bc[:], _dram_ref(d["onorm"].rearrange("l e -> (l e)").partition_broadcast(128)))
        for l in range(DEPTH):
            self.dma(self.poolw[4 * l:4 * l + 4], _dram_ref(d["poolw"][l].rearrange("g c e -> c g e")))
        self.act(self.nea[:], self.nea[:], AF.Exp)
        self.ts(self.nea[:], self.nea[:], -1.0, ALU.mult)

    def load_x(self):
        d = self.d
        A = Alloc(self.arenaF, "sbF", *self.FZ)
        stg = Ring([A(D) for _ in range(3)])
        k = 0
        for tb in range(T // 128 + 1):
            s = stg.next()
            if tb < T // 128:
                npart, col0 = 128, colp(tb * 128)
                self.dma(s.p(0, 128), _dram_ref(d["xp"][tb * 128:(tb + 1) * 128, :]))
            else:
                npart, col0 = NS, SC0
                self.dma(s.p(0, NS), _dram_ref(d["xs"]))
            for half in range(2):
                ps = self.PS[k % 8]
                k += 1
                for c4 in range(4):
                    c = half * 4 + c4
                    self.tr(ps.p(0, 128, slice(c4 * 128, c4 * 128 + npart)), s.p(0, npart, slice(c * 128, (c + 1) * 128)))
                src = Ref(ps.base[:, 0:512].rearrange("p (c t) -> p c t", c=4)[:, :, 0:npart], ps[:].ivs)
                dst = self.X[half * 4:(half + 1) * 4, col0:col0 + npart]
                if (k % 2) == 0:
                    self.cp(dst, src, r=True, eng="act")
                else:
                    self.cp(dst, src, r=True, eng="dve")

    def store_y(self):
        d = self.d
        A = Alloc(self.arenaF, "sbF", *self.FZ)
        stg = Ring([A(D) for _ in range(3)])
        k = 0
        for tb in range(T // 128 + 1):
            s = stg.next()
            if tb < T // 128:
                npart, col0 = 128, colp(tb * 128)
            else:
                npart, col0 = NS, SC0
            for half in range(2):
                ps = self.PS[k % 8]
                k += 1
                for c4 in range(4):
                    c = half * 4 + c4
                    self.tr(ps.p(0, npart, slice(c4 * 128, (c4 + 1) * 128)), self.X[c, col0:col0 + npart])
                dst = s.p(0, npart, slice(half * 512, (half + 1) * 512))
                src = ps.p(0, npart)
                if (k % 2) == 0:
                    self.cp(dst, src, eng="act")
                else:
                    self.cp(dst, src, eng="dve")
            if tb < T // 128:
                self.dma(_dram_ref(d["yp"][tb * 128:(tb + 1) * 128, :]), s.p(0, 128))
            else:
                self.dma(_dram_ref(d["ys"]), s.p(0, NS))

    def layer_norm(self, rb, loc, n, col0, gi, bi, l, eps, tmp, par=0):
        sqr, mean_sb, m2, rstd, tt_ = tmp
        pm, pq = (self.PS[6], self.PS[7]) if par == 0 else (self.PS[4], self.PS[5])
        for c in range(DC):
            sq = sqr.next()
            self.act(sq[0:n], rb[c, loc:loc + n], AF.Square, r=True)
            self.mm(pm[0:n], self.onesdr, rb[c, loc:loc + n], start=(c == 0), stop=(c == DC - 1))
            self.mm(pq[0:n], self.onesdr, sq[0:n], start=(c == 0), stop=(c == DC - 1))
        self.cp(mean_sb[0:n], pm[0:n], eng="act")
        self.tt(m2[0:n], mean_sb[0:n], mean_sb[0:n], ALU.mult)
        self.tt(m2[0:n], pq[0:n], m2[0:n], ALU.subtract)
        self.ts(m2[0:n], m2[0:n], 0.0, ALU.max)
        self.act(rstd[0:n], m2[0:n], AF.Ln, bias=self.eps_ref(eps))
        self.act(rstd[0:n], rstd[0:n], AF.Exp, scale=-0.5)
        for c in range(DC):
            t = tt_.next()
            self.tt(t[0:n], rb[c, loc:loc + n], mean_sb[0:n], ALU.subtract)
            self.tt(t[0:n], t[0:n], rstd[0:n], ALU.mult)
            self.act(self.X[c, col0:col0 + n], t[0:n], AF.Identity,
                     scale=self.lnp[gi * 32 + l * 8 + c:gi * 32 + l * 8 + c + 1], bias=self.lnp[bi * 32 + l * 8 + c:bi * 32 + l * 8 + c + 1], r=True)

    def eps_ref(self, eps):
        return float(eps)

    def ffn(self, l, which, tiles):
        self.mark('ffn%d_%d_%d' % (l, which, len(tiles)))
        d = self.d
        wg, wu, wd = (d["wg1"], d["wu1"], d["wd1"]) if which == 1 else (d["wg2"], d["wu2"], d["wd2"])
        gi, bi = (0, 1) if which == 1 else (4, 5)
        G = sum(n for _, n in tiles)
        locs = []
        o = 0
        for (_, n) in tiles:
            locs.append(o)
            o += n
        groups = [(0, 5), (5, 10), (10, 14), (14, 18), (18, 22)]
        AR = Alloc(self.arenaR, "sbR", *self.RZ)
        AF_ = Alloc(self.arenaF, "sbF", *self.FZ)
        GM = 1040
        rb = AR(DC, GM)
        hid = AR(5, GM)
        wgu = Ring([AR(2, DC, 128) for _ in range(3)])
        wdr = Ring([AR(5, 128) for _ in range(3)])
        sgr = Ring([AF_(512) for _ in range(2)])
        tmp = (Ring([AR(512) for _ in range(2)]), AF_(512), AF_(512), AF_(512), Ring([AF_(512) for _ in range(2)]))
        wgv = wg[l].rearrange("(kc p) f -> p kc f", p=128)
        wuv = wu[l].rearrange("(kc p) f -> p kc f", p=128)
        pgi = 0
        pdi = 0
        for g, (f0, f1) in enumerate(groups):
            nj = f1 - f0
            for j in range(f0, f1):
                w = wgu.next()
                self.dma(Ref(w[0].r, w[0].ivs), _dram_ref(wgv[:, :, j * 128:(j + 1) * 128]))
                self.dma(Ref(w[1].r, w[1].ivs), _dram_ref(wuv[:, :, j * 128:(j + 1) * 128]))
                for ti, (col0, n) in enumerate(tiles):
                    pg = self.PS[pgi % 2]
                    pu = self.PS[2 + pgi % 2]
                    pgi += 1
                    for k in range(DC):
                        self.mm(pg[0:n], w[0, k], self.X[k, col0:col0 + n], start=(k == 0), stop=(k == DC - 1))
                    for k in range(DC):
                        self.mm(pu[0:n], w[1, k], self.X[k, col0:col0 + n], start=(k == 0), stop=(k == DC - 1))
                    sg = sgr.next()
                    self.act(sg[0:n], pg[0:n], AF.Silu)
                    self.tt(hid[j - f0, locs[ti]:locs[ti] + n], sg[0:n], pu[0:n], ALU.mult, r=True)
            for m in range(DC):
                wb = wdr.next()
                src = wd[l][f0 * 128:f1 * 128, m * 128:(m + 1) * 128].rearrange("(j p) c -> p j c", p=128)
                dst = wb[0:nj]
                self.dma(Ref(dst.r, dst.ivs), _dram_ref(src))
                for ti, (col0, n) in enumerate(tiles):
                    pa = self.PS[4 + pdi % 2]
                    pdi += 1
                    for j in range(nj):
                        self.mm(pa[0:n], wb[j], hid[j, locs[ti]:locs[ti] + n], start=(j == 0), stop=(j == nj - 1))
                    rr = rb[m, locs[ti]:locs[ti] + n]
                    if g == 0:
                        self.stt(rr, self.X[m, col0:col0 + n], 2.0 * ALPHA, pa[0:n], ALU.mult, ALU.add, r=True)
                    else:
                        self.tt(rr, rr, pa[0:n], ALU.add, r=True)
        if getattr(self, "skip_ln", False):
            return
        for ti, (col0, n) in enumerate(tiles):
            self.layer_norm(rb, locs[ti], n, col0, gi, bi, l, 4.0 * LN_EPS, tmp, par=ti % 2)

    def layer(self, l):
        ffn_tiles = [[(0, 348), (348, 348), (696, 344)], [(1040, 512), (1552, 512)]]
        if getattr(self, "only_sample", False):
            self.mixer_sample(l)
            return
        if getattr(self, "only_mix", False):
            self.mixer_prompt(l, 0)
            return
        for p in range(2):
            self.ffn(l, 1, ffn_tiles[p])
        if getattr(self, "ffn_only", False):
            return
        for c0 in (0, 512):
            self.mixer_prompt(l, c0)
        if self.do_sample:
            self.mixer_sample(l)
        for c0 in (1024, 1536):
            self.mixer_prompt(l, c0)
        for p in range(2):
            self.ffn(l, 2, ffn_tiles[p])

    def ms(self, out, val, r=False):
        o = out.r if r else out.ap
        self.S.op("dve", lambda h: h.memset(o, val), [], [out])

    def v4(self, ref, a=4):
        return Ref(ref.ap.rearrange("p (a b) -> p a b", a=a), ref.ivs)

    def mark(self, name):
        if not hasattr(self, 'marks'):
            self.marks = []
        self.marks.append((name, len(self.S.ops['pe'])))

    def mixer_prompt(self, l, c0):
        d = self.d
        self.mark('mix%d_%d_p1pool' % (l, c0))
        N = 512
        first = (c0 == 0)
        last = (c0 + N == T)
        PS = self.PS
        winv = d["win"][l].rearrange("(kc p) f -> p kc f", p=128)
        woutv = d["wout"][l].rearrange("(kc p) f -> p kc f", p=128)
        xc = colp(c0)
        Xs = lambda k: self.X[k, xc:xc + N]
        AR = Alloc(self.arenaR, "sbR", *self.RZ)
        qkvc = AR(12, N)
        mix = AR(8, N)
        wblk = Ring([AR(DC, 128) for _ in range(3)])
        wba = AR(DC, 8)
        RP = AR.cur
        AFt = Alloc(self.arenaF, "sbF", self.FZ[1] - 96, self.FZ[1])
        ba = AFt(4, 8)
        beta = AFt(4, 4)
        yv = AFt(4, 4)
        gv = AFt(4, 4)
        self.dma(wba[:], _dram_ref(winv[:, :, 2048:2056]))
        for i in range(4):
            for k in range(DC):
                self.mm(PS[4][8 * i:8 * i + 8], self.X[k, xc + 128 * i:xc + 128 * (i + 1)], wba[k], start=(k == 0), stop=(k == DC - 1))
        self.cp(ba[:], self.v4(PS[4][0:32]), eng="act")
        self.act(beta[:], ba[:, 0:4], AF.Sigmoid)
        dtb_l = self.dtb[l * 4:(l + 1) * 4]
        nea_l = self.nea[l * 4:(l + 1) * 4]
        self.tt(yv[:], ba[:, 4:8], self.bc(dtb_l, [128, 4, 4], 1), ALU.add)
        self.ts(yv[:], yv[:], 30.0, ALU.min)
        self.act(yv[:], yv[:], AF.Exp)
        self.act(yv[:], yv[:], AF.Ln, bias=1.0)
        self.tt(gv[:], yv[:], self.bc(nea_l, [128, 4, 4], 1), ALU.mult)
        AR1 = Alloc(self.arenaR, "sbR", RP, self.RZ[1])
        dd = AR1(4, N)
        pbr = Ring([(AR1(528), AR1(528)) for _ in range(3)])
        dgp = Ring([AR1(2, 128) for _ in range(3)])
        AR1b = Alloc(self.arenaR, "sbR", RP, self.RZ[1])
        sqr = Ring([AR1b(N) for _ in range(2)])
        preR = Ring([(AR1b(516), AR1b(516)) for _ in range(3)])
        dgr = Ring([AR1b(4, 128) for _ in range(3)])
        AF1 = Alloc(self.arenaF, "sbF", *self.FZ)
        acc = Ring([AF1(N) for _ in range(2)])
        rn = Ring([AF1(N) for _ in range(2)])
        z16 = AF1(16)
        t16 = AF1(16)
        pst = AF1(512)
        cst = AF1(QKV)
        if first:
            self.ms(z16[:], 0.0)
        pool_ctx = {}

        def poolA(g):
            w = POOL_W[g]
            wb = wblk.next()
            self.dma(wb[:], _dram_ref(winv[:, :, 2056 + 128 * g:2056 + 128 * (g + 1)]))
            ps = PS[g % 2]
            for k in range(DC):
                self.mm(ps[:], wb[k], Xs(k), start=(k == 0), stop=(k == DC - 1))
            pbA, pbB = pbr.next()
            if first:
                self.cp(pbA[0:16], z16[:], r=True)
                self.cp(pbB[0:15], z16[0:15], r=True)
            else:
                self.cp(pbA[1:16], self.halo_p[g, 1:16], r=True)
                self.cp(pbB[0:15], self.halo_p[g, 1:16], r=True)
            self.cp(pbA[16:528], ps[:], r=True, eng="act")
            self.cp(pbB[15:527], pbA[16:528], r=True)
            self.cp(self.halo_p[g, 1:16], pbA[513:528])
            dg = dgp.next()
            self.ts(dg[0], self.ident, 1.0 / w - 1.0, ALU.mult, r=True)
            self.ts(dg[1], self.ident, 1.0 / w, ALU.mult, r=True)
            pool_ctx[g] = (pbA, pbB, dg)

        def poolB(g):
            w = POOL_W[g]
            pbA, pbB, dg = pool_ctx[g]
            pd = PS[2 + g % 2]
            for i in range(w):
                win = pbA[16 - i:528 - i] if i % 2 == 0 else pbB[15 - i:527 - i]
                self.mm(pd[:], dg[0 if i == 0 else 1], win, start=(i == 0), stop=(i == w - 1))
            self.cp(dd[g], pd[:], r=True, eng="act")
            if first:
                self.tt(t16[:], pd[0:16], pbA[16:32], ALU.add)
                self.tt(t16[:], t16[:], self.sub(self.rc16, slice(None), slice(g * 16, (g + 1) * 16)), ALU.mult)
                self.stt(dd[g, 0:16], t16[:], float(w), pbA[16:32], ALU.mult, ALU.subtract, r=True)
            ps2 = PS[4 + g % 2]
            self.mm(ps2[:], self.poolw[4 * l + g], dd[g])
            self.act(mix[4 + g], ps2[:], AF.Identity, scale=self.pscale[l * 4 + g:l * 4 + g + 1], r=True)
            if last:
                self.tr(PS[6].p(0, 15, slice(g * 128, (g + 1) * 128)), pbA[513:528])

        for st_ in range(5):
            if st_ < 4:
                poolA(st_)
            if st_ >= 1:
                poolB(st_ - 1)
        if last:
            self.cp(pst.p(0, 15), PS[6].p(0, 15), eng="act")
            self.dma(_dram_ref(d["pprm"][l]), pst.p(0, 15))
        self.mark('mix%d_%d_p1qkv' % (l, c0))
        qctx = {}

        def qkvA(oc):
            wb = wblk.next()
            self.dma(wb[:], _dram_ref(winv[:, :, 128 * oc:128 * (oc + 1)]))
            ps = PS[oc % 2]
            for k in range(DC):
                self.mm(ps[:], wb[k], Xs(k), start=(k == 0), stop=(k == DC - 1))
            pr, prB = preR.next()
            if first:
                self.cp(pr[0:4], z16[0:4], r=True)
                self.cp(prB[0:3], z16[0:3], r=True)
            else:
                self.cp(pr[1:4], self.halo_c[oc, 1:4], r=True)
                self.cp(prB[0:3], self.halo_c[oc, 1:4], r=True)
            self.cp(pr[4:516], ps[:], r=True, eng="act")
            self.cp(prB[3:515], pr[4:516], r=True)
            self.cp(self.halo_c[oc, 1:4], pr[513:516])
            dg = dgr.next()
            for i in range(4):
                self.ts(dg[i], self.ident, self.convw[l * 48 + i * 12 + oc:l * 48 + i * 12 + oc + 1], ALU.mult, r=True)
            qctx[oc] = [pr, prB, dg, None]

        def qkvB(oc):
            pr, prB, dg, _ = qctx[oc]
            pc = PS[2 + oc % 2]
            for i in range(4):
                self.mm(pc[:], dg[i], (pr[1 + i:513 + i] if i % 2 else prB[i:512 + i]), start=(i == 0), stop=(i == 3))
            if oc >= 8:
                self.act(qkvc[oc], pc[:], AF.Silu, r=True)
            else:
                a = acc.next()
                qctx[oc][3] = a
                self.act(a[:], pc[:], AF.Silu)
                sq = sqr.next()
                self.act(sq[:], a[:], AF.Square, r=True)
                self.mm(PS[4 + oc % 2][:], self.onesr, sq[:])

        def qkvC(oc):
            pr, prB, dg, a = qctx[oc]
            if oc < 8:
                r_ = rn.next()
                self.act(r_[:], PS[4 + oc % 2][:], AF.Ln, bias=L2_EPS)
                self.act(r_[:], r_[:], AF.Exp, scale=-0.5, bias=(-0.5 * float(np.log(128.0)) if oc < 4 else 0.0))
                self.tt(qkvc[oc], a[:], r_[:], ALU.mult, r=True)
            if last:
                self.tr(PS[7].p(0, 3, slice((oc % 4) * 128, (oc % 4 + 1) * 128)), pr[513:516])
                if oc % 4 == 3:
                    b3 = oc // 4
                    self.cp(cst.p(0, 3, slice(b3 * 512, (b3 + 1) * 512)), PS[7].p(0, 3), eng="act")

        for st_ in range(14):
            if st_ < 12:
                qkvA(st_)
            if 1 <= st_ <= 12:
                qkvB(st_ - 1)
            if st_ >= 2:
                qkvC(st_ - 2)
        if last:
            self.dma(_dram_ref(d["cprm"][l]), cst.p(0, 3))
        if getattr(self, 'mix_stop', 99) <= 2:
            return
        self.mark('mix%d_%d_p2' % (l, c0))
        AR2 = Alloc(self.arenaR, "sbR", RP, self.RZ[1])
        ctxs = []
        for _p in range(2):
            ctxs.append(dict(TI5=AR2(N), QKm=AR2(N), kend=AR2(N), nwk=AR2(N), qdec=AR2(N), X5=AR2(4, 256)))
        Xtmp = AR2(4, 256)
        Tb = AR2(N)
        Pb = AR2(N)
        usb = AR2(N)
        AF2 = Alloc(self.arenaF, "sbF", *self.FZ)
        gcs = AF2(8)
        egc = AF2(4)
        kes = AF2(4)
        nbe = AF2(4)
        grhs = AF2(N)
        brhs = AF2(N)
        diff = AF2(N)
        dinc = AF2(N)
        W1 = AF2(N)
        for _p in range(2):
            ctxs[_p]["egr"] = AF2(N)
        if first:
            self.ms(grhs[:], 0.0)
            self.cp(self.Sst[:], self.v4(grhs[:]), r=True)
        H = lambda buf, h: buf[128 * h:128 * (h + 1)]

        def stageA(i):
            cx = ctxs[i % 2]
            TI5, QKm, kend, nwk, qdec, X5, egr = cx["TI5"], cx["QKm"], cx["kend"], cx["nwk"], cx["qdec"], cx["X5"], cx["egr"]
            tc0 = 128 * i
            qh = lambda h: qkvc[h, tc0:tc0 + 128]
            kh = lambda h: qkvc[4 + h, tc0:tc0 + 128]
            vh = lambda h: qkvc[8 + h, tc0:tc0 + 128]

            def s1():
                self.mm(PS[4][32:36], self.tri, gv[i], f32=True)
                self.mm(PS[4][36:40], self.blk, gv[i], f32=True)
                self.cp(gcs[:], PS[4][32:40], eng="act")
                self.act(egc[:], gcs[0:4], AF.Exp)
                self.tt(kes[:], gcs[4:8], gcs[0:4], ALU.subtract)
                self.act(kes[:], kes[:], AF.Exp)
                self.stt(nbe[:], beta[i], -1.0, egc[:], ALU.mult, ALU.mult)
                self.tt(self.v4(grhs[:]), self.bc(self.tri, [128, 4, 128], 1), self.bc(gv[i], [128, 4, 128], 2), ALU.mult)
                self.tt(self.v4(brhs[:]), self.bc(self.ident, [128, 4, 128], 1), self.bc(beta[i], [128, 4, 128], 2), ALU.mult)
                self.mm(PS[0][:], self.onesf, grhs[:], f32=True)
                self.mm(PS[1][:], self.onesf, brhs[:], f32=True)
                for h in range(4):
                    self.ts(H(diff, h), H(PS[0], h), gcs[h:h + 1], ALU.subtract, 0.0, ALU.min)
                self.act(egr[:], PS[0][:], AF.Exp)
                self.act(diff[:], diff[:], AF.Exp)
                self.tt(self.v4(dinc[:]), self.v4(diff[:]), self.bc(self.tri, [128, 4, 128], 1), ALU.mult)
                self.tt(self.v4(W1[:]), self.v4(diff[:]), self.bc(self.mstr, [128, 4, 128], 1), ALU.mult)
                self.stt(W1[:], W1[:], -1.0, PS[1][:], ALU.mult, ALU.mult)
                self.tt(self.v4(qdec[:]), qkvc[0:4, tc0:tc0 + 128], self.v4(egr[:]), ALU.mult, r=True)

            def s2():
                for h in range(4):
                    self.tr(H(PS[0], h), kh(h))
                for h in range(4):
                    self.tr(H(PS[1], h), vh(h))
                self.tt(Xtmp[:, 0:128], self.v4(PS[1][:]), self.bc(beta[i], [128, 4, 128], 2), ALU.mult, r=True)
                self.tt(Xtmp[:, 128:256], self.v4(PS[0][:]), self.bc(nbe[:], [128, 4, 128], 2), ALU.mult, r=True)
                self.tt(self.v4(kend[:]), self.v4(PS[0][:]), self.bc(kes[:], [128, 4, 128], 2), ALU.mult, r=True)

            def s3():
                for h in range(4):
                    self.mm(H(PS[2], h), kh(h), kh(h))
                for h in range(4):
                    self.mm(H(PS[3], h), kh(h), qh(h))
                self.tt(Tb[:], PS[2][:], W1[:], ALU.mult, r=True)
                self.tt(QKm[:], PS[3][:], dinc[:], ALU.mult, r=True)
                for h in range(4):
                    self.tr(H(PS[0], h), H(Tb, h))
                self.cp(Pb[:], PS[0][:], r=True, eng="act")

            def level(kx):
                def f():
                    Xk, Xn = (Xtmp, X5) if kx % 2 == 0 else (X5, Xtmp)
                    for h in range(4):
                        bank = PS[h // 2]
                        self.mm(bank[256 * (h % 2):256 * (h % 2 + 1)], H(Tb, h), Xk[h])
                    for h in range(4):
                        self.mm(H(PS[2], h), H(Pb, h), H(Tb, h))
                    if kx < 4:
                        for h in range(4):
                            self.mm(H(PS[3], h), H(Tb, h), H(Pb, h))
                    for hp in range(2):
                        self.tt(Xn[2 * hp:2 * hp + 2], Xk[2 * hp:2 * hp + 2], self.v4(PS[hp][:], 2), ALU.add, r=True)
                    if kx < 4:
                        self.cp(Tb[:], PS[2][:], r=True, eng="act")
                        self.cp(Pb[:], PS[3][:], r=True, eng="act")
                    else:
                        self.tt(self.v4(TI5[:]), self.v4(PS[2][:]), self.bc(self.ident, [128, 4, 128], 1), ALU.add, r=True)
                        for h in range(4):
                            self.mm(H(PS[4], h), X5[h, 128:256], H(TI5, h))
                        self.cp(nwk[:], PS[4][:], r=True, eng="act")
                return f

            return [s1, s2, s3] + [level(kx) for kx in range(5)]

        def stageB(i):
            cx = ctxs[i % 2]
            TI5, QKm, kend, nwk, qdec, X5, egr = cx["TI5"], cx["QKm"], cx["kend"], cx["nwk"], cx["qdec"], cx["X5"], cx["egr"]
            tc0 = 128 * i
            steps = []
            for c in range(2):
                p0, p1 = 64 * c, 64 * c + 64

                def a_(c=c, p0=p0, p1=p1):
                    for h in range(4):
                        self.mm(H(PS[7], h), H(TI5, h), X5[h, 0:128], start=True, stop=False)
                        self.mm(H(PS[7], h), H(nwk, h), self.Sst[h], start=False, stop=True)

                def b_(c=c, p0=p0, p1=p1):
                    self.cp(usb.p(p0, p1), PS[7].p(p0, p1), r=True, eng="act")

                def c_(c=c, p0=p0, p1=p1):
                    for h in range(4):
                        oc_ = slice(128 * h + 64 * c, 128 * h + 64 * c + 64)
                        hs = slice(128 * h, 128 * h + 128)
                        self.mm(PS[5][oc_], self.Sst[h], qdec[oc_], start=True, stop=False)
                        self.mm(PS[5][oc_], usb.p(p0, p1, hs), QKm.p(p0, p1, oc_), start=False, stop=True)
                    for h in range(4):
                        hs = slice(128 * h, 128 * h + 128)
                        self.mm(H(PS[6], h), kend.p(p0, p1, hs), usb.p(p0, p1, hs))

                def d_(c=c, p0=p0, p1=p1):
                    for h in range(4):
                        ge = egr[128 * h + 64 * c + 63:128 * h + 64 * c + 64]
                        self.stt(self.Sst[h], self.Sst[h], ge, H(PS[6], h), ALU.mult, ALU.add, r=True)

                steps += [a_, b_, c_, d_]

            def fin():
                self.cp(mix[0:4, tc0:tc0 + 128], self.v4(PS[5][:]), r=True, eng="act")
            return steps, fin

        zbuf = [Buf(self.arenaF, "sbF", self.FZ[0] + 4224 + N * h_, (N,)) for h_ in range(4)]

        def zA1(h):
            def f():
                wb = wblk.next()
                self.dma(wb[:], _dram_ref(winv[:, :, QKV + 128 * h:QKV + 128 * (h + 1)]))
                ps = PS[h % 2]
                for k in range(DC):
                    self.mm(ps[:], wb[k], Xs(k), start=(k == 0), stop=(k == DC - 1))
                self.act(zbuf[h][:], ps[:], AF.Silu)
            return f

        for f in stageA(0):
            f()
        for i in range(4):
            self.mark('mix%d_%d_p2tile%d' % (l, c0, i))
            segsA = stageA(i + 1) if i < 3 else [zA1(h_) for h_ in range(4)]
            stepsB, fin = stageB(i)
            if i < 3:
                A_, B_ = segsA, stepsB
                order = [A_[0], A_[1], B_[0], B_[1], A_[2], A_[3], B_[2], B_[3], A_[4], B_[4], B_[5], A_[5],
                         B_[6], B_[7], A_[6], A_[7]]
            else:
                order = []
                ai = 0
                for bi in range(0, 8, 2):
                    if ai < len(segsA):
                        order.append(segsA[ai])
                        ai += 1
                    order += stepsB[bi:bi + 2]
                order += segsA[ai:]
            for f in order:
                f()
            fin()
        if last:
            for h in range(4):
                self.dma(_dram_ref(d["dprm"][l, h]), self.Sst[h])
        self.mark('mix%d_%d_p3' % (l, c0))
        AR3 = Alloc(self.arenaR, "sbR", RP, self.RZ[1])
        rb = AR3(DC, N)
        sqr3 = Ring([AR3(N) for _ in range(2)])
        AF3 = Alloc(self.arenaF, "sbF", *self.FZ)
        zt = Ring([AF3(N) for _ in range(2)])
        rt = Ring([AF3(N) for _ in range(2)])
        tmp = (sqr3, AF3(N), AF3(N), AF3(N), Ring([AF3(N) for _ in range(2)]))
        zctx = {}

        def zA(h):
            z = zbuf[h]
            sq = sqr3.next()
            self.act(sq[:], mix[h], AF.Square, r=True)
            self.mm(PS[4 + h % 2][:], self.onesr, sq[:])
            zctx[h] = z

        def zB(h):
            z = zctx[h]
            r_ = rt.next()
            self.act(r_[:], PS[4 + h % 2][:], AF.Ln, scale=1.0 / 128.0, bias=RMS_EPS)
            self.act(r_[:], r_[:], AF.Exp, scale=-0.5)
            self.tt(z[:], z[:], r_[:], ALU.mult)
            self.stt(mix[h], mix[h], self.onorm[l:l + 1], z[:], ALU.mult, ALU.mult, r=True)

        for st_ in range(5):
            if st_ < 4:
                zA(st_)
            if st_ >= 1:
                zB(st_ - 1)
        for m in range(DC):
            wb = wblk.next()
            self.dma(wb[:], _dram_ref(woutv[:, :, 128 * m:128 * (m + 1)]))
            ps = PS[m % 4]
            for k in range(DC):
                self.mm(ps[:], wb[k], mix[k], start=(k == 0), stop=(k == DC - 1))
            self.stt(rb[m], Xs(m), ALPHA, ps[:], ALU.mult, ALU.add, r=True)
        self.layer_norm(rb, 0, N, xc, 2, 3, l, LN_EPS, tmp)

    def red(self, out, in_, op=None):
        o, i = out.ap, in_.ap
        op = op or ALU.add
        self.S.op("dve", lambda h: h.tensor_reduce(o, i, AX.X, op), [in_], [out])

    def mixer_sample(self, l):
        d = self.d
        self.mark('sample%d' % l)
        PS = self.PS
        P = NS
        winv = d["win"][l].rearrange("(kc p) f -> p kc f", p=128)
        woutv = d["wout"][l].rearrange("(kc p) f -> p kc f", p=128)
        Xs = lambda k: self.X[k, SC0:SC0 + NS]
        ident16 = Ref(self.ident.ap[0:P, 0:P], self.ident.ivs)
        ones16 = Ref(self.onesf.ap[0:P, 0:128], self.onesf.ivs)
        v3 = lambda ref, a: Ref(ref.ap.rearrange("p (a b) -> p a b", a=a), ref.ivs)
        AR = Alloc(self.arenaR, "sbR", *self.RZ)
        wblk = Ring([AR(DC, 128) for _ in range(3)])
        wba = AR(DC, 8)
        Sall = AR(4 * NS, 128)
        kqT = AR(8, P)
        km = AR(4, P, P)
        qm = AR(4, P, P)
        qkn = AR(1024)
        umr = Ring([AR(512) for _ in range(2)])
        rhsr = AR(64)
        mixs = AR(8, P)
        ddT = AR(4, P)
        rbs = AR(DC, P)
        sqs = Ring([AR(P) for _ in range(2)])
        wblk = Ring(wblk.bufs + [AR(DC, 128) for _ in range(6)])
        AFa = Alloc(self.arenaF, "sbF", *self.FZ)
        qkv = AFa(QKV)
        zt = AFa(512)
        pt = AFa(512)
        bat = AFa(8)
        F2 = AFa.cur
        cst = AFa(3, 512)
        cw = AFa(4, 512)
        sdv = d["sdelta"][l].rearrange("s h d e -> d (s h) e")
        for j in range(8):
            self.dma(Sall[8 * j:8 * j + 8], _dram_ref(sdv[:, 8 * j:8 * j + 8, :]))
        cols = [128 * b for b in range(16)] + [2056 + 128 * g for g in range(4)]
        dsts = [qkv.p(0, P, slice(0, 512)), qkv.p(0, P, slice(512, 1024)), qkv.p(0, P, slice(1024, 1536)), zt.p(0, P), pt.p(0, P)]
        for grp in range(5):
            bank = PS[grp % 4]
            for b4 in range(4):
                wb = wblk.next()
                c = cols[4 * grp + b4]
                self.dma(wb[:], _dram_ref(winv[:, :, c:c + 128]))
                for k in range(DC):
                    self.mm(bank.p(0, P, slice(128 * b4, 128 * b4 + 128)), Xs(k), wb[k], start=(k == 0), stop=(k == DC - 1))
            self.cp(dsts[grp], bank.p(0, P), eng="act")
        self.dma(wba[:], _dram_ref(winv[:, :, 2048:2056]))
        for k in range(DC):
            self.mm(PS[4].p(0, P, slice(0, 8)), Xs(k), wba[k], start=(k == 0), stop=(k == DC - 1))
        self.cp(bat.p(0, P), PS[4].p(0, P, slice(0, 8)), eng="act")
        if getattr(self, 'smp_stop', 99) <= 1:
            return
        self.mark('smp%d_1' % l)
        self.dma(_dram_ref(d["csmp"][l, :, 0:2, :]), _dram_ref(d["sconv"][l, :, 1:3, :]))
        self.dma(_dram_ref(d["csmp"][l, :, 2, :]), qkv.p(0, P))
        self.dma(_dram_ref(d["psmp"][l, :, 0:14, :]), _dram_ref(d["spool"][l, :, 1:15, :]))
        self.dma(_dram_ref(d["psmp"][l, :, 14, :]), pt.p(0, P))
        if getattr(self, 'smp_stop', 99) <= 2:
            return
        self.mark('smp%d_2' % l)
        for j in range(3):
            self.dma(cst.p(0, P), _dram_ref(d["sconv"][l][:, :, 512 * j:512 * (j + 1)]))
            for i in range(4):
                self.dma(cw.p(0, P, i), _dram_ref(d["convw"][l, i, 512 * j:512 * (j + 1)].partition_broadcast(P)))
            pre = qkv.p(0, P, slice(512 * j, 512 * (j + 1)))
            tA = Buf(self.arenaF, "sbF", AFa.cur, (512,)).p(0, P)
            tB = Buf(self.arenaF, "sbF", AFa.cur + 512, (512,)).p(0, P)
            self.tt(tA, pre, cw.p(0, P, 3), ALU.mult)
            for i in range(3):
                self.tt(tB, cst.p(0, P, i), cw.p(0, P, i), ALU.mult)
                self.tt(tA, tA, tB, ALU.add)
            self.act(pre, tA, AF_SILU)
        if getattr(self, 'smp_stop', 99) <= 3:
            return
        self.mark('smp%d_3' % l)
        AFb = Alloc(self.arenaF, "sbF", F2, self.FZ[1])
        t1024 = AFb(1024)
        ss8 = AFb(8)
        rn8 = AFb(8)
        beta = AFb(4)
        yv = AFb(4)
        eg = AFb(4)
        nbe = AFb(4)
        qkd = AFb(4)
        rhse = AFb(P, 4)
        egb = AFb(64)
        u = AFb(512)
        t2 = AFb(512)
        o = AFb(512)
        ss4 = AFb(4)
        dtok = AFb(512)
        sumg = AFb(128)
        pst1 = Buf(self.arenaF, "sbF", F2, (15, 128))
        self.ms(t1024[:], 0.0)
        self.cp(qkn[:], t1024[:], r=True)
        self.cp(rhsr[:], t1024[0:64], r=True)
        for _ in range(2):
            umz = umr.next()
            self.cp(umz[:], t1024[0:512], r=True)
        qk = qkv.p(0, P, slice(0, 1024))
        self.tt(t1024.p(0, P), qk, qk, ALU.mult)
        self.red(ss8.p(0, P), v3(t1024.p(0, P), 8))
        self.act(rn8.p(0, P), ss8.p(0, P), AF.Ln, bias=L2_EPS)
        self.act(rn8.p(0, P), rn8.p(0, P), AF.Exp, scale=-0.5)
        self.ts(rn8.p(0, P, slice(0, 4)), rn8.p(0, P, slice(0, 4)), float(128.0 ** -0.5), ALU.mult)
        self.tt(v3(qkn.p(0, P), 8), v3(qk, 8), self.bc(rn8.p(0, P), [P, 8, 128], 2), ALU.mult, r=True)
        qn = qkn.p(0, P, slice(0, 512))
        kn = qkn.p(0, P, slice(512, 1024))
        vt = qkv.p(0, P, slice(1024, 1536))
        if getattr(self, 'smp_stop', 99) <= 4:
            return
        self.mark('smp%d_4' % l)
        self.act(beta.p(0, P), bat.p(0, P, slice(0, 4)), AF.Sigmoid)
        self.tt(yv.p(0, P), bat.p(0, P, slice(4, 8)), self.dtb.p(0, P, slice(4 * l, 4 * l + 4)), ALU.add)
        self.ts(yv.p(0, P), yv.p(0, P), 30.0, ALU.min)
        self.act(yv.p(0, P), yv.p(0, P), AF.Exp)
        self.act(yv.p(0, P), yv.p(0, P), AF.Ln, bias=1.0)
        self.tt(yv.p(0, P), yv.p(0, P), self.nea.p(0, P, slice(4 * l, 4 * l + 4)), ALU.mult)
        self.act(eg.p(0, P), yv.p(0, P), AF.Exp)
        self.stt(nbe.p(0, P), beta.p(0, P), -1.0, eg.p(0, P), ALU.mult, ALU.mult)
        if getattr(self, 'smp_stop', 99) <= 5:
            return
        self.mark('smp%d_5' % l)
        for j in range(8):
            self.tr(PS[5][P * j:P * j + P], qkn.p(0, P, slice(128 * j, 128 * j + 128)))
        self.cp(kqT[:], v3(PS[5][0:8 * P], 8), r=True, eng="act")
        eye = v3(self.eye16, P)
        self.tt(km[:], self.bc(kqT[4:8], [128, 4, P, P], 2), self.bc(eye, [128, 4, P, P], 1), ALU.mult, r=True)
        self.tt(qm[:], self.bc(kqT[0:4], [128, 4, P, P], 2), self.bc(eye, [128, 4, P, P], 1), ALU.mult, r=True)
        for (msk, bank) in ((km, PS[6]), (qm, PS[7])):
            for h in range(4):
                for i in range(P):
                    self.mm(bank.p(0, P, slice(128 * h, 128 * h + 128)), msk[h, i], Sall[4 * i + h], start=(i == 0), stop=(i == P - 1))
        if getattr(self, 'smp_stop', 99) <= 6:
            return
        self.mark('smp%d_6' % l)
        b3 = lambda x: self.bc(x.p(0, P), [P, 4, 128], 2)
        self.tt(v3(u.p(0, P), 4), v3(vt, 4), b3(beta), ALU.mult)
        self.tt(v3(t2.p(0, P), 4), v3(PS[6].p(0, P), 4), b3(nbe), ALU.mult)
        self.tt(u.p(0, P), u.p(0, P), t2.p(0, P), ALU.add)
        self.tt(t2.p(0, P), qn, kn, ALU.mult)
        self.red(qkd.p(0, P), v3(t2.p(0, P), 4))
        self.tt(v3(o.p(0, P), 4), v3(PS[7].p(0, P), 4), b3(eg), ALU.mult)
        self.tt(v3(t2.p(0, P), 4), v3(u.p(0, P), 4), b3(qkd), ALU.mult)
        self.tt(o.p(0, P), o.p(0, P), t2.p(0, P), ALU.add)
        if getattr(self, 'smp_stop', 99) <= 7:
            return
        self.mark('smp%d_7' % l)
        self.tt(Ref(rhsr.p(0, P).ap.rearrange("p (a b) -> p a b", a=P), rhsr[:].ivs), self.bc(eg.p(0, P), [P, P, 4], 1), self.bc(ident16, [P, P, 4], 2), ALU.mult, r=True)
        self.mm(PS[4][64:128], self.onesr, rhsr[:])
        self.cp(egb[:], PS[4][64:128], eng="act")
        if getattr(self, 'smp_stop', 99) <= 7.1:
            return
        import os as _os
        ums = {}

        def mk_um(i):
            ums[i] = umr.next()
            self.ts(ums[i].p(0, P), u.p(0, P), Ref(self.ident.ap[0:P, i:i + 1], self.ident.ivs), ALU.mult, r=True)

        mk_um(0)
        for i in range(P if not _os.environ.get('DBG_SKIP_SUPD') else 0):
            if getattr(self, 'smp_stop', 99) <= 7.2 and i >= 1:
                break
            um = ums[i]
            import os as _os
            bank = PS[(4 if _os.environ.get('DBG_BANK') else 0) + i % 4]
            if _os.environ.get("DBG_V") == "B":
                continue
            for h in range(4 if not _os.environ.get('DBG_H1') else 1):
                if _os.environ.get("DBG_OP") == "R":
                    self.mm(bank[128 * h:128 * h + 128], self.identr, Sall[1])
                elif _os.environ.get("DBG_OP") == "P":
                    self.mm(bank[128 * h:128 * h + 128], Sall[0], um[128 * h:128 * h + 128])
                elif _os.environ.get("DBG_OP") == "Q":
                    self.mm(bank[128 * h:128 * h + 128], qkn[512 + 128 * h:512 + 128 * h + 128], Sall[1])
                else:
                    self.mm(bank[128 * h:128 * h + 128], qkn[512 + 128 * h:512 + 128 * h + 128], um[128 * h:128 * h + 128])
            if i + 1 < P:
                mk_um(i + 1)
            if _os.environ.get("DBG_V") == "A":
                continue
            for h in range(4 if not _os.environ.get('DBG_H1') else 1):
                pi = 4 * i + h
                if _os.environ.get("DBG_V") == "C":
                    self.stt(Sall[pi], Sall[pi], 0.5, bank[128 * h:128 * h + 128], ALU.mult, ALU.add, r=True)
                elif _os.environ.get("DBG_V") == "K":
                    self.cp(t1024[0:128], Ref(PS[5][128 * h:128 * h + 128].ap, bank[128 * h:128 * h + 128].ivs))
                elif _os.environ.get("DBG_V") == "N":
                    self.cp(t1024.p(0, 32, slice(0, 128)), bank.p(0, 32, slice(128 * h, 128 * h + 128)))
                elif _os.environ.get("DBG_V") == "O":
                    self.cp(t1024.p(64, 128, slice(0, 128)), bank.p(64, 128, slice(128 * h, 128 * h + 128)))
                elif _os.environ.get("DBG_V") == "H":
                    self.cp(t1024[0:128], bank[128 * h:128 * h + 128])
                elif _os.environ.get("DBG_V") == "I":
                    self.cp(t1024[0:128], bank[128 * h:128 * h + 128], eng="act")
                elif _os.environ.get("DBG_V") == "E":
                    self.tt(t1024[0:128], Sall[pi], bank[128 * h:128 * h + 128], ALU.add)
                elif _os.environ.get("DBG_V") == "F":
                    self.cp(Sall[pi], bank[128 * h:128 * h + 128], r=True)
                elif _os.environ.get("DBG_V") == "D":
                    self.tt(Sall[pi], Sall[pi], bank[128 * h:128 * h + 128], ALU.add, r=True)
                else:
                    self.stt(Sall[pi], Sall[pi], egb[pi:pi + 1], bank[128 * h:128 * h + 128], ALU.mult, ALU.add, r=True)
        if getattr(self, 'smp_stop', 99) <= 7.3:
            return
        dov = d["dsmp"][l].rearrange("s h d e -> d (s h) e")
        for j in range(8 if not _os.environ.get('DBG_SKIP_SUPD') else 0):
            self.dma(_dram_ref(dov[:, 8 * j:8 * j + 8, :]), Sall[8 * j:8 * j + 8])
        if getattr(self, 'smp_stop', 99) <= 8:
            return
        self.mark('smp%d_8' % l)
        self.tt(t2.p(0, P), o.p(0, P), o.p(0, P), ALU.mult)
        self.red(ss4.p(0, P), v3(t2.p(0, P), 4))
        self.act(ss4.p(0, P), ss4.p(0, P), AF.Ln, scale=1.0 / 128.0, bias=RMS_EPS)
        self.act(ss4.p(0, P), ss4.p(0, P), AF.Exp, scale=-0.5)
        self.tt(v3(o.p(0, P), 4), v3(o.p(0, P), 4), b3(ss4), ALU.mult)
        self.tt(v3(o.p(0, P), 4), v3(o.p(0, P), 4), self.bc(self.onorm_bc.p(0, P, slice(128 * l, 128 * l + 128)), [P, 4, 128], 1), ALU.mult)
        self.act(zt.p(0, P), zt.p(0, P), AF_SILU)
        self.tt(o.p(0, P), o.p(0, P), zt.p(0, P), ALU.mult)
        for h in range(4):
            self.tr(PS[5][P * h:P * h + P], o.p(0, P, slice(128 * h, 128 * h + 128)))
        self.cp(mixs[0:4], v3(PS[5][0:4 * P], 4), r=True, eng="act")
        if getattr(self, 'smp_stop', 99) <= 9:
            return
        self.mark('smp%d_9' % l)
        for g in range(4):
            w = POOL_W[g]
            pg = pt.p(0, P, slice(128 * g, 128 * g + 128))
            pv = pst1.p(0, P, slice(0, w - 1))
            self.dma(pv, _dram_ref(d["spool"][l][:, 16 - w:15, 128 * g:128 * g + 128]))
            if w == 2:
                self.tt(sumg.p(0, P), pst1.p(0, P, 0), pg, ALU.add)
            else:
                self.red(sumg.p(0, P), Ref(pv.ap.rearrange("p r c -> p c r"), pv.ivs))
                self.tt(sumg.p(0, P), sumg.p(0, P), pg, ALU.add)
            self.stt(dtok.p(0, P, slice(128 * g, 128 * g + 128)), sumg.p(0, P), 1.0 / w, pg, ALU.mult, ALU.subtract)
        for g in range(4):
            self.tr(PS[5][64 + P * g:64 + P * g + P], dtok.p(0, P, slice(128 * g, 128 * g + 128)))
        self.cp(ddT[:], v3(PS[5][64:64 + 4 * P], 4), r=True, eng="act")
        for g in range(4):
            self.mm(PS[4][P * g:P * g + P], self.poolw[4 * l + g], ddT[g])
            self.act(mixs[4 + g], PS[4][P * g:P * g + P], AF.Identity, scale=self.pscale[l * 4 + g:l * 4 + g + 1], r=True)
        if getattr(self, 'smp_stop', 99) <= 10:
            return
        self.mark('smp%d_10' % l)
        for m in range(DC):
            wb = wblk.next()
            self.dma(wb[:], _dram_ref(woutv[:, :, 128 * m:128 * (m + 1)]))
            ps = PS[m % 4]
            for k in range(DC):
                self.mm(ps[0:P], wb[k], mixs[k], start=(k == 0), stop=(k == DC - 1))
            self.stt(rbs[m], Xs(m), ALPHA, ps[0:P], ALU.mult, ALU.add, r=True)
        AF3 = Alloc(self.arenaF, "sbF", AFb.cur, self.FZ[1])
        tmp = (sqs, AF3(P), AF3(P), AF3(P), Ring([AF3(P) for _ in range(2)]))
        self.layer_norm(rbs, 0, P, SC0, 2, 3, l, LN_EPS, tmp)


_CACHE = {}


def _get_program():
    if "nc" not in _CACHE:
        k = Kern()
        _CACHE["nc"] = k.build()
    return _CACHE["nc"]


def make_in_maps(inputs):
    f = lambda a: np.ascontiguousarray(np.asarray(a, dtype=np.float32))
    consts = make_consts()
    shared = {
        "wg1": f(inputs["ffn1_w_gate"]), "wu1": f(inputs["ffn1_w_up"]), "wd1": f(inputs["ffn1_w_down"]),
        "win": f(inputs["w_in"]), "wout": f(inputs["w_out"]),
        "wg2": f(inputs["ffn2_w_gate"]), "wu2": f(inputs["ffn2_w_up"]), "wd2": f(inputs["ffn2_w_down"]),
        "poolw": f(inputs["pool_w"]),
        "ln1g": f(inputs["ln1_g"]), "ln1b": f(inputs["ln1_b"]), "ln2g": f(inputs["ln2_g"]),
        "ln2b": f(inputs["ln2_b"]), "ln3g": f(inputs["ln3_g"]), "ln3b": f(inputs["ln3_b"]),
        "convw": f(inputs["conv_w"]), "alog": f(inputs["a_log"]), "dtb": f(inputs["dt_bias"]),
        "onorm": f(inputs["onorm_g"]), "pscale": f(inputs["pool_scale"]),
        "constf": consts, "constr": np.ascontiguousarray(np.concatenate([consts[:, C_ID:C_ID + 128], consts[:, C_ONES:C_ONES + 128], consts[:, C_ONESD:C_ONESD + 128]], axis=1)),
    }
    xp = f(inputs["x_prompt"])
    xs = f(inputs["x_sample"])
    sd = f(inputs["state_delta"])
    sc = f(inputs["state_conv"])
    sp = f(inputs["state_pool"])
    maps = []
    for c in range(NCORES):
        m = dict(shared)
        m["xp"] = xp[c]
        m["xs"] = np.ascontiguousarray(xs[c * NS:(c + 1) * NS, 0, :])
        m["sdelta"] = np.ascontiguousarray(sd[:, c * NS:(c + 1) * NS])
        m["sconv"] = np.ascontiguousarray(sc[:, c * NS:(c + 1) * NS])
        m["spool"] = np.ascontiguousarray(sp[:, c * NS:(c + 1) * NS])
        maps.append(m)
    return maps


def kernel(**inputs):
    nc = _get_program()
    maps = make_in_maps(inputs)
    res = run_bass_kernel_spmd(nc, maps, core_ids=list(range(NCORES)))
    R = res.results
    yp = np.stack([R[c]["yp"] for c in range(NCORES)]).astype(np.float32)
    ys = np.concatenate([R[c]["ys"] for c in range(NCORES)], axis=0).reshape(NCORES * NS, 1, D).astype(np.float32)
    dprm = np.stack([R[c]["dprm"] for c in range(NCORES)], axis=1).astype(np.float32)
    cprm = np.stack([R[c]["cprm"] for c in range(NCORES)], axis=1).astype(np.float32)
    pprm = np.stack([R[c]["pprm"] for c in range(NCORES)], axis=1).astype(np.float32)
    dsmp = np.concatenate([R[c]["dsmp"] for c in range(NCORES)], axis=1).astype(np.float32)
    csmp = np.concatenate([R[c]["csmp"] for c in range(NCORES)], axis=1).astype(np.float32)
    psmp = np.concatenate([R[c]["psmp"] for c in range(NCORES)], axis=1).astype(np.float32)
    return (yp, ys, dprm, cprm, pprm, dsmp, csmp, psmp)
```

```python
import bisect
from contextlib import ExitStack
from functools import reduce
import numpy as np
import concourse.bass as bass
import concourse.mybir as mybir
from concourse.bass_utils import run_bass_kernel_spmd

F32 = mybir.dt.float32
F32R = mybir.dt.float32r
AF = mybir.ActivationFunctionType
ALU = mybir.AluOpType
AX = mybir.AxisListType
AF_SILU = AF.Silu

NCORES = 8
D = 1024
DC = 8
DFF = 2816
NF = 22
T = 2048
NS = 16
NTOK = T + NS
DEPTH = 4
QKV = 1536
INDIM = 2568
ALPHA = float((2.0 * DEPTH) ** 0.25)
LN_EPS = 1e-5
RMS_EPS = 1e-6
L2_EPS = 1e-6
POOL_W = (2, 4, 8, 16)
SC0 = 1024


def colp(t):
    return t if t < 1024 else t + NS

ENG_NAMES = ("pe", "act", "dve", "pool", "sp")
SEM_EPOCH = 30000
N_DMA_SEMS = 8


class _Op:
    __slots__ = ("eng", "fn", "deps", "is_dma", "sem", "val", "signaled")

    def __init__(self, eng, fn, deps, is_dma):
        self.eng = eng
        self.fn = fn
        self.deps = deps
        self.is_dma = is_dma
        self.sem = None
        self.val = 0
        self.signaled = is_dma


class _IMap:
    def __init__(self, size, excl_read=False):
        self.b = [0, size]
        self.w = [None]
        self.r = [{}]
        self.excl_read = excl_read

    def _split(self, x):
        i = bisect.bisect_right(self.b, x) - 1
        if self.b[i] == x:
            return i
        self.b.insert(i + 1, x)
        self.w.insert(i + 1, self.w[i])
        self.r.insert(i + 1, dict(self.r[i]))
        return i + 1

    def access(self, lo, hi, op, write, deps):
        i0 = self._split(lo)
        i1 = self._split(hi)
        for i in range(i0, i1):
            w = self.w[i]
            if w is not None:
                deps.append(w)
            if write:
                for v in self.r[i].values():
                    if isinstance(v, list):
                        deps.extend(v)
                    else:
                        deps.append(v)
                self.w[i] = op
                self.r[i] = {}
            else:
                if self.excl_read:
                    for k, v in self.r[i].items():
                        if k != op.eng and not isinstance(v, list):
                            deps.append(v)
                if op.is_dma:
                    self.r[i].setdefault("dma_" + op.eng, []).append(op)
                else:
                    self.r[i][op.eng] = op


class Sched:
    def __init__(self, nc, es, sizes):
        self.nc = nc
        self.es = es
        self.ops = {e: [] for e in ENG_NAMES}
        self.maps = {k: _IMap(v, excl_read=(k == "ps")) for k, v in sizes.items()}
        self.nsem = 0

    def new_sem(self, name):
        self.nsem += 1
        return self.es.enter_context(self.nc.semaphore(name))

    def op(self, eng, fn, reads=(), writes=(), dma=False):
        o = _Op(eng, fn, [], dma)
        deps = o.deps
        for ref in reads:
            for (sp, lo, hi) in ref.ivs:
                if sp == "ps":
                    lo, hi = lo // 512 * 512, (hi + 511) // 512 * 512
                self.maps[sp].access(lo, hi, o, False, deps)
        for ref in writes:
            for (sp, lo, hi) in ref.ivs:
                if sp == "ps":
                    lo, hi = lo // 512 * 512, (hi + 511) // 512 * 512
                self.maps[sp].access(lo, hi, o, True, deps)
        self.ops[eng].append(o)
        return o

    def emit(self):
        nc = self.nc
        for e in ENG_NAMES:
            for o in self.ops[e]:
                for d in o.deps:
                    if d is o:
                        continue
                    if o.eng == "pe" and d.eng == "pe":
                        continue
                    d.signaled = True
        dma_tail = {}
        for e in ENG_NAMES:
            cnt = 0
            sem = None
            dma_sems, dma_cnt, dma_last = [], [], []
            nd = 0
            for o in self.ops[e]:
                if o.is_dma:
                    if len(dma_sems) < N_DMA_SEMS:
                        dma_sems.append(self.new_sem("d_%s_%d" % (e, len(dma_sems))))
                        dma_cnt.append(0)
                        dma_last.append(None)
                    j = nd % N_DMA_SEMS
                    nd += 1
                    if dma_last[j] is not None:
                        o.deps.append(dma_last[j])
                    dma_cnt[j] += 16
                    o.sem = dma_sems[j]
                    o.val = dma_cnt[j]
                    dma_last[j] = o
                elif o.signaled:
                    if sem is None or cnt >= SEM_EPOCH:
                        sem = self.new_sem("e_%s_%d" % (e, self.nsem))
                        cnt = 0
                    cnt += 1
                    o.sem = sem
                    o.val = cnt
            dma_tail[e] = [x for x in dma_last if x is not None]
        sched = self

        def run_engine(e, h):
            waited = {}
            for o in sched.ops[e]:
                need = {}
                for d in o.deps:
                    if d.sem is None or d is o:
                        continue
                    if e == "pe" and d.eng == "pe":
                        continue
                    key = id(d.sem)
                    if waited.get(key, 0) >= d.val:
                        continue
                    if key not in need or need[key][1] < d.val:
                        need[key] = (d.sem, d.val)
                for key, (sem, val) in need.items():
                    h.wait_ge(sem, val)
                    waited[key] = val
                inst = o.fn(h)
                if o.sem is not None:
                    inst.then_inc(o.sem, 16 if o.is_dma else 1)
            for d in dma_tail[e]:
                if waited.get(id(d.sem), 0) < d.val:
                    h.wait_ge(d.sem, d.val)

        with nc.Block() as block:
            @block.tensor
            def _(h):
                run_engine("pe", h)

            @block.scalar
            def _(h):
                run_engine("act", h)

            @block.vector
            def _(h):
                run_engine("dve", h)

            @block.gpsimd
            def _(h):
                run_engine("pool", h)

            @block.sync
            def _(h):
                run_engine("sp", h)


class Ref:
    __slots__ = ("ap", "ivs")

    def __init__(self, ap, ivs):
        self.ap = ap
        self.ivs = ivs

    @property
    def r(self):
        return self.ap.bitcast(F32R)


def _runs(dims, rng):
    if len(dims) == 1:
        return [(rng[0][0], rng[0][1])]
    inner = 1
    for d in dims[1:]:
        inner *= d
    sub = _runs(dims[1:], rng[1:])
    if len(sub) == 1 and sub[0] == (0, inner):
        return [(rng[0][0] * inner, rng[0][1] * inner)]
    out = []
    for i in range(rng[0][0], rng[0][1]):
        for (a, b) in sub:
            out.append((i * inner + a, i * inner + b))
    return out


class Buf:
    def __init__(self, mem, space, off, shape):
        self.space = space
        self.off = off
        self.shape = tuple(shape)
        n = 1
        for s in shape:
            n *= s
        self.n = n
        base = mem[:, off:off + n]
        if len(shape) == 2:
            base = base.rearrange("p (a b) -> p a b", a=shape[0])
        elif len(shape) == 3:
            base = base.rearrange("p (a b c) -> p a b c", a=shape[0], b=shape[1])
        self.base = base

    def _norm(self, idx):
        if not isinstance(idx, tuple):
            idx = (idx,)
        idx = tuple(idx) + (slice(None),) * (len(self.shape) - len(idx))
        rng = []
        for i, s in zip(idx, self.shape):
            if isinstance(i, slice):
                lo = 0 if i.start is None else i.start
                hi = s if i.stop is None else i.stop
            else:
                lo, hi = i, i + 1
            assert 0 <= lo < hi <= s, (idx, self.shape)
            rng.append((lo, hi))
        return idx, rng

    def ref(self, p0, p1, idx):
        idx, rng = self._norm(idx)
        ap = self.base[(slice(p0, p1),) + idx]
        runs = _runs(self.shape, rng)
        if len(runs) > 24:
            runs = [(runs[0][0], runs[-1][1])]
        return Ref(ap, [(self.space, self.off + a, self.off + b) for (a, b) in runs])

    def __getitem__(self, idx):
        return self.ref(0, 128, idx)

    def p(self, p0, p1, *idx):
        return self.ref(p0, p1, tuple(idx) if idx else (slice(None),))


class Alloc:
    def __init__(self, mem, space, lo, hi):
        self.mem, self.space, self.cur, self.hi = mem, space, lo, hi

    def __call__(self, *shape):
        n = 1
        for s in shape:
            n *= s
        n2 = (n + 1) // 2 * 2
        b = Buf(self.mem, self.space, self.cur, shape)
        self.cur += n2
        assert self.cur <= self.hi, ("arena overflow", self.space, self.cur, self.hi)
        return b


class Ring:
    def __init__(self, bufs):
        self.bufs = bufs
        self.i = 0

    def next(self):
        b = self.bufs[self.i % len(self.bufs)]
        self.i += 1
        return b


def _dram_ref(ap):
    return Ref(ap, [])


NCONST = 128 * 6 + 256 + 64
C_ID, C_TRI, C_BLK, C_MSTR, C_ONES, C_ONESD = 0, 128, 256, 384, 512, 640
C_EYE16 = 768
C_RC16 = 1024


def make_consts():
    c = np.zeros((128, NCONST), np.float32)
    idx = np.arange(128)
    same = (idx[:, None] // 64) == (idx[None, :] // 64)
    c[:, C_ID:C_ID + 128] = np.eye(128, dtype=np.float32)
    c[:, C_TRI:C_TRI + 128] = (same & (idx[:, None] <= idx[None, :])).astype(np.float32)
    c[:, C_BLK:C_BLK + 128] = same.astype(np.float32)
    c[:, C_MSTR:C_MSTR + 128] = (same & (idx[:, None] < idx[None, :])).astype(np.float32)
    c[:, C_ONES:C_ONES + 128] = 1.0
    c[:, C_ONESD:C_ONESD + 128] = 1.0 / D
    e16 = np.eye(16, dtype=np.float32).reshape(1, 256)
    c[:, C_EYE16:C_EYE16 + 256] = e16
    rc = np.zeros((4, 16), np.float32)
    for gi, w in enumerate(POOL_W):
        rc[gi] = 1.0 / np.minimum(w, np.arange(16) + 1)
    c[:, C_RC16:C_RC16 + 64] = rc.reshape(1, 64)
    return c


class Kern:
    def __init__(self, nlayers=DEPTH, dbg=None, do_sample=True):
        self.nlayers = nlayers
        self.dbg = dbg
        self.do_sample = do_sample
        nc = bass.Bass("TRN2", target_bir_lowering=False)
        nc.dge_precook = False
        self.nc = nc
        self.es = ExitStack()

    def dram_in(self, name, shape, dt=F32):
        return self.nc.dram_tensor(name, list(shape), dt, kind="ExternalInput").ap()

    def dram_out(self, name, shape):
        return self.nc.dram_tensor(name, list(shape), F32, kind="ExternalOutput").ap()

    def mm(self, out, lhsT, rhs, start=True, stop=True, f32=False):
        o = out.ap
        l = lhsT.ap if f32 else lhsT.r
        r = rhs.ap if f32 else rhs.r
        self.S.op("pe", lambda h: h.matmul(o, l, r, start=start, stop=stop), [lhsT, rhs], [out])

    def tr(self, out, in_):
        o, i, idn = out.ap, in_.ap, self.ident.ap
        np_ = in_.ap.shape[0]
        idn = self.identb.p(0, np_, slice(0, np_)).ap
        self.S.op("pe", lambda h: h.transpose(o, i, idn), [in_, self.ident], [out])

    def act(self, out, in_, func, scale=1.0, bias=0.0, r=False, eng="act"):
        o = out.r if r else out.ap
        i = in_.ap
        reads = [in_]
        kw = {}
        if isinstance(scale, Ref):
            reads.append(scale)
            kw["scale"] = scale.ap
        elif scale != 1.0:
            kw["scale"] = float(scale)
        if isinstance(bias, Ref):
            reads.append(bias)
            kw["bias"] = bias.ap
        elif bias != 0.0:
            kw["bias"] = float(bias)
        self.S.op("act", lambda h: h.activation(o, i, func, **kw), reads, [out])

    def tt(self, out, in0, in1, op, r=False, eng="dve"):
        o = out.r if r else out.ap
        a, b = in0.ap, in1.ap
        self.S.op(eng, lambda h: h.tensor_tensor(o, a, b, op), [in0, in1], [out])

    def ts(self, out, in0, s1, op0, s2=None, op1=None, r=False, eng="dve"):
        o = out.r if r else out.ap
        a = in0.ap
        reads = [in0]
        v1 = s1
        if isinstance(s1, Ref):
            reads.append(s1)
            v1 = s1.ap
        v2 = s2
        if isinstance(s2, Ref):
            reads.append(s2)
            v2 = s2.ap
        if op1 is None:
            self.S.op(eng, lambda h: h.tensor_scalar(o, a, v1, None, op0), reads, [out])
        else:
            self.S.op(eng, lambda h: h.tensor_scalar(o, a, v1, v2, op0, op1), reads, [out])

    def stt(self, out, in0, scalar, in1, op0, op1, r=False):
        o = out.r if r else out.ap
        a, b = in0.ap, in1.ap
        reads = [in0, in1]
        sv = scalar
        if isinstance(scalar, Ref):
            reads.append(scalar)
            sv = scalar.ap
        self.S.op("dve", lambda h: h.scalar_tensor_tensor(o, a, sv, b, op0, op1), reads, [out])

    def cp(self, out, in_, r=False, eng="dve"):
        o = out.r if r else out.ap
        i = in_.ap
        if eng == "act":
            self.S.op("act", lambda h: h.copy(o, i), [in_], [out])
        else:
            self.S.op(eng, lambda h: h.tensor_copy(o, i), [in_], [out])

    def dma(self, out, in_, q="sp"):
        o, i = out.ap, in_.ap
        if i.dtype == F32R and o.dtype != F32R:
            o = o.bitcast(F32R)
        if o.dtype == F32R and i.dtype != F32R:
            i = i.bitcast(F32R)
        self.S.op(q, lambda h: h.dma_start(out=o, in_=i), [in_], [out], dma=True)

    def bc(self, ref, shape, axis):
        ap = ref.ap.unsqueeze(axis).broadcast_to(list(shape))
        return Ref(ap, ref.ivs)

    def sub(self, ref, *idx):
        return Ref(ref.ap[idx], ref.ivs)

    def build(self):
        nc, es = self.nc, self.es
        L = self.nlayers
        di, do = self.dram_in, self.dram_out
        self.d = d = {}
        d["xp"] = di("xp", (T, D))
        d["xs"] = di("xs", (NS, D))
        d["sdelta"] = di("sdelta", (DEPTH, NS, 4, 128, 128), F32R)
        d["sconv"] = di("sconv", (DEPTH, NS, 3, QKV))
        d["spool"] = di("spool", (DEPTH, NS, 15, 512))
        for nm, shp in (("wg1", (DEPTH, D, DFF)), ("wu1", (DEPTH, D, DFF)), ("wd1", (DEPTH, DFF, D)),
                        ("win", (DEPTH, D, INDIM)), ("wout", (DEPTH, D, D)),
                        ("wg2", (DEPTH, D, DFF)), ("wu2", (DEPTH, D, DFF)), ("wd2", (DEPTH, DFF, D)),
                        ("poolw", (DEPTH, 4, 128, 128))):
            d[nm] = di(nm, shp, F32R)
        for nm in ("ln1g", "ln1b", "ln2g", "ln2b", "ln3g", "ln3b"):
            d[nm] = di(nm, (DEPTH, D))
        d["convw"] = di("convw", (DEPTH, 4, QKV))
        d["alog"] = di("alog", (DEPTH, 4))
        d["dtb"] = di("dtb", (DEPTH, 4))
        d["onorm"] = di("onorm", (DEPTH, 128))
        d["pscale"] = di("pscale", (DEPTH, 512))
        d["constf"] = di("constf", (128, NCONST))
        d["constr"] = di("constr", (128, 384), F32R)
        d["yp"] = do("yp", (T, D))
        d["ys"] = do("ys", (NS, D))
        d["dprm"] = do("dprm", (DEPTH, 4, 128, 128))
        d["cprm"] = do("cprm", (DEPTH, 3, QKV))
        d["pprm"] = do("pprm", (DEPTH, 15, 512))
        d["dsmp"] = do("dsmp", (DEPTH, NS, 4, 128, 128))
        d["csmp"] = do("csmp", (DEPTH, NS, 3, QKV))
        d["psmp"] = do("psmp", (DEPTH, NS, 15, 512))
        if self.dbg:
            d["dbg"] = do("dbg", self.dbg)

        AWF, AWR = 10000, 43000
        arenaF = es.enter_context(nc.sbuf_tensor("arenaF", [128, AWF], F32))
        arenaR = es.enter_context(nc.sbuf_tensor("arenaR", [128, AWR], F32))
        psum = es.enter_context(nc.psum_tensor("psum", [128, 4096], F32))
        self.S = S = Sched(nc, es, {"sbF": AWF, "sbR": AWR, "ps": 4096})
        A = Alloc(arenaF, "sbF", 0, AWF)
        AR = Alloc(arenaR, "sbR", 0, AWR)
        self.arenaF, self.arenaR = arenaF, arenaR
        self.PS = [Buf(psum, "ps", 512 * i, (512,)) for i in range(8)]

        self.X = AR(DC, NTOK)
        self.cr = AR(384)
        self.poolw = AR(DEPTH * 4, 128)
        self.Sst = AR(4, 128)
        self.cf = A(NCONST)
        self.lnp = A(192)
        self.convw = A(192)
        self.pscale = A(16)
        self.onorm = A(4)
        self.dtb = A(16)
        self.nea = A(16)
        self.onorm_bc = A(DEPTH * 128)
        self.halo_c = A(12, 4)
        self.halo_p = A(4, 16)
        self.FZ = (A.cur, AWF)
        self.RZ = (AR.cur, AWR)

        cf, cr = self.cf, self.cr
        self.ident = cf[C_ID:C_ID + 128]
        self.identb = Buf(arenaF, "sbF", cf.off + C_ID, (128,))
        self.tri = cf[C_TRI:C_TRI + 128]
        self.blk = cf[C_BLK:C_BLK + 128]
        self.mstr = cf[C_MSTR:C_MSTR + 128]
        self.onesf = cf[C_ONES:C_ONES + 128]
        self.eye16 = cf[C_EYE16:C_EYE16 + 256]
        self.rc16 = cf[C_RC16:C_RC16 + 64]
        self.identr = cr[0:128]
        self.onesr = cr[128:256]
        self.onesdr = cr[256:384]

        self.load_consts()
        self.load_x()
        for l in range(L):
            self.layer(l)
        self.store_y()
        S.emit()
        return nc

    def load_consts(self):
        d = self.d
        self.dma(self.cf[:], _dram_ref(d["constf"]))
        self.dma(self.cr[:], _dram_ref(d["constr"]))
        A = Alloc(self.arenaF, "sbF", *self.FZ)
        stg = [A(128) for _ in range(5)]
        names = ("ln1g", "ln1b", "ln2g", "ln2b", "ln3g", "ln3b")
        for i, nm in enumerate(names):
            self.dma(stg[i // 3].p(32 * (i % 3), 32 * (i % 3) + 32), _dram_ref(d[nm].rearrange("l (c p) -> (l c) p", p=128)))
        cwv = d["convw"].rearrange("l i (c p) -> (l i c) p", p=128)
        self.dma(stg[2].p(0, 96), _dram_ref(cwv[0:96, :]))
        self.dma(stg[3].p(0, 96), _dram_ref(cwv[96:192, :]))
        self.dma(stg[4].p(0, 16), _dram_ref(d["pscale"].rearrange("l (c p) -> (l c) p", p=128)))
        self.dma(stg[4].p(16, 20), _dram_ref(d["onorm"]))
        ps = self.PS[0]
        for i in range(4):
            self.tr(ps[96 * i:96 * i + 96], stg[i].p(0, 96))
        self.tr(self.PS[1][0:20], stg[4].p(0, 20))
        self.cp(self.lnp[:], ps[0:192])
        self.cp(self.convw[:], ps[192:384])
        self.cp(self.pscale[:], self.PS[1][0:16])
        self.cp(self.onorm[:], self.PS[1][16:20])
        self.dma(self.dtb[:], _dram_ref(d["dtb"].rearrange("l h -> (l h)").partition_broadcast(128)))
        self.dma(self.nea[:], _dram_ref(d["alog"].rearrange("l h -> (l h)").partition_broadcast(128)))
        self.dma(self.onorm_bc[:], _dram_ref(d["onorm"].rearrange("l e -> (l e)").partition_broadcast(128)))
        for l in range(DEPTH):
            self.dma(self.poolw[4 * l:4 * l + 4], _dram_ref(d["poolw"][l].rearrange("g c e -> c g e")))
        self.act(self.nea[:], self.nea[:], AF.Exp)
        self.ts(self.nea[:], self.nea[:], -1.0, ALU.mult)

    def load_x(self):
        d = self.d
        A = Alloc(self.arenaF, "sbF", *self.FZ)
        stg = Ring([A(D) for _ in range(3)])
        k = 0
        for tb in range(T // 128 + 1):
            s = stg.next()
            if tb < T // 128:
                npart, col0 = 128, colp(tb * 128)
                self.dma(s.p(0, 128), _dram_ref(d["xp"][tb * 128:(tb + 1) * 128, :]))
            else:
                npart, col0 = NS, SC0
                self.dma(s.p(0, NS), _dram_ref(d["xs"]))
            for half in range(2):
                ps = self.PS[k % 8]
                k += 1
                for c4 in range(4):
                    c = half * 4 + c4
                    self.tr(ps.p(0, 128, slice(c4 * 128, c4 * 128 + npart)), s.p(0, npart, slice(c * 128, (c + 1) * 128)))
                src = Ref(ps.base[:, 0:512].rearrange("p (c t) -> p c t", c=4)[:, :, 0:npart], ps[:].ivs)
                dst = self.X[half * 4:(half + 1) * 4, col0:col0 + npart]
                if (k % 2) == 0:
                    self.cp(dst, src, r=True, eng="act")
                else:
                    self.cp(dst, src, r=True, eng="dve")

    def store_y(self):
        d = self.d
        A = Alloc(self.arenaF, "sbF", *self.FZ)
        stg = Ring([A(D) for _ in range(3)])
        k = 0
        for tb in range(T // 128 + 1):
            s = stg.next()
            if tb < T // 128:
                npart, col0 = 128, colp(tb * 128)
            else:
                npart, col0 = NS, SC0
            for half in range(2):
                ps = self.PS[k % 8]
                k += 1
                for c4 in range(4):
                    c = half * 4 + c4
                    self.tr(ps.p(0, npart, slice(c4 * 128, (c4 + 1) * 128)), self.X[c, col0:col0 + npart])
                dst = s.p(0, npart, slice(half * 512, (half + 1) * 512))
                src = ps.p(0, npart)
                if (k % 2) == 0:
                    self.cp(dst, src, eng="act")
                else:
                    self.cp(dst, src, eng="dve")
            if tb < T // 128:
                self.dma(_dram_ref(d["yp"][tb * 128:(tb + 1) * 128, :]), s.p(0, 128))
            else:
                self.dma(_dram_ref(d["ys"]), s.p(0, NS))

    def layer_norm(self, rb, loc, n, col0, gi, bi, l, eps, tmp, par=0, split=False):
        sqr, mean_sb, m2, rstd, tt_ = tmp
        pm, pq = (self.PS[6], self.PS[7]) if par == 0 else (self.PS[4], self.PS[5])

        def stats():
            for c in range(DC):
                sq = sqr.next()
                self.act(sq[0:n], rb[c, loc:loc + n], AF.Square, r=True)
                self.mm(pm[0:n], self.onesdr, rb[c, loc:loc + n], start=(c == 0), stop=(c == DC - 1))
                self.mm(pq[0:n], self.onesdr, sq[0:n], start=(c == 0), stop=(c == DC - 1))

        def tail():
            self.cp(mean_sb[0:n], pm[0:n], eng="act")
            self.tt(m2[0:n], mean_sb[0:n], mean_sb[0:n], ALU.mult)
            self.tt(m2[0:n], pq[0:n], m2[0:n], ALU.subtract)
            self.ts(m2[0:n], m2[0:n], 0.0, ALU.max)
            self.act(rstd[0:n], m2[0:n], AF.Ln, bias=self.eps_ref(eps))
            self.act(rstd[0:n], rstd[0:n], AF.Exp, scale=-0.5)
            for c in range(DC):
                t = tt_.next()
                self.tt(t[0:n], rb[c, loc:loc + n], mean_sb[0:n], ALU.subtract)
                self.tt(t[0:n], t[0:n], rstd[0:n], ALU.mult)
                self.act(self.X[c, col0:col0 + n], t[0:n], AF.Identity,
                         scale=self.lnp[gi * 32 + l * 8 + c:gi * 32 + l * 8 + c + 1], bias=self.lnp[bi * 32 + l * 8 + c:bi * 32 + l * 8 + c + 1], r=True)

        if split:
            return stats, tail
        stats()
        tail()

    def eps_ref(self, eps):
        return float(eps)

    def ffn(self, l, which, tiles):
        self.mark('ffn%d_%d_%d' % (l, which, len(tiles)))
        d = self.d
        wg, wu, wd = (d["wg1"], d["wu1"], d["wd1"]) if which == 1 else (d["wg2"], d["wu2"], d["wd2"])
        gi, bi = (0, 1) if which == 1 else (4, 5)
        G = sum(n for _, n in tiles)
        locs = []
        o = 0
        for (_, n) in tiles:
            locs.append(o)
            o += n
        groups = [(0, 5), (5, 10), (10, 14), (14, 18), (18, 22)]
        AR = Alloc(self.arenaR, "sbR", *self.RZ)
        AF_ = Alloc(self.arenaF, "sbF", *self.FZ)
        GM = 1040
        rb = AR(DC, GM)
        hid = AR(5, GM)
        wgu = Ring([AR(2, DC, 128) for _ in range(3)])
        wdr = Ring([AR(5, 128) for _ in range(3)])
        sgr = Ring([AF_(512) for _ in range(2)])
        tmp = (Ring([AR(512) for _ in range(2)]), AF_(512), AF_(512), AF_(512), Ring([AF_(512) for _ in range(2)]))
        wgv = wg[l].rearrange("(kc p) f -> p kc f", p=128)
        wuv = wu[l].rearrange("(kc p) f -> p kc f", p=128)
        pgi = 0
        pdi = 0
        for g, (f0, f1) in enumerate(groups):
            nj = f1 - f0
            for j in range(f0, f1):
                w = wgu.next()
                self.dma(Ref(w[0].r, w[0].ivs), _dram_ref(wgv[:, :, j * 128:(j + 1) * 128]))
                self.dma(Ref(w[1].r, w[1].ivs), _dram_ref(wuv[:, :, j * 128:(j + 1) * 128]))
                for ti, (col0, n) in enumerate(tiles):
                    pg = self.PS[pgi % 2]
                    pu = self.PS[2 + pgi % 2]
                    pgi += 1
                    for k in range(DC):
                        self.mm(pg[0:n], w[0, k], self.X[k, col0:col0 + n], start=(k == 0), stop=(k == DC - 1))
                    for k in range(DC):
                        self.mm(pu[0:n], w[1, k], self.X[k, col0:col0 + n], start=(k == 0), stop=(k == DC - 1))
                    sg = sgr.next()
                    self.act(sg[0:n], pg[0:n], AF.Silu)
                    self.tt(hid[j - f0, locs[ti]:locs[ti] + n], sg[0:n], pu[0:n], ALU.mult, r=True)
            for m in range(DC):
                wb = wdr.next()
                src = wd[l][f0 * 128:f1 * 128, m * 128:(m + 1) * 128].rearrange("(j p) c -> p j c", p=128)
                dst = wb[0:nj]
                self.dma(Ref(dst.r, dst.ivs), _dram_ref(src))
                for ti, (col0, n) in enumerate(tiles):
                    pa = self.PS[4 + pdi % 2]
                    pdi += 1
                    for j in range(nj):
                        self.mm(pa[0:n], wb[j], hid[j, locs[ti]:locs[ti] + n], start=(j == 0), stop=(j == nj - 1))
                    rr = rb[m, locs[ti]:locs[ti] + n]
                    if g == 0:
                        self.stt(rr, self.X[m, col0:col0 + n], 2.0 * ALPHA, pa[0:n], ALU.mult, ALU.add, r=True)
                    else:
                        self.tt(rr, rr, pa[0:n], ALU.add, r=True)
        if getattr(self, "skip_ln", False):
            return
        parts = [self.layer_norm(rb, locs[ti], n, col0, gi, bi, l, 4.0 * LN_EPS, tmp, par=ti % 2, split=True)
                 for ti, (col0, n) in enumerate(tiles)]
        parts[0][0]()
        for ti in range(len(parts)):
            if ti + 1 < len(parts):
                parts[ti + 1][0]()
            parts[ti][1]()

    def layer(self, l):
        ffn_tiles = [[(0, 348), (348, 348), (696, 344)], [(1040, 512), (1552, 512)]]
        if getattr(self, "only_sample", False):
            self.mixer_sample(l)
            return
        if getattr(self, "only_mix", False):
            self.mixer_prompt(l, 0)
            return
        for p in range(2):
            self.ffn(l, 1, ffn_tiles[p])
        if getattr(self, "ffn_only", False):
            return
        for c0 in (0, 512):
            self.mixer_prompt(l, c0)
        if self.do_sample:
            self.mixer_sample(l)
        for c0 in (1024, 1536):
            self.mixer_prompt(l, c0)
        for p in range(2):
            self.ffn(l, 2, ffn_tiles[p])

    def ms(self, out, val, r=False):
        o = out.r if r else out.ap
        self.S.op("dve", lambda h: h.memset(o, val), [], [out])

    def v4(self, ref, a=4):
        return Ref(ref.ap.rearrange("p (a b) -> p a b", a=a), ref.ivs)

    def mark(self, name):
        if not hasattr(self, 'marks'):
            self.marks = []
        self.marks.append((name, len(self.S.ops['pe'])))

    def mixer_prompt(self, l, c0):
        d = self.d
        self.mark('mix%d_%d_p1pool' % (l, c0))
        N = 512
        first = (c0 == 0)
        last = (c0 + N == T)
        PS = self.PS
        winv = d["win"][l].rearrange("(kc p) f -> p kc f", p=128)
        woutv = d["wout"][l].rearrange("(kc p) f -> p kc f", p=128)
        xc = colp(c0)
        Xs = lambda k: self.X[k, xc:xc + N]
        AR = Alloc(self.arenaR, "sbR", *self.RZ)
        qkvc = AR(12, N)
        mix = AR(8, N)
        wblk = Ring([AR(DC, 128) for _ in range(3)])
        wba = AR(DC, 8)
        RP = AR.cur
        AFt = Alloc(self.arenaF, "sbF", self.FZ[1] - 96, self.FZ[1])
        ba = AFt(4, 8)
        beta = AFt(4, 4)
        yv = AFt(4, 4)
        gv = AFt(4, 4)
        self.dma(wba[:], _dram_ref(winv[:, :, 2048:2056]))
        for i in range(4):
            for k in range(DC):
                self.mm(PS[4][8 * i:8 * i + 8], self.X[k, xc + 128 * i:xc + 128 * (i + 1)], wba[k], start=(k == 0), stop=(k == DC - 1))
        self.cp(ba[:], self.v4(PS[4][0:32]), eng="act")
        self.act(beta[:], ba[:, 0:4], AF.Sigmoid)
        dtb_l = self.dtb[l * 4:(l + 1) * 4]
        nea_l = self.nea[l * 4:(l + 1) * 4]
        self.tt(yv[:], ba[:, 4:8], self.bc(dtb_l, [128, 4, 4], 1), ALU.add)
        self.ts(yv[:], yv[:], 30.0, ALU.min)
        self.act(yv[:], yv[:], AF.Exp)
        self.act(yv[:], yv[:], AF.Ln, bias=1.0)
        self.tt(gv[:], yv[:], self.bc(nea_l, [128, 4, 4], 1), ALU.mult)
        AR1 = Alloc(self.arenaR, "sbR", RP, self.RZ[1])
        dd = AR1(4, N)
        pbr = Ring([(AR1(528), AR1(528)) for _ in range(3)])
        dgp = Ring([AR1(2, 128) for _ in range(3)])
        AR1b = Alloc(self.arenaR, "sbR", RP, self.RZ[1])
        sqr = Ring([AR1b(N) for _ in range(2)])
        preR = Ring([(AR1b(516), AR1b(516)) for _ in range(3)])
        dgr = Ring([AR1b(4, 128) for _ in range(3)])
        AF1 = Alloc(self.arenaF, "sbF", *self.FZ)
        acc = Ring([AF1(N) for _ in range(2)])
        rn = Ring([AF1(N) for _ in range(2)])
        z16 = AF1(16)
        t16 = AF1(16)
        pst = AF1(512)
        cst = AF1(QKV)
        if first:
            self.ms(z16[:], 0.0)
        pool_ctx = {}

        def poolA(g):
            w = POOL_W[g]
            wb = wblk.next()
            self.dma(wb[:], _dram_ref(winv[:, :, 2056 + 128 * g:2056 + 128 * (g + 1)]))
            ps = PS[g % 2]
            for k in range(DC):
                self.mm(ps[:], wb[k], Xs(k), start=(k == 0), stop=(k == DC - 1))
            pbA, pbB = pbr.next()
            if first:
                self.cp(pbA[0:16], z16[:], r=True)
                self.cp(pbB[0:15], z16[0:15], r=True)
            else:
                self.cp(pbA[1:16], self.halo_p[g, 1:16], r=True)
                self.cp(pbB[0:15], self.halo_p[g, 1:16], r=True)
            self.cp(pbA[16:528], ps[:], r=True, eng="act")
            self.cp(pbB[15:527], pbA[16:528], r=True)
            self.cp(self.halo_p[g, 1:16], pbA[513:528])
            dg = dgp.next()
            self.ts(dg[0], self.ident, 1.0 / w - 1.0, ALU.mult, r=True)
            self.ts(dg[1], self.ident, 1.0 / w, ALU.mult, r=True)
            pool_ctx[g] = (pbA, pbB, dg)

        def poolB(g):
            w = POOL_W[g]
            pbA, pbB, dg = pool_ctx[g]
            pd = PS[2 + g % 2]
            for i in range(w):
                win = pbA[16 - i:528 - i] if i % 2 == 0 else pbB[15 - i:527 - i]
                self.mm(pd[:], dg[0 if i == 0 else 1], win, start=(i == 0), stop=(i == w - 1))
            self.cp(dd[g], pd[:], r=True, eng="act")
            if first:
                self.tt(t16[:], pd[0:16], pbA[16:32], ALU.add)
                self.tt(t16[:], t16[:], self.sub(self.rc16, slice(None), slice(g * 16, (g + 1) * 16)), ALU.mult)
                self.stt(dd[g, 0:16], t16[:], float(w), pbA[16:32], ALU.mult, ALU.subtract, r=True)
            ps2 = PS[4 + g % 2]
            self.mm(ps2[:], self.poolw[4 * l + g], dd[g])
            self.act(mix[4 + g], ps2[:], AF.Identity, scale=self.pscale[l * 4 + g:l * 4 + g + 1], r=True)
            if last:
                self.tr(PS[6].p(0, 15, slice(g * 128, (g + 1) * 128)), pbA[513:528])

        for st_ in range(5):
            if st_ < 4:
                poolA(st_)
            if st_ >= 1:
                poolB(st_ - 1)
        if last:
            self.cp(pst.p(0, 15), PS[6].p(0, 15), eng="act")
            self.dma(_dram_ref(d["pprm"][l]), pst.p(0, 15))
        self.mark('mix%d_%d_p1qkv' % (l, c0))
        qctx = {}

        def qkvA(oc):
            wb = wblk.next()
            self.dma(wb[:], _dram_ref(winv[:, :, 128 * oc:128 * (oc + 1)]))
            ps = PS[oc % 2]
            for k in range(DC):
                self.mm(ps[:], wb[k], Xs(k), start=(k == 0), stop=(k == DC - 1))
            pr, prB = preR.next()
            if first:
                self.cp(pr[0:4], z16[0:4], r=True)
                self.cp(prB[0:3], z16[0:3], r=True)
            else:
                self.cp(pr[1:4], self.halo_c[oc, 1:4], r=True)
                self.cp(prB[0:3], self.halo_c[oc, 1:4], r=True)
            self.cp(pr[4:516], ps[:], r=True, eng="act")
            self.cp(prB[3:515], pr[4:516], r=True)
            self.cp(self.halo_c[oc, 1:4], pr[513:516])
            dg = dgr.next()
            for i in range(4):
                self.ts(dg[i], self.ident, self.convw[l * 48 + i * 12 + oc:l * 48 + i * 12 + oc + 1], ALU.mult, r=True)
            qctx[oc] = [pr, prB, dg, None]

        def qkvB(oc):
            pr, prB, dg, _ = qctx[oc]
            pc = PS[2 + oc % 2]
            for i in range(4):
                self.mm(pc[:], dg[i], (pr[1 + i:513 + i] if i % 2 else prB[i:512 + i]), start=(i == 0), stop=(i == 3))
            if oc >= 8:
                self.act(qkvc[oc], pc[:], AF.Silu, r=True)
            else:
                a = acc.next()
                qctx[oc][3] = a
                self.act(a[:], pc[:], AF.Silu)
                sq = sqr.next()
                self.act(sq[:], a[:], AF.Square, r=True)
                self.mm(PS[4 + oc % 2][:], self.onesr, sq[:])

        def qkvC(oc):
            pr, prB, dg, a = qctx[oc]
            if oc < 8:
                r_ = rn.next()
                self.act(r_[:], PS[4 + oc % 2][:], AF.Ln, bias=L2_EPS)
                self.act(r_[:], r_[:], AF.Exp, scale=-0.5, bias=(-0.5 * float(np.log(128.0)) if oc < 4 else 0.0))
                self.tt(qkvc[oc], a[:], r_[:], ALU.mult, r=True)
            if last:
                self.tr(PS[7].p(0, 3, slice((oc % 4) * 128, (oc % 4 + 1) * 128)), pr[513:516])
                if oc % 4 == 3:
                    b3 = oc // 4
                    self.cp(cst.p(0, 3, slice(b3 * 512, (b3 + 1) * 512)), PS[7].p(0, 3), eng="act")

        for st_ in range(14):
            if st_ < 12:
                qkvA(st_)
            if 1 <= st_ <= 12:
                qkvB(st_ - 1)
            if st_ >= 2:
                qkvC(st_ - 2)
        if last:
            self.dma(_dram_ref(d["cprm"][l]), cst.p(0, 3))
        if getattr(self, 'mix_stop', 99) <= 2:
            return
        self.mark('mix%d_%d_p2' % (l, c0))
        AR2 = Alloc(self.arenaR, "sbR", RP, self.RZ[1])
        ctxs = []
        for _p in range(2):
            ctxs.append(dict(TI5=AR2(N), QKm=AR2(N), kend=AR2(N), nwk=AR2(N), qdec=AR2(N), X5=AR2(4, 256)))
        Xtmp = AR2(4, 256)
        Tb = AR2(N)
        Pb = AR2(N)
        usb = AR2(N)
        AF2 = Alloc(self.arenaF, "sbF", *self.FZ)
        gcs = AF2(8)
        egc = AF2(4)
        kes = AF2(4)
        nbe = AF2(4)
        grhs = AF2(N)
        brhs = AF2(N)
        diff = AF2(N)
        dinc = AF2(N)
        W1 = AF2(N)
        for _p in range(2):
            ctxs[_p]["egr"] = AF2(N)
        if first:
            self.ms(grhs[:], 0.0)
            self.cp(self.Sst[:], self.v4(grhs[:]), r=True)
        H = lambda buf, h: buf[128 * h:128 * (h + 1)]

        def stageA(i):
            cx = ctxs[i % 2]
            TI5, QKm, kend, nwk, qdec, X5, egr = cx["TI5"], cx["QKm"], cx["kend"], cx["nwk"], cx["qdec"], cx["X5"], cx["egr"]
            tc0 = 128 * i
            qh = lambda h: qkvc[h, tc0:tc0 + 128]
            kh = lambda h: qkvc[4 + h, tc0:tc0 + 128]
            vh = lambda h: qkvc[8 + h, tc0:tc0 + 128]

            def s1():
                self.mm(PS[4][32:36], self.tri, gv[i], f32=True)
                self.mm(PS[4][36:40], self.blk, gv[i], f32=True)
                self.cp(gcs[:], PS[4][32:40], eng="act")
                self.act(egc[:], gcs[0:4], AF.Exp)
                self.tt(kes[:], gcs[4:8], gcs[0:4], ALU.subtract)
                self.act(kes[:], kes[:], AF.Exp)
                self.stt(nbe[:], beta[i], -1.0, egc[:], ALU.mult, ALU.mult)
                self.tt(self.v4(grhs[:]), self.bc(self.tri, [128, 4, 128], 1), self.bc(gv[i], [128, 4, 128], 2), ALU.mult)
                self.tt(self.v4(brhs[:]), self.bc(self.ident, [128, 4, 128], 1), self.bc(beta[i], [128, 4, 128], 2), ALU.mult)
                self.mm(PS[0][:], self.onesf, grhs[:], f32=True)
                self.mm(PS[1][:], self.onesf, brhs[:], f32=True)
                for h in range(4):
                    self.ts(H(diff, h), H(PS[0], h), gcs[h:h + 1], ALU.subtract, 0.0, ALU.min)
                self.act(egr[:], PS[0][:], AF.Exp)
                self.act(diff[:], diff[:], AF.Exp)
                self.tt(self.v4(dinc[:]), self.v4(diff[:]), self.bc(self.tri, [128, 4, 128], 1), ALU.mult)
                self.tt(self.v4(W1[:]), self.v4(diff[:]), self.bc(self.mstr, [128, 4, 128], 1), ALU.mult)
                self.stt(W1[:], W1[:], -1.0, PS[1][:], ALU.mult, ALU.mult)
                self.tt(self.v4(qdec[:]), qkvc[0:4, tc0:tc0 + 128], self.v4(egr[:]), ALU.mult, r=True)

            def s2():
                for h in range(4):
                    self.tr(H(PS[0], h), kh(h))
                for h in range(4):
                    self.tr(H(PS[1], h), vh(h))
                self.tt(Xtmp[:, 0:128], self.v4(PS[1][:]), self.bc(beta[i], [128, 4, 128], 2), ALU.mult, r=True)
                self.tt(Xtmp[:, 128:256], self.v4(PS[0][:]), self.bc(nbe[:], [128, 4, 128], 2), ALU.mult, r=True)
                self.tt(self.v4(kend[:]), self.v4(PS[0][:]), self.bc(kes[:], [128, 4, 128], 2), ALU.mult, r=True)

            def s3():
                for h in range(4):
                    self.mm(H(PS[2], h), kh(h), kh(h))
                for h in range(4):
                    self.mm(H(PS[3], h), kh(h), qh(h))
                self.tt(Tb[:], PS[2][:], W1[:], ALU.mult, r=True)
                self.tt(QKm[:], PS[3][:], dinc[:], ALU.mult, r=True)
                for h in range(4):
                    self.tr(H(PS[0], h), H(Tb, h))
                self.cp(Pb[:], PS[0][:], r=True, eng="act")

            def level(kx):
                def f():
                    Xk, Xn = (Xtmp, X5) if kx % 2 == 0 else (X5, Xtmp)
                    for h in range(4):
                        bank = PS[h // 2]
                        self.mm(bank[256 * (h % 2):256 * (h % 2 + 1)], H(Tb, h), Xk[h])
                    for h in range(4):
                        self.mm(H(PS[2], h), H(Pb, h), H(Tb, h))
                    if kx < 4:
                        for h in range(4):
                            self.mm(H(PS[3], h), H(Tb, h), H(Pb, h))
                    for hp in range(2):
                        self.tt(Xn[2 * hp:2 * hp + 2], Xk[2 * hp:2 * hp + 2], self.v4(PS[hp][:], 2), ALU.add, r=True)
                    if kx < 4:
                        self.cp(Tb[:], PS[2][:], r=True, eng="act")
                        self.cp(Pb[:], PS[3][:], r=True, eng="act")
                    else:
                        self.tt(self.v4(TI5[:]), self.v4(PS[2][:]), self.bc(self.ident, [128, 4, 128], 1), ALU.add, r=True)
                        for h in range(4):
                            self.mm(H(PS[4], h), X5[h, 128:256], H(TI5, h))
                        self.cp(nwk[:], PS[4][:], r=True, eng="act")
                return f

            return [s1, s2, s3] + [level(kx) for kx in range(5)]

        def stageB(i):
            cx = ctxs[i % 2]
            TI5, QKm, kend, nwk, qdec, X5, egr = cx["TI5"], cx["QKm"], cx["kend"], cx["nwk"], cx["qdec"], cx["X5"], cx["egr"]
            tc0 = 128 * i
            steps = []
            for c in range(2):
                p0, p1 = 64 * c, 64 * c + 64

                def a_(c=c, p0=p0, p1=p1):
                    for h in range(4):
                        self.mm(H(PS[7], h), H(TI5, h), X5[h, 0:128], start=True, stop=False)
                        self.mm(H(PS[7], h), H(nwk, h), self.Sst[h], start=False, stop=True)

                def b_(c=c, p0=p0, p1=p1):
                    self.cp(usb.p(p0, p1), PS[7].p(p0, p1), r=True, eng="act")

                def c_(c=c, p0=p0, p1=p1):
                    for h in range(4):
                        oc_ = slice(128 * h + 64 * c, 128 * h + 64 * c + 64)
                        hs = slice(128 * h, 128 * h + 128)
                        self.mm(PS[5][oc_], self.Sst[h], qdec[oc_], start=True, stop=False)
                        self.mm(PS[5][oc_], usb.p(p0, p1, hs), QKm.p(p0, p1, oc_), start=False, stop=True)
                    for h in range(4):
                        hs = slice(128 * h, 128 * h + 128)
                        self.mm(H(PS[6], h), kend.p(p0, p1, hs), usb.p(p0, p1, hs))

                def d_(c=c, p0=p0, p1=p1):
                    for h in range(4):
                        ge = egr[128 * h + 64 * c + 63:128 * h + 64 * c + 64]
                        self.stt(self.Sst[h], self.Sst[h], ge, H(PS[6], h), ALU.mult, ALU.add, r=True)

                steps += [a_, b_, c_, d_]

            def fin():
                self.cp(mix[0:4, tc0:tc0 + 128], self.v4(PS[5][:]), r=True, eng="act")
            return steps, fin

        zbuf = [Buf(self.arenaF, "sbF", self.FZ[0] + 4224 + N * h_, (N,)) for h_ in range(4)]

        def zA1(h):
            def f():
                wb = wblk.next()
                self.dma(wb[:], _dram_ref(winv[:, :, QKV + 128 * h:QKV + 128 * (h + 1)]))
                ps = PS[h % 2]
                for k in range(DC):
                    self.mm(ps[:], wb[k], Xs(k), start=(k == 0), stop=(k == DC - 1))
                self.act(zbuf[h][:], ps[:], AF.Silu)
            return f

        for f in stageA(0):
            f()
        for i in range(4):
            self.mark('mix%d_%d_p2tile%d' % (l, c0, i))
            segsA = stageA(i + 1) if i < 3 else [zA1(h_) for h_ in range(4)]
            stepsB, fin = stageB(i)
            order = []
            ai = 0
            for bi in range(0, 8, 2):
                if ai < len(segsA):
                    order.append(segsA[ai])
                    ai += 1
                order += stepsB[bi:bi + 2]
            order += segsA[ai:]
            for f in order:
                f()
            fin()
        if last:
            for h in range(4):
                self.dma(_dram_ref(d["dprm"][l, h]), self.Sst[h])
        self.mark('mix%d_%d_p3' % (l, c0))
        AR3 = Alloc(self.arenaR, "sbR", RP, self.RZ[1])
        rb = AR3(DC, N)
        sqr3 = Ring([AR3(N) for _ in range(2)])
        AF3 = Alloc(self.arenaF, "sbF", *self.FZ)
        zt = Ring([AF3(N) for _ in range(2)])
        rt = Ring([AF3(N) for _ in range(2)])
        tmp = (sqr3, AF3(N), AF3(N), AF3(N), Ring([AF3(N) for _ in range(2)]))
        zctx = {}

        def zA(h):
            z = zbuf[h]
            sq = sqr3.next()
            self.act(sq[:], mix[h], AF.Square, r=True)
            self.mm(PS[4 + h % 2][:], self.onesr, sq[:])
            zctx[h] = z

        def zB(h):
            z = zctx[h]
            r_ = rt.next()
            self.act(r_[:], PS[4 + h % 2][:], AF.Ln, scale=1.0 / 128.0, bias=RMS_EPS)
            self.act(r_[:], r_[:], AF.Exp, scale=-0.5)
            self.tt(z[:], z[:], r_[:], ALU.mult)
            self.stt(mix[h], mix[h], self.onorm[l:l + 1], z[:], ALU.mult, ALU.mult, r=True)

        for st_ in range(5):
            if st_ < 4:
                zA(st_)
            if st_ >= 1:
                zB(st_ - 1)
        for m in range(DC):
            wb = wblk.next()
            self.dma(wb[:], _dram_ref(woutv[:, :, 128 * m:128 * (m + 1)]))
            ps = PS[m % 4]
            for k in range(DC):
                self.mm(ps[:], wb[k], mix[k], start=(k == 0), stop=(k == DC - 1))
            self.stt(rb[m], Xs(m), ALPHA, ps[:], ALU.mult, ALU.add, r=True)
        self.layer_norm(rb, 0, N, xc, 2, 3, l, LN_EPS, tmp)

    def red(self, out, in_, op=None):
        o, i = out.ap, in_.ap
        op = op or ALU.add
        self.S.op("dve", lambda h: h.tensor_reduce(o, i, AX.X, op), [in_], [out])

    def mixer_sample(self, l):
        d = self.d
        self.mark('sample%d' % l)
        PS = self.PS
        P = NS
        winv = d["win"][l].rearrange("(kc p) f -> p kc f", p=128)
        woutv = d["wout"][l].rearrange("(kc p) f -> p kc f", p=128)
        Xs = lambda k: self.X[k, SC0:SC0 + NS]
        ident16 = Ref(self.ident.ap[0:P, 0:P], self.ident.ivs)
        ones16 = Ref(self.onesf.ap[0:P, 0:128], self.onesf.ivs)
        v3 = lambda ref, a: Ref(ref.ap.rearrange("p (a b) -> p a b", a=a), ref.ivs)
        AR = Alloc(self.arenaR, "sbR", *self.RZ)
        wblk = Ring([AR(DC, 128) for _ in range(3)])
        wba = AR(DC, 8)
        Sall = AR(4 * NS, 128)
        kqT = AR(8, P)
        km = AR(4, P, P)
        qm = AR(4, P, P)
        qkn = AR(1024)
        umr = Ring([AR(512) for _ in range(2)])
        rhsr = AR(64)
        mixs = AR(8, P)
        ddT = AR(4, P)
        rbs = AR(DC, P)
        sqs = Ring([AR(P) for _ in range(2)])
        wblk = Ring(wblk.bufs + [AR(DC, 128) for _ in range(6)])
        AFa = Alloc(self.arenaF, "sbF", *self.FZ)
        qkv = AFa(QKV)
        zt = AFa(512)
        pt = AFa(512)
        bat = AFa(8)
        F2 = AFa.cur
        cst = AFa(3, 512)
        cw = AFa(4, 512)
        sdv = d["sdelta"][l].rearrange("s h d e -> d (s h) e")
        for j in range(8):
            self.dma(Sall[8 * j:8 * j + 8], _dram_ref(sdv[:, 8 * j:8 * j + 8, :]))
        cols = [128 * b for b in range(16)] + [2056 + 128 * g for g in range(4)]
        dsts = [qkv.p(0, P, slice(0, 512)), qkv.p(0, P, slice(512, 1024)), qkv.p(0, P, slice(1024, 1536)), zt.p(0, P), pt.p(0, P)]
        for grp in range(5):
            bank = PS[grp % 4]
            for b4 in range(4):
                wb = wblk.next()
                c = cols[4 * grp + b4]
                self.dma(wb[:], _dram_ref(winv[:, :, c:c + 128]))
                for k in range(DC):
                    self.mm(bank.p(0, P, slice(128 * b4, 128 * b4 + 128)), Xs(k), wb[k], start=(k == 0), stop=(k == DC - 1))
            self.cp(dsts[grp], bank.p(0, P), eng="act")
        self.dma(wba[:], _dram_ref(winv[:, :, 2048:2056]))
        for k in range(DC):
            self.mm(PS[4].p(0, P, slice(0, 8)), Xs(k), wba[k], start=(k == 0), stop=(k == DC - 1))
        self.cp(bat.p(0, P), PS[4].p(0, P, slice(0, 8)), eng="act")
        if getattr(self, 'smp_stop', 99) <= 1:
            return
        self.mark('smp%d_1' % l)
        self.dma(_dram_ref(d["csmp"][l, :, 0:2, :]), _dram_ref(d["sconv"][l, :, 1:3, :]))
        self.dma(_dram_ref(d["csmp"][l, :, 2, :]), qkv.p(0, P))
        self.dma(_dram_ref(d["psmp"][l, :, 0:14, :]), _dram_ref(d["spool"][l, :, 1:15, :]))
        self.dma(_dram_ref(d["psmp"][l, :, 14, :]), pt.p(0, P))
        if getattr(self, 'smp_stop', 99) <= 2:
            return
        self.mark('smp%d_2' % l)
        for j in range(3):
            self.dma(cst.p(0, P), _dram_ref(d["sconv"][l][:, :, 512 * j:512 * (j + 1)]))
            for i in range(4):
                self.dma(cw.p(0, P, i), _dram_ref(d["convw"][l, i, 512 * j:512 * (j + 1)].partition_broadcast(P)))
            pre = qkv.p(0, P, slice(512 * j, 512 * (j + 1)))
            tA = Buf(self.arenaF, "sbF", AFa.cur, (512,)).p(0, P)
            tB = Buf(self.arenaF, "sbF", AFa.cur + 512, (512,)).p(0, P)
            self.tt(tA, pre, cw.p(0, P, 3), ALU.mult)
            for i in range(3):
                self.tt(tB, cst.p(0, P, i), cw.p(0, P, i), ALU.mult)
                self.tt(tA, tA, tB, ALU.add)
            self.act(pre, tA, AF_SILU)
        if getattr(self, 'smp_stop', 99) <= 3:
            return
        self.mark('smp%d_3' % l)
        AFb = Alloc(self.arenaF, "sbF", F2, self.FZ[1])
        t1024 = AFb(1024)
        ss8 = AFb(8)
        rn8 = AFb(8)
        beta = AFb(4)
        yv = AFb(4)
        eg = AFb(4)
        nbe = AFb(4)
        qkd = AFb(4)
        rhse = AFb(P, 4)
        egb = AFb(64)
        u = AFb(512)
        t2 = AFb(512)
        o = AFb(512)
        ss4 = AFb(4)
        dtok = AFb(512)
        sumg = AFb(128)
        pst1 = Buf(self.arenaF, "sbF", F2, (15, 128))
        self.ms(t1024[:], 0.0)
        self.cp(qkn[:], t1024[:], r=True)
        self.cp(rhsr[:], t1024[0:64], r=True)
        for _ in range(2):
            umz = umr.next()
            self.cp(umz[:], t1024[0:512], r=True)
        qk = qkv.p(0, P, slice(0, 1024))
        self.tt(t1024.p(0, P), qk, qk, ALU.mult)
        self.red(ss8.p(0, P), v3(t1024.p(0, P), 8))
        self.act(rn8.p(0, P), ss8.p(0, P), AF.Ln, bias=L2_EPS)
        self.act(rn8.p(0, P), rn8.p(0, P), AF.Exp, scale=-0.5)
        self.ts(rn8.p(0, P, slice(0, 4)), rn8.p(0, P, slice(0, 4)), float(128.0 ** -0.5), ALU.mult)
        self.tt(v3(qkn.p(0, P), 8), v3(qk, 8), self.bc(rn8.p(0, P), [P, 8, 128], 2), ALU.mult, r=True)
        qn = qkn.p(0, P, slice(0, 512))
        kn = qkn.p(0, P, slice(512, 1024))
        vt = qkv.p(0, P, slice(1024, 1536))
        if getattr(self, 'smp_stop', 99) <= 4:
            return
        self.mark('smp%d_4' % l)
        self.act(beta.p(0, P), bat.p(0, P, slice(0, 4)), AF.Sigmoid)
        self.tt(yv.p(0, P), bat.p(0, P, slice(4, 8)), self.dtb.p(0, P, slice(4 * l, 4 * l + 4)), ALU.add)
        self.ts(yv.p(0, P), yv.p(0, P), 30.0, ALU.min)
        self.act(yv.p(0, P), yv.p(0, P), AF.Exp)
        self.act(yv.p(0, P), yv.p(0, P), AF.Ln, bias=1.0)
        self.tt(yv.p(0, P), yv.p(0, P), self.nea.p(0, P, slice(4 * l, 4 * l + 4)), ALU.mult)
        self.act(eg.p(0, P), yv.p(0, P), AF.Exp)
        self.stt(nbe.p(0, P), beta.p(0, P), -1.0, eg.p(0, P), ALU.mult, ALU.mult)
        if getattr(self, 'smp_stop', 99) <= 5:
            return
        self.mark('smp%d_5' % l)
        for j in range(8):
            self.tr(PS[5][P * j:P * j + P], qkn.p(0, P, slice(128 * j, 128 * j + 128)))
        self.cp(kqT[:], v3(PS[5][0:8 * P], 8), r=True, eng="act")
        eye = v3(self.eye16, P)
        self.tt(km[:], self.bc(kqT[4:8], [128, 4, P, P], 2), self.bc(eye, [128, 4, P, P], 1), ALU.mult, r=True)
        self.tt(qm[:], self.bc(kqT[0:4], [128, 4, P, P], 2), self.bc(eye, [128, 4, P, P], 1), ALU.mult, r=True)
        for (msk, bank) in ((km, PS[6]), (qm, PS[7])):
            for h in range(4):
                for i in range(P):
                    self.mm(bank.p(0, P, slice(128 * h, 128 * h + 128)), msk[h, i], Sall[4 * i + h], start=(i == 0), stop=(i == P - 1))
        if getattr(self, 'smp_stop', 99) <= 6:
            return
        self.mark('smp%d_6' % l)
        b3 = lambda x: self.bc(x.p(0, P), [P, 4, 128], 2)
        self.tt(v3(u.p(0, P), 4), v3(vt, 4), b3(beta), ALU.mult)
        self.tt(v3(t2.p(0, P), 4), v3(PS[6].p(0, P), 4), b3(nbe), ALU.mult)
        self.tt(u.p(0, P), u.p(0, P), t2.p(0, P), ALU.add)
        self.tt(t2.p(0, P), qn, kn, ALU.mult)
        self.red(qkd.p(0, P), v3(t2.p(0, P), 4))
        self.tt(v3(o.p(0, P), 4), v3(PS[7].p(0, P), 4), b3(eg), ALU.mult)
        self.tt(v3(t2.p(0, P), 4), v3(u.p(0, P), 4), b3(qkd), ALU.mult)
        self.tt(o.p(0, P), o.p(0, P), t2.p(0, P), ALU.add)
        if getattr(self, 'smp_stop', 99) <= 7:
            return
        self.mark('smp%d_7' % l)
        self.tt(Ref(rhsr.p(0, P).ap.rearrange("p (a b) -> p a b", a=P), rhsr[:].ivs), self.bc(eg.p(0, P), [P, P, 4], 1), self.bc(ident16, [P, P, 4], 2), ALU.mult, r=True)
        self.mm(PS[4][64:128], self.onesr, rhsr[:])
        self.cp(egb[:], PS[4][64:128], eng="act")
        if getattr(self, 'smp_stop', 99) <= 7.1:
            return
        import os as _os
        ums = {}

        def mk_um(i):
            ums[i] = umr.next()
            self.ts(ums[i].p(0, P), u.p(0, P), Ref(self.ident.ap[0:P, i:i + 1], self.ident.ivs), ALU.mult, r=True)

        mk_um(0)
        for i in range(P if not _os.environ.get('DBG_SKIP_SUPD') else 0):
            if getattr(self, 'smp_stop', 99) <= 7.2 and i >= 1:
                break
            um = ums[i]
            import os as _os
            bank = PS[(4 if _os.environ.get('DBG_BANK') else 0) + i % 4]
            if _os.environ.get("DBG_V") == "B":
                continue
            for h in range(4 if not _os.environ.get('DBG_H1') else 1):
                if _os.environ.get("DBG_OP") == "R":
                    self.mm(bank[128 * h:128 * h + 128], self.identr, Sall[1])
                elif _os.environ.get("DBG_OP") == "P":
                    self.mm(bank[128 * h:128 * h + 128], Sall[0], um[128 * h:128 * h + 128])
                elif _os.environ.get("DBG_OP") == "Q":
                    self.mm(bank[128 * h:128 * h + 128], qkn[512 + 128 * h:512 + 128 * h + 128], Sall[1])
                else:
                    self.mm(bank[128 * h:128 * h + 128], qkn[512 + 128 * h:512 + 128 * h + 128], um[128 * h:128 * h + 128])
            if i + 1 < P:
                mk_um(i + 1)
            if _os.environ.get("DBG_V") == "A":
                continue
            for h in range(4 if not _os.environ.get('DBG_H1') else 1):
                pi = 4 * i + h
                if _os.environ.get("DBG_V") == "C":
                    self.stt(Sall[pi], Sall[pi], 0.5, bank[128 * h:128 * h + 128], ALU.mult, ALU.add, r=True)
                elif _os.environ.get("DBG_V") == "K":
                    self.cp(t1024[0:128], Ref(PS[5][128 * h:128 * h + 128].ap, bank[128 * h:128 * h + 128].ivs))
                elif _os.environ.get("DBG_V") == "N":
                    self.cp(t1024.p(0, 32, slice(0, 128)), bank.p(0, 32, slice(128 * h, 128 * h + 128)))
                elif _os.environ.get("DBG_V") == "O":
                    self.cp(t1024.p(64, 128, slice(0, 128)), bank.p(64, 128, slice(128 * h, 128 * h + 128)))
                elif _os.environ.get("DBG_V") == "H":
                    self.cp(t1024[0:128], bank[128 * h:128 * h + 128])
                elif _os.environ.get("DBG_V") == "I":
                    self.cp(t1024[0:128], bank[128 * h:128 * h + 128], eng="act")
                elif _os.environ.get("DBG_V") == "E":
                    self.tt(t1024[0:128], Sall[pi], bank[128 * h:128 * h + 128], ALU.add)
                elif _os.environ.get("DBG_V") == "F":
                    self.cp(Sall[pi], bank[128 * h:128 * h + 128], r=True)
                elif _os.environ.get("DBG_V") == "D":
                    self.tt(Sall[pi], Sall[pi], bank[128 * h:128 * h + 128], ALU.add, r=True)
                else:
                    self.stt(Sall[pi], Sall[pi], egb[pi:pi + 1], bank[128 * h:128 * h + 128], ALU.mult, ALU.add, r=True)
        if getattr(self, 'smp_stop', 99) <= 7.3:
            return
        dov = d["dsmp"][l].rearrange("s h d e -> d (s h) e")
        for j in range(8 if not _os.environ.get('DBG_SKIP_SUPD') else 0):
            self.dma(_dram_ref(dov[:, 8 * j:8 * j + 8, :]), Sall[8 * j:8 * j + 8])
        if getattr(self, 'smp_stop', 99) <= 8:
            return
        self.mark('smp%d_8' % l)
        self.tt(t2.p(0, P), o.p(0, P), o.p(0, P), ALU.mult)
        self.red(ss4.p(0, P), v3(t2.p(0, P), 4))
        self.act(ss4.p(0, P), ss4.p(0, P), AF.Ln, scale=1.0 / 128.0, bias=RMS_EPS)
        self.act(ss4.p(0, P), ss4.p(0, P), AF.Exp, scale=-0.5)
        self.tt(v3(o.p(0, P), 4), v3(o.p(0, P), 4), b3(ss4), ALU.mult)
        self.tt(v3(o.p(0, P), 4), v3(o.p(0, P), 4), self.bc(self.onorm_bc.p(0, P, slice(128 * l, 128 * l + 128)), [P, 4, 128], 1), ALU.mult)
        self.act(zt.p(0, P), zt.p(0, P), AF_SILU)
        self.tt(o.p(0, P), o.p(0, P), zt.p(0, P), ALU.mult)
        for h in range(4):
            self.tr(PS[5][P * h:P * h + P], o.p(0, P, slice(128 * h, 128 * h + 128)))
        self.cp(mixs[0:4], v3(PS[5][0:4 * P], 4), r=True, eng="act")
        if getattr(self, 'smp_stop', 99) <= 9:
            return
        self.mark('smp%d_9' % l)
        for g in range(4):
            w = POOL_W[g]
            pg = pt.p(0, P, slice(128 * g, 128 * g + 128))
            pv = pst1.p(0, P, slice(0, w - 1))
            self.dma(pv, _dram_ref(d["spool"][l][:, 16 - w:15, 128 * g:128 * g + 128]))
            if w == 2:
                self.tt(sumg.p(0, P), pst1.p(0, P, 0), pg, ALU.add)
            else:
                self.red(sumg.p(0, P), Ref(pv.ap.rearrange("p r c -> p c r"), pv.ivs))
                self.tt(sumg.p(0, P), sumg.p(0, P), pg, ALU.add)
            self.stt(dtok.p(0, P, slice(128 * g, 128 * g + 128)), sumg.p(0, P), 1.0 / w, pg, ALU.mult, ALU.subtract)
        for g in range(4):
            self.tr(PS[5][64 + P * g:64 + P * g + P], dtok.p(0, P, slice(128 * g, 128 * g + 128)))
        self.cp(ddT[:], v3(PS[5][64:64 + 4 * P], 4), r=True, eng="act")
        for g in range(4):
            self.mm(PS[4][P * g:P * g + P], self.poolw[4 * l + g], ddT[g])
            self.act(mixs[4 + g], PS[4][P * g:P * g + P], AF.Identity, scale=self.pscale[l * 4 + g:l * 4 + g + 1], r=True)
        if getattr(self, 'smp_stop', 99) <= 10:
            return
        self.mark('smp%d_10' % l)
        for m in range(DC):
            wb = wblk.next()
            self.dma(wb[:], _dram_ref(woutv[:, :, 128 * m:128 * (m + 1)]))
            ps = PS[m % 4]
            for k in range(DC):
                self.mm(ps[0:P], wb[k], mixs[k], start=(k == 0), stop=(k == DC - 1))
            self.stt(rbs[m], Xs(m), ALPHA, ps[0:P], ALU.mult, ALU.add, r=True)
        AF3 = Alloc(self.arenaF, "sbF", AFb.cur, self.FZ[1])
        tmp = (sqs, AF3(P), AF3(P), AF3(P), Ring([AF3(P) for _ in range(2)]))
        self.layer_norm(rbs, 0, P, SC0, 2, 3, l, LN_EPS, tmp)


_CACHE = {}


def _get_program():
    if "nc" not in _CACHE:
        k = Kern()
        _CACHE["nc"] = k.build()
    return _CACHE["nc"]


def make_in_maps(inputs):
    f = lambda a: np.ascontiguousarray(np.asarray(a, dtype=np.float32))
    consts = make_consts()
    shared = {
        "wg1": f(inputs["ffn1_w_gate"]), "wu1": f(inputs["ffn1_w_up"]), "wd1": f(inputs["ffn1_w_down"]),
        "win": f(inputs["w_in"]), "wout": f(inputs["w_out"]),
        "wg2": f(inputs["ffn2_w_gate"]), "wu2": f(inputs["ffn2_w_up"]), "wd2": f(inputs["ffn2_w_down"]),
        "poolw": f(inputs["pool_w"]),
        "ln1g": f(inputs["ln1_g"]), "ln1b": f(inputs["ln1_b"]), "ln2g": f(inputs["ln2_g"]),
        "ln2b": f(inputs["ln2_b"]), "ln3g": f(inputs["ln3_g"]), "ln3b": f(inputs["ln3_b"]),
        "convw": f(inputs["conv_w"]), "alog": f(inputs["a_log"]), "dtb": f(inputs["dt_bias"]),
        "onorm": f(inputs["onorm_g"]), "pscale": f(inputs["pool_scale"]),
        "constf": consts, "constr": np.ascontiguousarray(np.concatenate([consts[:, C_ID:C_ID + 128], consts[:, C_ONES:C_ONES + 128], consts[:, C_ONESD:C_ONESD + 128]], axis=1)),
    }
    xp = f(inputs["x_prompt"])
    xs = f(inputs["x_sample"])
    sd = f(inputs["state_delta"])
    sc = f(inputs["state_conv"])
    sp = f(inputs["state_pool"])
    maps = []
    for c in range(NCORES):
        m = dict(shared)
        m["xp"] = xp[c]
        m["xs"] = np.ascontiguousarray(xs[c * NS:(c + 1) * NS, 0, :])
        m["sdelta"] = np.ascontiguousarray(sd[:, c * NS:(c + 1) * NS])
        m["sconv"] = np.ascontiguousarray(sc[:, c * NS:(c + 1) * NS])
        m["spool"] = np.ascontiguousarray(sp[:, c * NS:(c + 1) * NS])
        maps.append(m)
    return maps


def kernel(**inputs):
    nc = _get_program()
    maps = make_in_maps(inputs)
    res = run_bass_kernel_spmd(nc, maps, core_ids=list(range(NCORES)))
    R = res.results
    yp = np.stack([R[c]["yp"] for c in range(NCORES)]).astype(np.float32)
    ys = np.concatenate([R[c]["ys"] for c in range(NCORES)], axis=0).reshape(NCORES * NS, 1, D).astype(np.float32)
    dprm = np.stack([R[c]["dprm"] for c in range(NCORES)], axis=1).astype(np.float32)
    cprm = np.stack([R[c]["cprm"] for c in range(NCORES)], axis=1).astype(np.float32)
    pprm = np.stack([R[c]["pprm"] for c in range(NCORES)], axis=1).astype(np.float32)
    dsmp = np.concatenate([R[c]["dsmp"] for c in range(NCORES)], axis=1).astype(np.float32)
    csmp = np.concatenate([R[c]["csmp"] for c in range(NCORES)], axis=1).astype(np.float32)
    psmp = np.concatenate([R[c]["psmp"] for c in range(NCORES)], axis=1).astype(np.float32)
    return (yp, ys, dprm, cprm, pprm, dsmp, csmp, psmp)
```
